# Optimizing a Trainium2 kernel written in Bass

```python
import math
import jax, jax.numpy as jnp
from jax import lax
import numpy as np

D_MODEL = 2048
BATCH = 2
SEQ = 16384
DEPTH = 4

MIX_WIDTH = D_MODEL
ATTN_WIDTH = MIX_WIDTH // 2
CONV_WIDTH = MIX_WIDTH - ATTN_WIDTH
N_HEADS = 8
HEAD_DIM = ATTN_WIDTH // N_HEADS
QK_DIM = HEAD_DIM // 2
CONV_K = 3
D_FF = ((8 * D_MODEL + 3 * 256 - 1) // (3 * 256)) * 256
N_BUCKETS = 32
MAX_DISTANCE = 128
Q_BLOCK = 128
NORM_EPS = 1e-6
SUBLN_EPS = 1e-5
IN_COLS = 3 * ATTN_WIDTH + 3 * CONV_WIDTH

kernel_name = "hybrid_diffattn_shortconv_sandwich"


def rms_norm(x, g, eps=NORM_EPS):
    xf = x.astype(jnp.float32)
    y = xf * lax.rsqrt(jnp.mean(xf * xf, axis=-1, keepdims=True) + eps)
    return (y * g.astype(jnp.float32)).astype(x.dtype)


def rel_bucket(dist):
    n = jnp.maximum(dist, 0)
    max_exact = N_BUCKETS // 2
    nf = jnp.maximum(n, max_exact).astype(jnp.float32)
    large = max_exact + (jnp.log(nf / max_exact) / math.log(MAX_DISTANCE / max_exact)
                         * (N_BUCKETS - max_exact)).astype(jnp.int32)
    large = jnp.minimum(large, N_BUCKETS - 1)
    return jnp.where(n < max_exact, n, large)


def diff_attention(q, k, v, lam, rel_bias):
    b, s = q.shape[0], q.shape[1]
    nb = s // Q_BLOCK
    scale = QK_DIM ** -0.5
    k_pos = jnp.arange(s, dtype=jnp.int32)
    q_blocks = q.reshape(b, nb, Q_BLOCK, N_HEADS, 2, QK_DIM).transpose(1, 0, 2, 3, 4, 5)
    starts = jnp.arange(nb, dtype=jnp.int32) * Q_BLOCK

    def block(args):
        q_blk, start = args
        q_pos = start + jnp.arange(Q_BLOCK, dtype=jnp.int32)
        dist = q_pos[:, None] - k_pos[None, :]
        bias = jnp.transpose(rel_bias[rel_bucket(dist)], (2, 0, 1)).astype(jnp.float32)
        logits = jnp.einsum('bqhcd,bkhcd->bhcqk', q_blk, k).astype(jnp.float32) * scale
        logits = logits + bias[None, :, None]
        logits = jnp.where(dist >= 0, logits, -jnp.inf)
        p = jax.nn.softmax(logits, axis=-1)
        w = (p[:, :, 0] - lam * p[:, :, 1]).astype(v.dtype)
        return jnp.einsum('bhqk,bkhe->bqhe', w, v)

    o = lax.map(block, (q_blocks, starts))
    return o.transpose(1, 0, 2, 3, 4).reshape(b, s, N_HEADS, HEAD_DIM)


def short_conv(h, w):
    return lax.conv_general_dilated(
        h, w[:, None, :].astype(h.dtype), window_strides=(1,),
        padding=[(CONV_K - 1, 0)], dimension_numbers=('NWC', 'WIO', 'NWC'),
        feature_group_count=h.shape[-1])


def setup_inputs(seed: int = 0) -> dict:
    key = jax.random.key(seed)
    ks = jax.random.split(key, 20)
    f32 = jnp.float32
    nrm = lambda k, shape, s: jax.random.normal(k, shape, f32) * s
    gain = lambda k, shape: 1.0 + 0.05 * jax.random.normal(k, shape, f32)
    return {
        "x": jax.random.normal(ks[0], (BATCH, SEQ, D_MODEL), f32),
        "w_in": nrm(ks[1], (DEPTH, D_MODEL, IN_COLS), D_MODEL ** -0.5),
        "w_out": nrm(ks[2], (DEPTH, MIX_WIDTH, D_MODEL), MIX_WIDTH ** -0.5),
        "lambda_q1": nrm(ks[3], (DEPTH, QK_DIM), 0.1),
        "lambda_k1": nrm(ks[4], (DEPTH, QK_DIM), 0.1),
        "lambda_q2": nrm(ks[5], (DEPTH, QK_DIM), 0.1),
        "lambda_k2": nrm(ks[6], (DEPTH, QK_DIM), 0.1),
        "subln_gain": gain(ks[7], (DEPTH, HEAD_DIM)),
        "conv_w": nrm(ks[8], (DEPTH, CONV_K, CONV_WIDTH), CONV_K ** -0.5),
        "conv_norm_gain": gain(ks[9], (DEPTH, CONV_WIDTH)),
        "rel_bias": nrm(ks[10], (N_BUCKETS, N_HEADS), 0.5),
        "w_gate": nrm(ks[11], (DEPTH, D_MODEL, D_FF), D_MODEL ** -0.5),
        "w_up": nrm(ks[12], (DEPTH, D_MODEL, D_FF), D_MODEL ** -0.5),
        "w_down": nrm(ks[13], (DEPTH, D_FF, D_MODEL), D_FF ** -0.5),
        "norm_mix_pre": gain(ks[14], (DEPTH, D_MODEL)),
        "norm_mix_post": gain(ks[15], (DEPTH, D_MODEL)),
        "norm_ffn_pre": gain(ks[16], (DEPTH, D_MODEL)),
        "norm_ffn_post": gain(ks[17], (DEPTH, D_MODEL)),
    }


def reference(x, w_in, w_out, lambda_q1, lambda_k1, lambda_q2, lambda_k2, subln_gain,
              conv_w, conv_norm_gain, rel_bias, w_gate, w_up, w_down,
              norm_mix_pre, norm_mix_post, norm_ffn_pre, norm_ffn_post):
    b, s, _ = x.shape
    A, C = ATTN_WIDTH, CONV_WIDTH
    split_at = [A, 2 * A, 3 * A, 3 * A + C, 3 * A + 2 * C]
    for l in range(DEPTH):
        lambda_init = 0.8 - 0.6 * math.exp(-0.3 * l)
        hn = rms_norm(x, norm_mix_pre[l])
        proj = hn @ w_in[l]
        q, k, v, gate_b, gate_c, hc = jnp.split(proj, split_at, axis=-1)
        q = q.reshape(b, s, N_HEADS, 2, QK_DIM)
        k = k.reshape(b, s, N_HEADS, 2, QK_DIM)
        v = v.reshape(b, s, N_HEADS, HEAD_DIM)
        lam = (jnp.exp(jnp.sum(lambda_q1[l].astype(jnp.float32) * lambda_k1[l].astype(jnp.float32)))
               - jnp.exp(jnp.sum(lambda_q2[l].astype(jnp.float32) * lambda_k2[l].astype(jnp.float32)))
               + lambda_init)
        a = diff_attention(q, k, v, lam, rel_bias)
        a = rms_norm(a, subln_gain[l], SUBLN_EPS) * (1.0 - lambda_init)
        a = a.reshape(b, s, A)
        c = gate_b * short_conv(gate_c * hc, conv_w[l])
        c = rms_norm(c, conv_norm_gain[l])
        mix = jnp.concatenate([a, c], axis=-1) @ w_out[l]
        x = x + rms_norm(mix, norm_mix_post[l])
        hn = rms_norm(x, norm_ffn_pre[l])
        f = (jax.nn.silu(hn @ w_gate[l]) * (hn @ w_up[l])) @ w_down[l]
        x = x + rms_norm(f, norm_ffn_post[l])
    return x
```

```python
import math
from contextlib import ExitStack

import numpy as np
import ml_dtypes

import concourse.bass as bass
import concourse.mybir as mybir
from concourse.bass_utils import run_bass_kernel_spmd

F32 = mybir.dt.float32
BF16 = mybir.dt.bfloat16
AF = mybir.ActivationFunctionType
ALU = mybir.AluOpType
NPBF16 = ml_dtypes.bfloat16

D_MODEL = 2048
BATCH = 2
SEQ = 16384
DEPTH = 4
A_W = 1024
C_W = 1024
N_HEADS = 8
HEAD_DIM = 128
QK_DIM = 64
D_FF = 5632
IN_COLS = 6144
N_CORES = 8
TOK = BATCH * SEQ // N_CORES
KC = D_MODEL // 128
NORM_EPS = 1e-6
SUBLN_EPS = 1e-5
NEG = -30000.0


class Sched:
    def __init__(self, nc, stack):
        self.nc = nc
        self.stack = stack
        self.eng = {}
        for name, h in (("pe", nc.tensor), ("act", nc.scalar), ("dve", nc.vector),
                        ("pool", nc.gpsimd), ("sp", nc.sync)):
            sem = stack.enter_context(nc.semaphore("sem_" + name))
            self.eng[name] = dict(h=h, sem=sem, n=0, waited={}, name=name)
        self.last_w = {}
        self.readers = {}
        self.slots = {}

    def _deps(self, reads, writes):
        deps = []
        for k in reads:
            t = self.last_w.get(k)
            if t is not None:
                deps.append((t, "raw"))
        for k in writes:
            t = self.last_w.get(k)
            if t is not None:
                deps.append((t, "waw"))
            for t in self.readers.get(k, {}).values():
                deps.append((t, "war"))
        return deps

    def _emit_waits(self, e, deps):
        need = {}
        for (tok, kind) in deps:
            sem, val, src = tok
            if src == e["name"]:
                if src in ("pe", "sp") or kind == "war":
                    continue
            key = id(sem)
            if e["waited"].get(key, 0) >= val:
                continue
            if key not in need or need[key][1] < val:
                need[key] = (sem, val)
        for key, (sem, val) in need.items():
            e["h"].wait_ge(sem, val)
            e["waited"][key] = val

    def _record(self, tok, reads, writes):
        for k in writes:
            self.last_w[k] = tok
            self.readers[k] = {}
        for k in reads:
            self.readers.setdefault(k, {})[id(tok[0])] = tok

    def op(self, eng, fn, reads=(), writes=()):
        e = self.eng[eng]
        self._emit_waits(e, self._deps(reads, writes))
        ins = fn()
        e["n"] += 1
        ins.then_inc(e["sem"], 1)
        tok = (e["sem"], e["n"], eng)
        self._record(tok, reads, writes)
        return tok

    def dma(self, queue, out, in_, slot, reads=(), writes=()):
        e = self.eng[queue]
        self._emit_waits(e, self._deps(reads, writes))
        if slot not in self.slots:
            sem = self.stack.enter_context(self.nc.semaphore("dsem_" + slot))
            self.slots[slot] = [sem, 0]
        s = self.slots[slot]
        e["h"].dma_start(out=out, in_=in_).then_inc(s[0], 16)
        s[1] += 16
        tok = (s[0], s[1], "dma")
        self._record(tok, reads, writes)
        return tok

    def barrier(self):
        toks = set()
        for t in self.last_w.values():
            toks.add(t)
        for ts in self.readers.values():
            toks.update(ts.values())
        best = {}
        for (sem, val, src) in toks:
            if id(sem) not in best or best[id(sem)][1] < val:
                best[id(sem)] = (sem, val)
        for e in self.eng.values():
            for key, (sem, val) in best.items():
                if e["sem"] is sem:
                    continue
                if e["waited"].get(key, 0) >= val:
                    continue
                e["h"].wait_ge(sem, val)
                e["waited"][key] = val
        self.last_w.clear()
        self.readers.clear()

    def finish(self):
        e = self.eng["sp"]
        best = {}
        toks = set(self.last_w.values())
        for ts in self.readers.values():
            toks.update(ts.values())
        for (sem, val, src) in toks:
            if id(sem) not in best or best[id(sem)][1] < val:
                best[id(sem)] = (sem, val)
        for key, (sem, val) in best.items():
            if e["sem"] is sem:
                continue
            e["h"].wait_ge(sem, val)


def _mm_group(nc, out, pairs):
    n = len(pairs)
    ins = None
    for i, (l, r) in enumerate(pairs):
        ins = nc.tensor.matmul(out, l, r, start=(i == 0), stop=(i == n - 1))
    return ins


def build_A():
    nc = bass.Bass("TRN2", target_bir_lowering=False)
    TS = 2048
    NS = TOK // TS
    SUB = 256
    xT = nc.dram_tensor("xT", [D_MODEL, TOK], F32, kind="ExternalInput").ap()
    w_in = nc.dram_tensor("w_in", [D_MODEL, IN_COLS], F32, kind="ExternalInput").ap()
    g_pre = nc.dram_tensor("g_pre", [128, KC], F32, kind="ExternalInput").ap()
    qT = nc.dram_tensor("qT", [A_W, TOK], BF16, kind="ExternalOutput").ap()
    kT = nc.dram_tensor("kT", [A_W, TOK], BF16, kind="ExternalOutput").ap()
    v = nc.dram_tensor("v", [TOK, A_W], BF16, kind="ExternalOutput").ap()
    gbT = nc.dram_tensor("gbT", [C_W, TOK], F32, kind="ExternalOutput").ap()
    gcT = nc.dram_tensor("gcT", [C_W, TOK], F32, kind="ExternalOutput").ap()
    hcT = nc.dram_tensor("hcT", [C_W, TOK], F32, kind="ExternalOutput").ap()
    xT_v = xT.rearrange("(kc p) t -> p kc t", p=128)
    w_v = w_in.rearrange("(kc p) c -> p kc c", p=128)

    with ExitStack() as st:
        T = lambda name, shape, dt: st.enter_context(nc.sbuf_tensor(name, shape, dt))
        xt = [T(f"xt{i}", [128, KC, SUB], F32) for i in range(2)]
        sq = T("sq", [128, KC * SUB], BF16)
        hn = T("hn", [128, KC, TS], BF16)
        wb = [T(f"wb{i}", [128, KC, 512], BF16) for i in range(2)]
        stg = [T(f"stg{i}", [128, 2048], F32) for i in range(2)]
        stgb = [T(f"stgb{i}", [128, 2048], BF16) for i in range(2)]
        gp = T("gp", [128, KC], F32)
        ones = T("ones", [128, 128], BF16)
        epsn = T("epsn", [128, 1], F32)
        rstd = [T(f"rstd{i}", [128, SUB], F32) for i in range(2)]
        ps = [st.enter_context(nc.psum_tensor(f"ps{i}", [128, 1024], F32)) for i in range(4)]
        bank = lambda i: ps[i // 2][:, (i % 2) * 512:(i % 2) * 512 + 512]
        st.enter_context(nc.Block())
        S = Sched(nc, st)

        S.dma("sp", gp[:], g_pre, "gp", writes=["gp"])
        S.op("dve", lambda: nc.vector.memset(ones[:], 1.0 / D_MODEL), writes=["ones"])
        S.op("dve", lambda: nc.vector.memset(epsn[:], NORM_EPS), writes=["epsn"])

        nmm = 0
        nst = 0
        for s in range(NS):
            for j in range(TS // SUB):
                b = j % 2
                t0 = s * TS + j * SUB
                S.dma("sp", xt[b][:], xT_v[:, :, t0:t0 + SUB], f"xt{b}", writes=[f"xt{b}"])
                S.op("act", lambda: nc.scalar.activation(out=sq[:], in_=xt[b][:].rearrange("p k t -> p (k t)"),
                                                         func=AF.Square),
                     reads=[f"xt{b}"], writes=["sq"])
                S.op("pe", lambda: _mm_group(nc, bank(7)[:, 0:SUB],
                                             [(ones[:], sq[:, k * SUB:(k + 1) * SUB]) for k in range(KC)]),
                     reads=["ones", "sq"], writes=["bank7"])
                S.op("act", lambda: nc.scalar.activation(out=rstd[b][:], in_=bank(7)[:, 0:SUB], func=AF.Sqrt,
                                                         bias=epsn[:], scale=1.0),
                     reads=["bank7", "epsn"], writes=[f"rstd{b}"])
                S.op("dve", lambda: nc.vector.reciprocal(rstd[b][:], rstd[b][:]),
                     reads=[f"rstd{b}"], writes=[f"rstd{b}"])
                for k in range(KC):
                    S.op("dve", lambda: nc.vector.scalar_tensor_tensor(
                        hn[:, k, j * SUB:(j + 1) * SUB], xt[b][:, k, :], gp[:, k:k + 1], rstd[b][:],
                        ALU.mult, ALU.mult),
                        reads=[f"xt{b}", "gp", f"rstd{b}"], writes=[("hn", j)])
            hn_keys = [("hn", j) for j in range(TS // SUB)]
            for blk in range(IN_COLS // 512):
                wbi = blk % 2
                S.dma("pool", wb[wbi][:], w_v[:, :, blk * 512:(blk + 1) * 512], f"wb{wbi}", writes=[f"wb{wbi}"])
                if blk in (4, 5):
                    for g4 in range(TS // 512):
                        si = nst % 2
                        nst += 1
                        for tt in range(4):
                            tq = g4 * 4 + tt
                            bk = nmm % 6
                            nmm += 1
                            S.op("pe", lambda: _mm_group(nc, bank(bk), [
                                (hn[:, k, tq * 128:(tq + 1) * 128], wb[wbi][:, k, :]) for k in range(KC)]),
                                reads=hn_keys + [f"wb{wbi}"], writes=[f"bank{bk}"])
                            dst = stgb[si][:, tt * 512:(tt + 1) * 512]
                            if nmm % 2:
                                S.op("act", lambda: nc.scalar.copy(out=dst, in_=bank(bk)),
                                     reads=[f"bank{bk}"], writes=[f"stgb{si}"])
                            else:
                                S.op("dve", lambda: nc.vector.tensor_copy(dst, bank(bk)),
                                     reads=[f"bank{bk}"], writes=[f"stgb{si}"])
                        r0 = s * TS + g4 * 512
                        S.dma("sp", v[r0:r0 + 512, (blk - 4) * 512:(blk - 3) * 512].rearrange("(a p) c -> p a c", p=128),
                              stgb[si][:].rearrange("p (a c) -> p a c", a=4), f"stgb{si}",
                              reads=[f"stgb{si}"], writes=[("v", blk, s, g4)])
                    continue
                for cc in range(4):
                    col = blk * 512 + cc * 128
                    is_bf = col < 2 * A_W
                    si = nst % 2
                    nst += 1
                    stag = stgb[si] if is_bf else stg[si]
                    sname = (f"stgb{si}" if is_bf else f"stg{si}")
                    for t in range(TS // 512):
                        bk = nmm % 6
                        nmm += 1
                        S.op("pe", lambda: _mm_group(nc, bank(bk), [
                            (wb[wbi][:, k, cc * 128:(cc + 1) * 128], hn[:, k, t * 512:(t + 1) * 512])
                            for k in range(KC)]),
                            reads=hn_keys + [f"wb{wbi}"], writes=[f"bank{bk}"])
                        dst = stag[:, t * 512:(t + 1) * 512]
                        if nmm % 2:
                            S.op("act", lambda: nc.scalar.copy(out=dst, in_=bank(bk)),
                                 reads=[f"bank{bk}"], writes=[sname])
                        else:
                            S.op("dve", lambda: nc.vector.tensor_copy(dst, bank(bk)),
                                 reads=[f"bank{bk}"], writes=[sname])
                    if col < A_W:
                        dram = qT[col:col + 128, s * TS:(s + 1) * TS]
                    elif col < 2 * A_W:
                        dram = kT[col - A_W:col - A_W + 128, s * TS:(s + 1) * TS]
                    elif col < 3 * A_W + C_W:
                        c0 = col - 3 * A_W
                        dram = gbT[c0:c0 + 128, s * TS:(s + 1) * TS]
                    elif col < 3 * A_W + 2 * C_W:
                        c0 = col - 3 * A_W - C_W
                        dram = gcT[c0:c0 + 128, s * TS:(s + 1) * TS]
                    else:
                        c0 = col - 3 * A_W - 2 * C_W
                        dram = hcT[c0:c0 + 128, s * TS:(s + 1) * TS]
                    S.dma("sp", dram, stag[:], sname, reads=[sname], writes=[("o", col, s)])
        S.finish()
    return nc


def build_B(seq=SEQ, dbg=False):
    nc = bass.Bass("TRN2", target_bir_lowering=False)
    if dbg:
        dbg_o = nc.dram_tensor("dbg_o", [128, 6, 512], F32, kind="ExternalOutput").ap()
    NQB = seq // 512
    NKC = seq // 128
    qT = nc.dram_tensor("qT", [256, seq], BF16, kind="ExternalInput").ap()
    kT = nc.dram_tensor("kT", [256, seq], BF16, kind="ExternalInput").ap()
    v = nc.dram_tensor("v", [seq, 256], BF16, kind="ExternalInput").ap()
    U = nc.dram_tensor("U", [128, 2, 1024], F32, kind="ExternalInput").ap()
    cfar = nc.dram_tensor("cfar", [128, 2], F32, kind="ExternalInput").ap()
    lamv = nc.dram_tensor("lamv", [128, 4, 64], F32, kind="ExternalInput").ap()
    cst = nc.dram_tensor("cst", [128, 2], F32, kind="ExternalInput").ap()
    gsub = nc.dram_tensor("gsub", [128, 1], F32, kind="ExternalInput").ap()
    aT = nc.dram_tensor("aT", [256, seq], BF16, kind="ExternalOutput").ap()

    with ExitStack() as st:
        T = lambda name, shape, dt: st.enter_context(nc.sbuf_tensor(name, shape, dt))
        k_sb = T("k_sb", [128, 2, seq], BF16)
        v_sb = T("v_sb", [128, NKC, 256], BF16)
        qb = [T(f"qb{i}", [128, 512], BF16) for i in range(2)]
        U_sb = T("U_sb", [128, 2, 1024], F32)
        cf = T("cf", [128, 2], F32)
        lv = T("lv", [128, 4, 64], F32)
        cs = T("cs", [128, 2], F32)
        gs = T("gs", [128, 1], F32)
        gsc = T("gsc", [128, 1], F32)
        lam = T("lam", [128, 4], F32)
        prod = T("prod", [128, 64], F32)
        epss = T("epss", [128, 1], F32)
        ones = T("ones", [128, 128], BF16)
        onesf = T("onesf", [128, 128], F32)
        P = [T(f"P{i}", [128, 1024], BF16) for i in range(3)]
        tmp = [T(f"tmp{i}", [128, 1024], F32) for i in range(2)]
        rec = [T(f"rec{i}", [128, 512], F32) for i in range(2)]
        tt = [T(f"tt{i}", [128, 512], F32) for i in range(2)]
        sqf = T("sqf", [128, 512], F32)
        rs = T("rs", [128, 512], F32)
        ao = [T(f"ao{i}", [128, 512], BF16) for i in range(2)]
        ps = [st.enter_context(nc.psum_tensor(f"ps{i}", [128, 1024], F32)) for i in range(4)]
        st.enter_context(nc.Block())
        S = Sched(nc, st)

        for hl in range(2):
            S.dma("pool", k_sb[:, hl, :], kT[hl * 128:(hl + 1) * 128, :], "kv", writes=["kv"])
        vv = v.rearrange("(c p) e -> p c e", p=128)
        npiece = max(1, NKC // 16)
        for i in range(npiece):
            c0 = i * (NKC // npiece)
            c1 = (i + 1) * (NKC // npiece)
            S.dma("pool", v_sb[:, c0:c1, :], vv[:, c0:c1, :], "kv", writes=["kv"])
        S.dma("sp", U_sb[:], U, "cst", writes=["cst"])
        S.dma("sp", cf[:], cfar, "cst", writes=["cst"])
        S.dma("sp", lv[:], lamv, "cst", writes=["cst"])
        S.dma("sp", cs[:], cst, "cst", writes=["cst"])
        S.dma("sp", gs[:], gsub, "cst", writes=["cst"])
        S.op("dve", lambda: nc.vector.memset(ones[:], 1.0), writes=["ones"])
        S.op("dve", lambda: nc.vector.memset(onesf[:], 1.0 / HEAD_DIM), writes=["onesf"])
        S.op("dve", lambda: nc.vector.memset(epss[:], SUBLN_EPS), writes=["epss"])
        for i in range(2):
            S.op("dve", lambda: nc.vector.tensor_tensor(prod[:], lv[:, 2 * i, :], lv[:, 2 * i + 1, :], ALU.mult),
                 reads=["cst"], writes=["prod"])
            S.op("dve", lambda: nc.vector.reduce_sum(lam[:, i:i + 1], prod[:], axis=mybir.AxisListType.X),
                 reads=["prod"], writes=["lam"])
        S.op("act", lambda: nc.scalar.activation(out=lam[:, 0:2], in_=lam[:, 0:2], func=AF.Exp),
             reads=["lam"], writes=["lam"])
        S.op("dve", lambda: nc.vector.tensor_tensor(lam[:, 2:3], lam[:, 0:1], lam[:, 1:2], ALU.subtract),
             reads=["lam"], writes=["lam"])
        S.op("dve", lambda: nc.vector.tensor_tensor(lam[:, 2:3], lam[:, 2:3], cs[:, 0:1], ALU.add),
             reads=["lam", "cst"], writes=["lam"])
        S.op("dve", lambda: nc.vector.tensor_scalar(lam[:, 3:4], lam[:, 2:3], -1.0, None, ALU.mult),
             reads=["lam"], writes=["lam"])
        S.op("dve", lambda: nc.vector.tensor_tensor(gsc[:], gs[:], cs[:, 1:2], ALU.mult),
             reads=["cst"], writes=["gsc"])

        O1 = ps[2][:, 0:512]
        O2 = ps[2][:, 512:1024]
        R1 = ps[3][:, 0:512]
        R2 = ps[3][:, 512:1024]

        units = [(hl, j, kc) for hl in range(2) for j in range(NQB) for kc in range(4 * (j + 1))]
        cnt = dict(s=0, p=0, q=0, t=0, a=0)
        qslot_of = {}
        pending = []

        def load_q(hl, j):
            qs = cnt["q"] % 2
            cnt["q"] += 1
            qslot_of[(hl, j)] = qs
            S.dma("sp", qb[qs][:], qT[hl * 128:(hl + 1) * 128, j * 512:(j + 1) * 512], f"qb{qs}",
                  writes=[f"qb{qs}"])

        def emit_qk(u, sl):
            hl, j, kc = u
            if kc == 0:
                if (hl, j) not in qslot_of:
                    load_q(hl, j)
                nj, nh = (j + 1, hl) if j + 1 < NQB else (0, hl + 1)
                if nh < 2 and (nh, nj) not in qslot_of:
                    load_q(nh, nj)
            qs = qslot_of[(hl, j)]

            def f():
                nc.tensor.matmul(ps[sl][:, 0:512], k_sb[0:64, hl, kc * 128:(kc + 1) * 128], qb[qs][0:64, :],
                                 start=True, stop=True)
                return nc.tensor.matmul(ps[sl][:, 512:1024], k_sb[64:128, hl, kc * 128:(kc + 1) * 128],
                                        qb[qs][64:128, :], start=True, stop=True)
            S.op("pe", f, reads=["kv", f"qb{qs}"], writes=[f"S{sl}"])
            return sl

        def emit_exp(u, sl):
            hl, j, kc = u
            delta = j * 512 - kc * 128
            pslot = cnt["p"] % 3
            cnt["p"] += 1
            if delta >= 256:
                S.op("act", lambda: nc.scalar.activation(out=P[pslot][:], in_=ps[sl][:], func=AF.Exp,
                                                         bias=cf[:, hl:hl + 1], scale=QK_DIM ** -0.5),
                     reads=[f"S{sl}", "cst"], writes=[f"P{pslot}"])
            else:
                off = delta + 384
                ts_ = cnt["t"] % 2
                cnt["t"] += 1
                for c in range(2):
                    S.op("dve", lambda: nc.vector.scalar_tensor_tensor(
                        tmp[ts_][:, c * 512:(c + 1) * 512], ps[sl][:, c * 512:(c + 1) * 512], QK_DIM ** -0.5,
                        U_sb[:, hl, off:off + 512], ALU.mult, ALU.add),
                        reads=[f"S{sl}", "cst"] if c == 1 else ["cst"] + [f"S{sl}"], writes=[f"tmp{ts_}"])
                S.op("act", lambda: nc.scalar.activation(out=P[pslot][:], in_=tmp[ts_][:], func=AF.Exp),
                     reads=[f"tmp{ts_}"], writes=[f"P{pslot}"])
            return pslot

        def emit_pv(u, pslot):
            hl, j, kc = u
            first = (kc == 0)
            last = (kc == 4 * (j + 1) - 1)

            def f():
                vch = v_sb[:, kc, hl * 128:(hl + 1) * 128]
                nc.tensor.matmul(O1, vch, P[pslot][:, 0:512], start=first, stop=last)
                nc.tensor.matmul(O2, vch, P[pslot][:, 512:1024], start=first, stop=last)
                nc.tensor.matmul(R1, ones[:], P[pslot][:, 0:512], start=first, stop=last)
                return nc.tensor.matmul(R2, ones[:], P[pslot][:, 512:1024], start=first, stop=last)
            S.op("pe", f, reads=[f"P{pslot}", "kv", "ones"], writes=["OR"])
            if last:
                emit_epilogue(hl, j)

        def emit_epilogue(hl, j):
            S.op("dve", lambda: nc.vector.reciprocal(rec[0][:], R1), reads=["OR"], writes=["rec0"])
            S.op("dve", lambda: nc.vector.reciprocal(rec[1][:], R2), reads=["OR"], writes=["rec1"])
            S.op("dve", lambda: nc.vector.tensor_tensor(tt[0][:], O1, rec[0][:], ALU.mult),
                 reads=["OR", "rec0"], writes=["tt0"])
            S.op("dve", lambda: nc.vector.tensor_tensor(tt[1][:], O2, rec[1][:], ALU.mult),
                 reads=["OR", "rec1"], writes=["tt1"])
            S.op("dve", lambda: nc.vector.scalar_tensor_tensor(tt[0][:], tt[1][:], lam[:, 3:4], tt[0][:],
                                                               ALU.mult, ALU.add),
                 reads=["tt0", "tt1", "lam"], writes=["tt0"])
            S.op("dve", lambda: nc.vector.tensor_tensor(sqf[:], tt[0][:], tt[0][:], ALU.mult),
                 reads=["tt0"], writes=["sqf"])
            if dbg and hl == 0 and j == 0:
                S.dma("sp", dbg_o[:, 0, :], rec[0][:], "dbg", reads=["rec0"], writes=["dbg0"])
                S.dma("sp", dbg_o[:, 1, :], rec[1][:], "dbg", reads=["rec1"], writes=["dbg1"])
                S.dma("sp", dbg_o[:, 2, :], tt[0][:], "dbg", reads=["tt0"], writes=["dbg2"])
                S.dma("sp", dbg_o[:, 3, :], tt[1][:], "dbg", reads=["tt1"], writes=["dbg3"])
                S.dma("sp", dbg_o[:, 4, 0:4], lam[:], "dbg", reads=["lam"], writes=["dbg4"])
                S.dma("sp", dbg_o[:, 5, :], P[0][:, 0:256].bitcast(F32) if False else sqf[:], "dbg", reads=["sqf"], writes=["dbg5"])

            def later():
                sl = cnt["s"] % 2
                S.op("pe", lambda: nc.tensor.matmul(ps[sl][:, 0:512], onesf[:], sqf[:], start=True, stop=True),
                     reads=["onesf", "sqf"], writes=[f"S{sl}"])
                S.op("act", lambda: nc.scalar.activation(out=rs[:], in_=ps[sl][:, 0:512], func=AF.Sqrt,
                                                         bias=epss[:], scale=1.0),
                     reads=[f"S{sl}", "epss"], writes=["rs"])
                S.op("dve", lambda: nc.vector.reciprocal(rs[:], rs[:]), reads=["rs"], writes=["rs"])
                a_ = cnt["a"] % 2
                cnt["a"] += 1
                S.op("dve", lambda: nc.vector.scalar_tensor_tensor(ao[a_][:], tt[0][:], gsc[:, 0:1], rs[:],
                                                                   ALU.mult, ALU.mult),
                     reads=["tt0", "gsc", "rs"], writes=[f"ao{a_}"])
                S.dma("sp", aT[hl * 128:(hl + 1) * 128, j * 512:(j + 1) * 512], ao[a_][:], f"ao{a_}",
                      reads=[f"ao{a_}"], writes=[("aT", hl, j)])
            pending.append(later)

        emit_qk(units[0], 0)
        since = 0
        for n, u in enumerate(units):
            if n + 1 < len(units):
                emit_qk(units[n + 1], (n + 1) % 2)
            pslot = emit_exp(u, n % 2)
            cnt["s"] = n
            npend = len(pending)
            emit_pv(u, pslot)
            if npend:
                since += 1
                if since >= 3:
                    pending.pop(0)()
                    since = 0
        while pending:
            pending.pop(0)()
        S.finish()
    return nc


def _rel_bucket_np(dist):
    n = np.maximum(dist, 0)
    me = 16
    nf = np.maximum(n, me).astype(np.float32)
    large = me + (np.log(nf / np.float32(me)) / np.float32(math.log(128 / me)) * np.float32(32 - me)).astype(np.int32)
    large = np.minimum(large, 31)
    return np.where(n < me, n, large)


def _bias_tables(rel_bias):
    kk = np.arange(128)[:, None]
    j = np.arange(1024)[None, :]
    dist = j - kk - 384
    idx = _rel_bucket_np(dist)
    U = rel_bias[idx]
    U = np.where((dist >= 0)[:, :, None], U, np.float32(NEG)).astype(np.float32)
    U = np.ascontiguousarray(U.transpose(0, 2, 1))
    cfar = np.ascontiguousarray(np.broadcast_to(rel_bias[31][None, :], (128, rel_bias.shape[1]))).astype(np.float32)
    return U, cfar


def build_C(tok=TOK, stop=None):
    nc = bass.Bass("TRN2", target_bir_lowering=False)
    TS = 1024
    NS = tok // TS
    NFF = D_FF // 128
    xT = nc.dram_tensor("xT", [D_MODEL, tok], F32, kind="ExternalInput").ap()
    aT = nc.dram_tensor("aT", [A_W, tok], BF16, kind="ExternalInput").ap()
    gbT = nc.dram_tensor("gbT", [C_W, tok], F32, kind="ExternalInput").ap()
    gcT = nc.dram_tensor("gcT", [C_W, tok + 2], F32, kind="ExternalInput").ap()
    hcT = nc.dram_tensor("hcT", [C_W, tok + 2], F32, kind="ExternalInput").ap()
    convw = nc.dram_tensor("convw", [128, 8, 3], F32, kind="ExternalInput").ap()
    gvec = nc.dram_tensor("gvec", [128, 56], F32, kind="ExternalInput").ap()
    w_out = nc.dram_tensor("w_out", [D_MODEL, D_MODEL], F32, kind="ExternalInput").ap()
    w_gate = nc.dram_tensor("w_gate", [D_MODEL, D_FF], F32, kind="ExternalInput").ap()
    w_up = nc.dram_tensor("w_up", [D_MODEL, D_FF], F32, kind="ExternalInput").ap()
    w_down = nc.dram_tensor("w_down", [D_FF, D_MODEL], F32, kind="ExternalInput").ap()
    xoT = nc.dram_tensor("xoT", [D_MODEL, tok], F32, kind="ExternalOutput").ap()
    fscr = nc.dram_tensor("fscr", [D_MODEL, tok], F32).ap()
    wo_v = w_out.rearrange("(kc p) c -> p kc c", p=128)
    wg_v = w_gate.rearrange("(kc p) c -> p kc c", p=128)
    wu_v = w_up.rearrange("(kc p) c -> p kc c", p=128)
    wd_v = w_down.rearrange("(kc p) c -> p kc c", p=128)
    xT_v = xT.rearrange("(kc p) t -> p kc t", p=128)
    xo_v = xoT.rearrange("(kc p) t -> p kc t", p=128)
    fs_v = fscr.rearrange("(kc p) t -> p kc t", p=128)
    aT_v = aT.rearrange("(kc p) t -> p kc t", p=128)
    GC, GP, GF, GO = 0, 8, 24, 40

    with ExitStack() as st:
        T = lambda name, shape, dt: st.enter_context(nc.sbuf_tensor(name, shape, dt))
        cw = T("cw", [128, 8, 3], F32)
        gv = T("gv", [128, 56], F32)
        onesD = T("onesD", [128, 128], BF16)
        onesC = T("onesC", [128, 128], BF16)
        epsn = T("epsn", [128, 1], F32)
        ps = [st.enter_context(nc.psum_tensor(f"ps{i}", [128, 1024], F32)) for i in range(4)]
        bank = lambda i: ps[i // 2][:, (i % 2) * 512:(i % 2) * 512 + 512]
        st.enter_context(nc.Block())
        S = Sched(nc, st)
        S.dma("sp", cw[:], convw, "cst", writes=["cst"])
        S.dma("sp", gv[:], gvec, "cst", writes=["cst"])
        S.op("dve", lambda: nc.vector.memset(onesD[:], 1.0 / D_MODEL), writes=["onesD"])
        S.op("dve", lambda: nc.vector.memset(onesC[:], 1.0 / C_W), writes=["onesC"])
        S.op("dve", lambda: nc.vector.memset(epsn[:], NORM_EPS), writes=["epsn"])
        cnt = dict(b=0, e=0)

        def evac(dst, src_bank, bkey, wkey):
            cnt["e"] += 1
            if cnt["e"] % 2:
                S.op("act", lambda: nc.scalar.copy(out=dst, in_=src_bank), reads=[bkey], writes=[wkey])
            else:
                S.op("dve", lambda: nc.vector.tensor_copy(dst, src_bank), reads=[bkey], writes=[wkey])

        def rstd_from(dst, stat_ap, skey, dkey):
            S.op("act", lambda: nc.scalar.activation(out=dst, in_=stat_ap, func=AF.Sqrt, bias=epsn[:], scale=1.0),
                 reads=[skey, "epsn"], writes=[dkey])
            S.op("dve", lambda: nc.vector.reciprocal(dst, dst), reads=[dkey], writes=[dkey])

        for s in range(NS):
            T0 = s * TS
            with ExitStack() as sup:
                TT = lambda name, shape, dt, ctx=sup: ctx.enter_context(nc.sbuf_tensor(f"{name}_{s}", shape, dt))
                hn2 = TT("hn2", [128, KC, TS], BF16)
                with ExitStack() as p1:
                    P1 = lambda name, shape, dt: p1.enter_context(nc.sbuf_tensor(f"{name}_{s}", shape, dt))
                    cat = P1("cat", [128, KC, 512], BF16)
                    mix = P1("mix", [128, KC, 512], F32)
                    cpre = P1("cpre", [128, 8, 512], F32)
                    gct = [P1(f"gct{i}", [128, 514], F32) for i in range(2)]
                    hct = [P1(f"hct{i}", [128, 514], F32) for i in range(2)]
                    gbt = [P1(f"gbt{i}", [128, 512], F32) for i in range(2)]
                    ut = P1("ut", [128, 514], F32)
                    yt = P1("yt", [128, 512], F32)
                    wb = [P1(f"wb{i}", [128, KC, 512], BF16) for i in range(2)]
                    xch = [P1(f"xch{i}", [128, 512], F32) for i in range(2)]
                    sqb = [P1(f"sqb{i}", [128, 512], BF16) for i in range(2)]
                    rsd = P1("rsd", [128, 512], F32)
                    nsq = 0
                    for half in range(2):
                        H0 = T0 + half * 512
                        S.dma("sp", cat[:, 0:8, :], aT_v[:, :, H0:H0 + 512], "cata", writes=["cata"])
                        for ch in range(8):
                            b = ch % 2
                            rows = slice(ch * 128, (ch + 1) * 128)
                            S.dma("sp", gct[b][:], gcT[rows, H0:H0 + 514], f"cv{b}", writes=[f"cv{b}"])
                            S.dma("sp", hct[b][:], hcT[rows, H0:H0 + 514], f"cv{b}", writes=[f"cv{b}"])
                            S.dma("sp", gbt[b][:], gbT[rows, H0:H0 + 512], f"cv{b}", writes=[f"cv{b}"])
                            S.op("dve", lambda: nc.vector.tensor_tensor(ut[:], gct[b][:], hct[b][:], ALU.mult),
                                 reads=[f"cv{b}"], writes=["ut"])
                            S.op("dve", lambda: nc.vector.tensor_scalar_mul(yt[:], ut[:, 2:514], cw[:, ch, 2:3]),
                                 reads=["ut", "cst"], writes=["yt"])
                            S.op("dve", lambda: nc.vector.scalar_tensor_tensor(yt[:], ut[:, 1:513], cw[:, ch, 1:2],
                                                                               yt[:], ALU.mult, ALU.add),
                                 reads=["ut", "yt", "cst"], writes=["yt"])
                            S.op("dve", lambda: nc.vector.scalar_tensor_tensor(yt[:], ut[:, 0:512], cw[:, ch, 0:1],
                                                                               yt[:], ALU.mult, ALU.add),
                                 reads=["ut", "yt", "cst"], writes=["yt"])
                            S.op("dve", lambda: nc.vector.tensor_tensor(cpre[:, ch, :], gbt[b][:], yt[:], ALU.mult),
                                 reads=[f"cv{b}", "yt"], writes=[("cpre", ch)])
                            q_ = nsq % 2
                            nsq += 1
                            S.op("act", lambda: nc.scalar.activation(out=sqb[q_][:], in_=cpre[:, ch, :], func=AF.Square),
                                 reads=[("cpre", ch)], writes=[f"sqb{q_}"])
                            S.op("pe", lambda: nc.tensor.matmul(bank(7), onesC[:], sqb[q_][:], start=(ch == 0),
                                                                stop=(ch == 7)),
                                 reads=["onesC", f"sqb{q_}"], writes=["bank7"])
                        rstd_from(rsd[:], bank(7), "bank7", "rsd")
                        for ch in range(8):
                            S.op("dve", lambda: nc.vector.scalar_tensor_tensor(
                                cat[:, 8 + ch, :], cpre[:, ch, :], gv[:, GC + ch:GC + ch + 1], rsd[:], ALU.mult, ALU.mult),
                                reads=[("cpre", ch), "cst", "rsd"], writes=[("catc", ch)])
                        cat_keys = ["cata"] + [("catc", ch) for ch in range(8)]
                        if stop == "p1a":
                            break
                        for blk in range(4):
                            wbi = blk % 2
                            S.dma("pool", wb[wbi][:], wo_v[:, :, blk * 512:(blk + 1) * 512], f"wb{wbi}",
                                  writes=[f"wb{wbi}"])
                            for cc in range(4):
                                fc = blk * 4 + cc
                                bk = cnt["b"] % 6
                                cnt["b"] += 1
                                S.op("pe", lambda: _mm_group(nc, bank(bk), [
                                    (wb[wbi][:, k, cc * 128:(cc + 1) * 128], cat[:, k, :]) for k in range(KC)]),
                                    reads=cat_keys + [f"wb{wbi}"], writes=[f"bank{bk}"])
                                S.op("dve", lambda: nc.vector.tensor_copy(mix[:, fc, :], bank(bk)),
                                     reads=[f"bank{bk}"], writes=[("mix", fc)])
                                q_ = nsq % 2
                                nsq += 1
                                S.op("act", lambda: nc.scalar.activation(out=sqb[q_][:], in_=mix[:, fc, :], func=AF.Square),
                                     reads=[("mix", fc)], writes=[f"sqb{q_}"])
                                S.op("pe", lambda: nc.tensor.matmul(bank(6), onesD[:], sqb[q_][:], start=(fc == 0),
                                                                    stop=(fc == KC - 1)),
                                     reads=["onesD", f"sqb{q_}"], writes=["bank6"])
                        rstd_from(rsd[:], bank(6), "bank6", "rsd")
                        if stop == "p1b":
                            break
                        for fc in range(KC):
                            b = fc % 2
                            S.dma("sp", xch[b][:], xT_v[:, fc, H0:H0 + 512], f"xch{b}", writes=[f"xch{b}"])
                            S.op("dve", lambda: nc.vector.scalar_tensor_tensor(
                                mix[:, fc, :], mix[:, fc, :], gv[:, GP + fc:GP + fc + 1], rsd[:], ALU.mult, ALU.mult),
                                reads=[("mix", fc), "cst", "rsd"], writes=[("mix", fc)])
                            S.op("dve", lambda: nc.vector.tensor_tensor(mix[:, fc, :], mix[:, fc, :], xch[b][:], ALU.add),
                                 reads=[("mix", fc), f"xch{b}"], writes=[("mix", fc)])
                            q_ = nsq % 2
                            nsq += 1
                            S.op("act", lambda: nc.scalar.activation(out=sqb[q_][:], in_=mix[:, fc, :], func=AF.Square),
                                 reads=[("mix", fc)], writes=[f"sqb{q_}"])
                            S.op("pe", lambda: nc.tensor.matmul(bank(7), onesD[:], sqb[q_][:], start=(fc == 0),
                                                                stop=(fc == KC - 1)),
                                 reads=["onesD", f"sqb{q_}"], writes=["bank7"])
                        mix_keys = [("mix", fc) for fc in range(KC)]
                        S.dma("sp", xo_v[:, :, H0:H0 + 512], mix[:], "x1st", reads=mix_keys, writes=[("xo", s, half)])
                        rstd_from(rsd[:], bank(7), "bank7", "rsd")
                        for fc in range(KC):
                            S.op("dve", lambda: nc.vector.scalar_tensor_tensor(
                                hn2[:, fc, half * 512:(half + 1) * 512], mix[:, fc, :], gv[:, GF + fc:GF + fc + 1],
                                rsd[:], ALU.mult, ALU.mult),
                                reads=[("mix", fc), "cst", "rsd"], writes=[("hn2", half)])
                    S.barrier()
                if stop in ("p1", "p1a", "p1b"):
                    continue
                with ExitStack() as p23:
                    hT = p23.enter_context(nc.sbuf_tensor(f"hT_{s}", [128, NFF, TS], BF16))
                    with ExitStack() as p2:
                        P2 = lambda name, shape, dt: p2.enter_context(nc.sbuf_tensor(f"{name}_{s}", shape, dt))
                        wg = [P2(f"wg{i}", [128, KC, 256], BF16) for i in range(2)]
                        wu = [P2(f"wu{i}", [128, KC, 256], BF16) for i in range(2)]
                        sg = [P2(f"sg{i}", [128, 512], F32) for i in range(2)]
                        npair = 0
                        for fb in range(D_FF // 256):
                            wi = fb % 2
                            S.dma("pool", wg[wi][:], wg_v[:, :, fb * 256:(fb + 1) * 256], f"wg{wi}", writes=[f"wg{wi}"])
                            S.dma("pool", wu[wi][:], wu_v[:, :, fb * 256:(fb + 1) * 256], f"wu{wi}", writes=[f"wu{wi}"])
                            for cc in range(2):
                                ffc = fb * 2 + cc
                                for t in range(2):
                                    pr = npair % 3
                                    npair += 1
                                    bg, bu = 2 * pr, 2 * pr + 1
                                    S.op("pe", lambda: _mm_group(nc, bank(bg), [
                                        (wg[wi][:, k, cc * 128:(cc + 1) * 128], hn2[:, k, t * 512:(t + 1) * 512])
                                        for k in range(KC)]), reads=[f"wg{wi}"], writes=[f"bank{bg}"])
                                    S.op("pe", lambda: _mm_group(nc, bank(bu), [
                                        (wu[wi][:, k, cc * 128:(cc + 1) * 128], hn2[:, k, t * 512:(t + 1) * 512])
                                        for k in range(KC)]), reads=[f"wu{wi}"], writes=[f"bank{bu}"])
                                    g_ = npair % 2
                                    S.op("act", lambda: nc.scalar.activation(out=sg[g_][:], in_=bank(bg), func=AF.Silu),
                                         reads=[f"bank{bg}"], writes=[f"sg{g_}"])
                                    S.op("dve", lambda: nc.vector.tensor_tensor(hT[:, ffc, t * 512:(t + 1) * 512],
                                                                                sg[g_][:], bank(bu), ALU.mult),
                                         reads=[f"sg{g_}", f"bank{bu}"], writes=[("hT", ffc, t)])
                        S.barrier()
                    if stop == "p2":
                        continue
                    with ExitStack() as p3:
                        P3 = lambda name, shape, dt: p3.enter_context(nc.sbuf_tensor(f"{name}_{s}", shape, dt))
                        wd = [P3(f"wd{i}", [128, NFF, 128], BF16) for i in range(2)]
                        fst = [P3(f"fst{i}", [128, TS], F32) for i in range(2)]
                        sqb = [P3(f"sqb3{i}", [128, 512], BF16) for i in range(2)]
                        nsq = 0
                        for fc in range(KC):
                            wi = fc % 2
                            fi = fc % 2
                            S.dma("pool", wd[wi][:], wd_v[:, :, fc * 128:(fc + 1) * 128], f"wd{wi}", writes=[f"wd{wi}"])
                            for t in range(2):
                                bk = cnt["b"] % 6
                                cnt["b"] += 1
                                S.op("pe", lambda: _mm_group(nc, bank(bk), [
                                    (wd[wi][:, k, :], hT[:, k, t * 512:(t + 1) * 512])
                                    for k in range(NFF)]), reads=[f"wd{wi}"], writes=[f"bank{bk}"])
                                S.op("dve", lambda: nc.vector.tensor_copy(fst[fi][:, t * 512:(t + 1) * 512], bank(bk)),
                                     reads=[f"bank{bk}"], writes=[f"fst{fi}"])
                                q_ = nsq % 2
                                nsq += 1
                                S.op("act", lambda: nc.scalar.activation(out=sqb[q_][:], in_=fst[fi][:, t * 512:(t + 1) * 512],
                                                                         func=AF.Square),
                                     reads=[f"fst{fi}"], writes=[f"sqb{q_}"])
                                S.op("pe", lambda: nc.tensor.matmul(bank(6 + t), onesD[:], sqb[q_][:], start=(fc == 0),
                                                                    stop=(fc == KC - 1)),
                                     reads=["onesD", f"sqb{q_}"], writes=[f"bank{6 + t}"])
                            S.dma("sp", fs_v[:, fc, T0:T0 + TS], fst[fi][:], f"fst{fi}", reads=[f"fst{fi}"],
                                  writes=[("fs", fc)])
                        S.barrier()
                    if stop == "p3":
                        continue
                    with ExitStack() as p4:
                        P4 = lambda name, shape, dt: p4.enter_context(nc.sbuf_tensor(f"{name}_{s}", shape, dt))
                        rsf = P4("rsf", [128, TS], F32)
                        ft = [P4(f"ft{i}", [128, TS], F32) for i in range(2)]
                        x1t = [P4(f"x1t{i}", [128, TS], F32) for i in range(2)]
                        for t in range(2):
                            rstd_from(rsf[:, t * 512:(t + 1) * 512], bank(6 + t), f"bank{6 + t}", "rsf")
                        for fc in range(KC):
                            b = fc % 2
                            S.dma("sp", ft[b][:], fs_v[:, fc, T0:T0 + TS], f"ft{b}", writes=[f"ft{b}"])
                            S.dma("sp", x1t[b][:], xo_v[:, fc, T0:T0 + TS], f"x1t{b}", writes=[f"x1t{b}"])
                            S.op("dve", lambda: nc.vector.scalar_tensor_tensor(
                                ft[b][:], ft[b][:], gv[:, GO + fc:GO + fc + 1], rsf[:], ALU.mult, ALU.mult),
                                reads=[f"ft{b}", "rsf"], writes=[f"ft{b}"])
                            S.op("dve", lambda: nc.vector.tensor_tensor(ft[b][:], ft[b][:], x1t[b][:], ALU.add),
                                 reads=[f"ft{b}", f"x1t{b}"], writes=[f"ft{b}"])
                            S.dma("sp", xo_v[:, fc, T0:T0 + TS], ft[b][:], f"ft{b}", reads=[f"ft{b}"],
                                  writes=[("xof", fc)])
                        S.barrier()
        S.finish()
    return nc


_PROGS = {}


def _prog(name):
    if name not in _PROGS:
        _PROGS[name] = {"A": build_A, "B": build_B, "C": build_C}[name]()
    return _PROGS[name]


def _run(name, in_maps):
    res = run_bass_kernel_spmd(_prog(name), in_maps, core_ids=list(range(N_CORES)))
    return res.results


def _c(a, dt=np.float32):
    return np.ascontiguousarray(a, dtype=dt)


def kernel(x, w_in, w_out, lambda_q1, lambda_k1, lambda_q2, lambda_k2, subln_gain, conv_w, conv_norm_gain,
           rel_bias, w_gate, w_up, w_down, norm_mix_pre, norm_mix_post, norm_ffn_pre, norm_ffn_post):
    f = lambda a: np.asarray(a, dtype=np.float32)
    x = f(x)
    xs = x.reshape(BATCH * SEQ, D_MODEL)
    RPB = N_CORES // BATCH
    xT = [_c(xs[c * TOK:(c + 1) * TOK].T) for c in range(N_CORES)]
    U, cfar = _bias_tables(f(rel_bias))
    w_in, w_out, w_gate, w_up, w_down = f(w_in), f(w_out), f(w_gate), f(w_up), f(w_down)
    for l in range(DEPTH):
        lam_init = 0.8 - 0.6 * math.exp(-0.3 * l)
        g_pre = _c(f(norm_mix_pre)[l].reshape(KC, 128).T)
        wl = _c(w_in[l])
        ra = _run("A", [{"xT": xT[c], "w_in": wl, "g_pre": g_pre} for c in range(N_CORES)])
        lamv = _c(np.broadcast_to(np.stack([f(lambda_q1)[l], f(lambda_k1)[l], f(lambda_q2)[l], f(lambda_k2)[l]])[None],
                                  (128, 4, QK_DIM)))
        cst = _c(np.broadcast_to(np.array([lam_init, 1.0 - lam_init], np.float32)[None], (128, 2)))
        gsub = _c(f(subln_gain)[l].reshape(128, 1))
        inb = []
        for b in range(BATCH):
            for hp in range(RPB):
                rows = slice(hp * 256, (hp + 1) * 256)
                inb.append({
                    "qT": np.ascontiguousarray(np.concatenate([ra[b * RPB + r]["qT"][rows] for r in range(RPB)], axis=1)),
                    "kT": np.ascontiguousarray(np.concatenate([ra[b * RPB + r]["kT"][rows] for r in range(RPB)], axis=1)),
                    "v": np.ascontiguousarray(np.concatenate([ra[b * RPB + r]["v"][:, rows] for r in range(RPB)], axis=0)),
                    "U": _c(U[:, 2 * hp:2 * hp + 2, :]), "cfar": _c(cfar[:, 2 * hp:2 * hp + 2]),
                    "lamv": lamv, "cst": cst, "gsub": gsub})
        rb = _run("B", inb)
        del inb
        gvec = _c(np.concatenate([f(conv_norm_gain)[l].reshape(8, 128).T, f(norm_mix_post)[l].reshape(KC, 128).T,
                                  f(norm_ffn_pre)[l].reshape(KC, 128).T, f(norm_ffn_post)[l].reshape(KC, 128).T], axis=1))
        convw = _c(f(conv_w)[l].reshape(3, 8, 128).transpose(2, 1, 0))
        wo, wg, wu, wd = _c(w_out[l]), _c(w_gate[l]), _c(w_up[l]), _c(w_down[l])
        inc = []
        for c in range(N_CORES):
            b, r = divmod(c, RPB)
            aT = np.ascontiguousarray(np.concatenate(
                [rb[b * RPB + hp]["aT"][:, r * TOK:(r + 1) * TOK] for hp in range(RPB)], axis=0))
            halo = {}
            for nm in ("gcT", "hcT"):
                cur = np.asarray(ra[c][nm])
                if r == 0:
                    left = np.zeros((C_W, 2), np.float32)
                else:
                    left = np.asarray(ra[c - 1][nm])[:, TOK - 2:TOK]
                halo[nm] = np.ascontiguousarray(np.concatenate([left, cur], axis=1))
            inc.append({"xT": xT[c], "aT": aT, "gbT": np.ascontiguousarray(ra[c]["gbT"]), "gcT": halo["gcT"],
                        "hcT": halo["hcT"], "convw": convw, "gvec": gvec, "w_out": wo, "w_gate": wg, "w_up": wu,
                        "w_down": wd})
        del ra, rb
        rc = _run("C", inc)
        del inc
        xT = [np.ascontiguousarray(rc[c]["xoT"]) for c in range(N_CORES)]
        del rc
    out = np.concatenate([t.T for t in xT], axis=0).reshape(BATCH, SEQ, D_MODEL)
    return np.ascontiguousarray(out, dtype=np.float32)
```

```python
import math
from contextlib import ExitStack

import numpy as np
import ml_dtypes

import concourse.bass as bass
import concourse.mybir as mybir
from concourse.bass_utils import run_bass_kernel_spmd

F32 = mybir.dt.float32
BF16 = mybir.dt.bfloat16
AF = mybir.ActivationFunctionType
ALU = mybir.AluOpType
NPBF16 = ml_dtypes.bfloat16

D_MODEL = 2048
BATCH = 2
SEQ = 16384
DEPTH = 4
A_W = 1024
C_W = 1024
N_HEADS = 8
HEAD_DIM = 128
QK_DIM = 64
D_FF = 5632
IN_COLS = 6144
N_CORES = 8
TOK = BATCH * SEQ // N_CORES
KC = D_MODEL // 128
NORM_EPS = 1e-6
SUBLN_EPS = 1e-5
NEG = -30000.0


class Sched:
    def __init__(self, nc, stack):
        self.nc = nc
        self.stack = stack
        self.eng = {}
        for name, h in (("pe", nc.tensor), ("act", nc.scalar), ("dve", nc.vector),
                        ("pool", nc.gpsimd), ("sp", nc.sync)):
            sem = stack.enter_context(nc.semaphore("sem_" + name))
            self.eng[name] = dict(h=h, sem=sem, n=0, waited={}, name=name)
        self.last_w = {}
        self.readers = {}
        self.slots = {}

    def _deps(self, reads, writes):
        deps = []
        for k in reads:
            t = self.last_w.get(k)
            if t is not None:
                deps.append((t, "raw"))
        for k in writes:
            t = self.last_w.get(k)
            if t is not None:
                deps.append((t, "waw"))
            for t in self.readers.get(k, {}).values():
                deps.append((t, "war"))
        return deps

    def _emit_waits(self, e, deps):
        need = {}
        for (tok, kind) in deps:
            sem, val, src = tok
            if src == e["name"]:
                if src in ("pe", "sp") or kind == "war":
                    continue
            key = id(sem)
            if e["waited"].get(key, 0) >= val:
                continue
            if key not in need or need[key][1] < val:
                need[key] = (sem, val)
        for key, (sem, val) in need.items():
            e["h"].wait_ge(sem, val)
            e["waited"][key] = val

    def _record(self, tok, reads, writes):
        for k in writes:
            self.last_w[k] = tok
            self.readers[k] = {}
        for k in reads:
            self.readers.setdefault(k, {})[id(tok[0])] = tok

    def op(self, eng, fn, reads=(), writes=()):
        e = self.eng[eng]
        self._emit_waits(e, self._deps(reads, writes))
        ins = fn()
        e["n"] += 1
        ins.then_inc(e["sem"], 1)
        tok = (e["sem"], e["n"], eng)
        self._record(tok, reads, writes)
        return tok

    def dma(self, queue, out, in_, slot, reads=(), writes=()):
        e = self.eng[queue]
        self._emit_waits(e, self._deps(reads, writes))
        if slot not in self.slots:
            sem = self.stack.enter_context(self.nc.semaphore("dsem_" + slot))
            self.slots[slot] = [sem, 0]
        s = self.slots[slot]
        e["h"].dma_start(out=out, in_=in_).then_inc(s[0], 16)
        s[1] += 16
        tok = (s[0], s[1], "dma")
        self._record(tok, reads, writes)
        return tok

    def barrier(self):
        toks = set()
        for t in self.last_w.values():
            toks.add(t)
        for ts in self.readers.values():
            toks.update(ts.values())
        best = {}
        for (sem, val, src) in toks:
            if id(sem) not in best or best[id(sem)][1] < val:
                best[id(sem)] = (sem, val)
        for e in self.eng.values():
            for key, (sem, val) in best.items():
                if e["sem"] is sem:
                    continue
                if e["waited"].get(key, 0) >= val:
                    continue
                e["h"].wait_ge(sem, val)
                e["waited"][key] = val
        self.last_w.clear()
        self.readers.clear()

    def finish(self):
        e = self.eng["sp"]
        best = {}
        toks = set(self.last_w.values())
        for ts in self.readers.values():
            toks.update(ts.values())
        for (sem, val, src) in toks:
            if id(sem) not in best or best[id(sem)][1] < val:
                best[id(sem)] = (sem, val)
        for key, (sem, val) in best.items():
            if e["sem"] is sem:
                continue
            e["h"].wait_ge(sem, val)


def _mm_group(nc, out, pairs):
    n = len(pairs)
    ins = None
    for i, (l, r) in enumerate(pairs):
        ins = nc.tensor.matmul(out, l, r, start=(i == 0), stop=(i == n - 1))
    return ins


def build_A():
    nc = bass.Bass("TRN2", target_bir_lowering=False)
    TS = 2048
    NS = TOK // TS
    SUB = 256
    xT = nc.dram_tensor("xT", [D_MODEL, TOK], F32, kind="ExternalInput").ap()
    w_in = nc.dram_tensor("w_in", [D_MODEL, IN_COLS], F32, kind="ExternalInput").ap()
    g_pre = nc.dram_tensor("g_pre", [128, KC], F32, kind="ExternalInput").ap()
    qT = nc.dram_tensor("qT", [A_W, TOK], BF16, kind="ExternalOutput").ap()
    kT = nc.dram_tensor("kT", [A_W, TOK], BF16, kind="ExternalOutput").ap()
    v = nc.dram_tensor("v", [TOK, A_W], BF16, kind="ExternalOutput").ap()
    gbT = nc.dram_tensor("gbT", [C_W, TOK], F32, kind="ExternalOutput").ap()
    gcT = nc.dram_tensor("gcT", [C_W, TOK], F32, kind="ExternalOutput").ap()
    hcT = nc.dram_tensor("hcT", [C_W, TOK], F32, kind="ExternalOutput").ap()
    xT_v = xT.rearrange("(kc p) t -> p kc t", p=128)
    w_v = w_in.rearrange("(kc p) c -> p kc c", p=128)

    with ExitStack() as st:
        T = lambda name, shape, dt: st.enter_context(nc.sbuf_tensor(name, shape, dt))
        xt = [T(f"xt{i}", [128, KC, SUB], F32) for i in range(2)]
        sq = T("sq", [128, KC * SUB], BF16)
        hn = T("hn", [128, KC, TS], BF16)
        wb = [T(f"wb{i}", [128, KC, 512], BF16) for i in range(2)]
        stg = [T(f"stg{i}", [128, 2048], F32) for i in range(2)]
        stgb = [T(f"stgb{i}", [128, 2048], BF16) for i in range(2)]
        gp = T("gp", [128, KC], F32)
        ones = T("ones", [128, 128], BF16)
        epsn = T("epsn", [128, 1], F32)
        rstd = [T(f"rstd{i}", [128, SUB], F32) for i in range(2)]
        ps = [st.enter_context(nc.psum_tensor(f"ps{i}", [128, 1024], F32)) for i in range(4)]
        bank = lambda i: ps[i // 2][:, (i % 2) * 512:(i % 2) * 512 + 512]
        st.enter_context(nc.Block())
        S = Sched(nc, st)

        S.dma("sp", gp[:], g_pre, "gp", writes=["gp"])
        S.op("dve", lambda: nc.vector.memset(ones[:], 1.0 / D_MODEL), writes=["ones"])
        S.op("dve", lambda: nc.vector.memset(epsn[:], NORM_EPS), writes=["epsn"])

        nmm = 0
        nst = 0
        for s in range(NS):
            for j in range(TS // SUB):
                b = j % 2
                t0 = s * TS + j * SUB
                S.dma("sp", xt[b][:], xT_v[:, :, t0:t0 + SUB], f"xt{b}", writes=[f"xt{b}"])
                S.op("act", lambda: nc.scalar.activation(out=sq[:], in_=xt[b][:].rearrange("p k t -> p (k t)"),
                                                         func=AF.Square),
                     reads=[f"xt{b}"], writes=["sq"])
                S.op("pe", lambda: _mm_group(nc, bank(7)[:, 0:SUB],
                                             [(ones[:], sq[:, k * SUB:(k + 1) * SUB]) for k in range(KC)]),
                     reads=["ones", "sq"], writes=["bank7"])
                S.op("act", lambda: nc.scalar.activation(out=rstd[b][:], in_=bank(7)[:, 0:SUB], func=AF.Sqrt,
                                                         bias=epsn[:], scale=1.0),
                     reads=["bank7", "epsn"], writes=[f"rstd{b}"])
                S.op("dve", lambda: nc.vector.reciprocal(rstd[b][:], rstd[b][:]),
                     reads=[f"rstd{b}"], writes=[f"rstd{b}"])
                for k in range(KC):
                    S.op("dve", lambda: nc.vector.scalar_tensor_tensor(
                        hn[:, k, j * SUB:(j + 1) * SUB], xt[b][:, k, :], gp[:, k:k + 1], rstd[b][:],
                        ALU.mult, ALU.mult),
                        reads=[f"xt{b}", "gp", f"rstd{b}"], writes=[("hn", j)])
            hn_keys = [("hn", j) for j in range(TS // SUB)]
            for blk in range(IN_COLS // 512):
                wbi = blk % 2
                S.dma("pool", wb[wbi][:], w_v[:, :, blk * 512:(blk + 1) * 512], f"wb{wbi}", writes=[f"wb{wbi}"])
                if blk in (4, 5):
                    for g4 in range(TS // 512):
                        si = nst % 2
                        nst += 1
                        for tt in range(4):
                            tq = g4 * 4 + tt
                            bk = nmm % 6
                            nmm += 1
                            S.op("pe", lambda: _mm_group(nc, bank(bk), [
                                (hn[:, k, tq * 128:(tq + 1) * 128], wb[wbi][:, k, :]) for k in range(KC)]),
                                reads=hn_keys + [f"wb{wbi}"], writes=[f"bank{bk}"])
                            dst = stgb[si][:, tt * 512:(tt + 1) * 512]
                            if nmm % 2:
                                S.op("act", lambda: nc.scalar.copy(out=dst, in_=bank(bk)),
                                     reads=[f"bank{bk}"], writes=[f"stgb{si}"])
                            else:
                                S.op("dve", lambda: nc.vector.tensor_copy(dst, bank(bk)),
                                     reads=[f"bank{bk}"], writes=[f"stgb{si}"])
                        r0 = s * TS + g4 * 512
                        S.dma("sp", v[r0:r0 + 512, (blk - 4) * 512:(blk - 3) * 512].rearrange("(a p) c -> p a c", p=128),
                              stgb[si][:].rearrange("p (a c) -> p a c", a=4), f"stgb{si}",
                              reads=[f"stgb{si}"], writes=[("v", blk, s, g4)])
                    continue
                for cc in range(4):
                    col = blk * 512 + cc * 128
                    is_bf = col < 2 * A_W
                    si = nst % 2
                    nst += 1
                    stag = stgb[si] if is_bf else stg[si]
                    sname = (f"stgb{si}" if is_bf else f"stg{si}")
                    for t in range(TS // 512):
                        bk = nmm % 6
                        nmm += 1
                        S.op("pe", lambda: _mm_group(nc, bank(bk), [
                            (wb[wbi][:, k, cc * 128:(cc + 1) * 128], hn[:, k, t * 512:(t + 1) * 512])
                            for k in range(KC)]),
                            reads=hn_keys + [f"wb{wbi}"], writes=[f"bank{bk}"])
                        dst = stag[:, t * 512:(t + 1) * 512]
                        if nmm % 2:
                            S.op("act", lambda: nc.scalar.copy(out=dst, in_=bank(bk)),
                                 reads=[f"bank{bk}"], writes=[sname])
                        else:
                            S.op("dve", lambda: nc.vector.tensor_copy(dst, bank(bk)),
                                 reads=[f"bank{bk}"], writes=[sname])
                    if col < A_W:
                        dram = qT[col:col + 128, s * TS:(s + 1) * TS]
                    elif col < 2 * A_W:
                        dram = kT[col - A_W:col - A_W + 128, s * TS:(s + 1) * TS]
                    elif col < 3 * A_W + C_W:
                        c0 = col - 3 * A_W
                        dram = gbT[c0:c0 + 128, s * TS:(s + 1) * TS]
                    elif col < 3 * A_W + 2 * C_W:
                        c0 = col - 3 * A_W - C_W
                        dram = gcT[c0:c0 + 128, s * TS:(s + 1) * TS]
                    else:
                        c0 = col - 3 * A_W - 2 * C_W
                        dram = hcT[c0:c0 + 128, s * TS:(s + 1) * TS]
                    S.dma("sp", dram, stag[:], sname, reads=[sname], writes=[("o", col, s)])
        S.finish()
    return nc


def build_B(seq=SEQ, dbg=False):
    nc = bass.Bass("TRN2", target_bir_lowering=False)
    if dbg:
        dbg_o = nc.dram_tensor("dbg_o", [128, 6, 512], F32, kind="ExternalOutput").ap()
    NQB = seq // 512
    NKC = seq // 128
    qT = nc.dram_tensor("qT", [256, seq], BF16, kind="ExternalInput").ap()
    kT = nc.dram_tensor("kT", [256, seq], BF16, kind="ExternalInput").ap()
    v = nc.dram_tensor("v", [seq, 256], BF16, kind="ExternalInput").ap()
    U = nc.dram_tensor("U", [128, 2, 1024], F32, kind="ExternalInput").ap()
    cfar = nc.dram_tensor("cfar", [128, 2], F32, kind="ExternalInput").ap()
    lamv = nc.dram_tensor("lamv", [128, 4, 64], F32, kind="ExternalInput").ap()
    cst = nc.dram_tensor("cst", [128, 2], F32, kind="ExternalInput").ap()
    gsub = nc.dram_tensor("gsub", [128, 1], F32, kind="ExternalInput").ap()
    aT = nc.dram_tensor("aT", [256, seq], BF16, kind="ExternalOutput").ap()

    with ExitStack() as st:
        T = lambda name, shape, dt: st.enter_context(nc.sbuf_tensor(name, shape, dt))
        k_sb = T("k_sb", [128, 2, seq], BF16)
        v_sb = T("v_sb", [128, NKC, 256], BF16)
        qb = [T(f"qb{i}", [128, 512], BF16) for i in range(2)]
        U_sb = T("U_sb", [128, 2, 1024], F32)
        cf = T("cf", [128, 2], F32)
        lv = T("lv", [128, 4, 64], F32)
        cs = T("cs", [128, 2], F32)
        gs = T("gs", [128, 1], F32)
        gsc = T("gsc", [128, 1], F32)
        lam = T("lam", [128, 4], F32)
        prod = T("prod", [128, 64], F32)
        epss = T("epss", [128, 1], F32)
        ones = T("ones", [128, 128], BF16)
        onesf = T("onesf", [128, 128], F32)
        P = [T(f"P{i}", [128, 1024], BF16) for i in range(3)]
        tmp = [T(f"tmp{i}", [128, 1024], F32) for i in range(2)]
        rec = [T(f"rec{i}", [128, 512], F32) for i in range(2)]
        tt = [T(f"tt{i}", [128, 512], F32) for i in range(2)]
        sqf = T("sqf", [128, 512], F32)
        rs = T("rs", [128, 512], F32)
        ao = [T(f"ao{i}", [128, 512], BF16) for i in range(2)]
        ps = [st.enter_context(nc.psum_tensor(f"ps{i}", [128, 1024], F32)) for i in range(4)]
        st.enter_context(nc.Block())
        S = Sched(nc, st)

        for hl in range(2):
            S.dma("pool", k_sb[:, hl, :], kT[hl * 128:(hl + 1) * 128, :], "kv", writes=["kv"])
        vv = v.rearrange("(c p) e -> p c e", p=128)
        npiece = max(1, NKC // 16)
        for i in range(npiece):
            c0 = i * (NKC // npiece)
            c1 = (i + 1) * (NKC // npiece)
            S.dma("pool", v_sb[:, c0:c1, :], vv[:, c0:c1, :], "kv", writes=["kv"])
        S.dma("sp", U_sb[:], U, "cst", writes=["cst"])
        S.dma("sp", cf[:], cfar, "cst", writes=["cst"])
        S.dma("sp", lv[:], lamv, "cst", writes=["cst"])
        S.dma("sp", cs[:], cst, "cst", writes=["cst"])
        S.dma("sp", gs[:], gsub, "cst", writes=["cst"])
        S.op("dve", lambda: nc.vector.memset(ones[:], 1.0), writes=["ones"])
        S.op("dve", lambda: nc.vector.memset(onesf[:], 1.0 / HEAD_DIM), writes=["onesf"])
        S.op("dve", lambda: nc.vector.memset(epss[:], SUBLN_EPS), writes=["epss"])
        for i in range(2):
            S.op("dve", lambda: nc.vector.tensor_tensor(prod[:], lv[:, 2 * i, :], lv[:, 2 * i + 1, :], ALU.mult),
                 reads=["cst"], writes=["prod"])
            S.op("dve", lambda: nc.vector.reduce_sum(lam[:, i:i + 1], prod[:], axis=mybir.AxisListType.X),
                 reads=["prod"], writes=["lam"])
        S.op("act", lambda: nc.scalar.activation(out=lam[:, 0:2], in_=lam[:, 0:2], func=AF.Exp),
             reads=["lam"], writes=["lam"])
        S.op("dve", lambda: nc.vector.tensor_tensor(lam[:, 2:3], lam[:, 0:1], lam[:, 1:2], ALU.subtract),
             reads=["lam"], writes=["lam"])
        S.op("dve", lambda: nc.vector.tensor_tensor(lam[:, 2:3], lam[:, 2:3], cs[:, 0:1], ALU.add),
             reads=["lam", "cst"], writes=["lam"])
        S.op("dve", lambda: nc.vector.tensor_scalar(lam[:, 3:4], lam[:, 2:3], -1.0, None, ALU.mult),
             reads=["lam"], writes=["lam"])
        S.op("dve", lambda: nc.vector.tensor_tensor(gsc[:], gs[:], cs[:, 1:2], ALU.mult),
             reads=["cst"], writes=["gsc"])

        O1 = ps[2][:, 0:512]
        O2 = ps[2][:, 512:1024]
        R1 = ps[3][:, 0:512]
        R2 = ps[3][:, 512:1024]

        units = [(hl, j, kc) for hl in range(2) for j in range(NQB) for kc in range(4 * (j + 1))]
        cnt = dict(s=0, p=0, q=0, t=0, a=0)
        qslot_of = {}
        pending = []

        def load_q(hl, j):
            qs = cnt["q"] % 2
            cnt["q"] += 1
            qslot_of[(hl, j)] = qs
            S.dma("sp", qb[qs][:], qT[hl * 128:(hl + 1) * 128, j * 512:(j + 1) * 512], f"qb{qs}",
                  writes=[f"qb{qs}"])

        def emit_qk(u, sl):
            hl, j, kc = u
            if kc == 0:
                if (hl, j) not in qslot_of:
                    load_q(hl, j)
                nj, nh = (j + 1, hl) if j + 1 < NQB else (0, hl + 1)
                if nh < 2 and (nh, nj) not in qslot_of:
                    load_q(nh, nj)
            qs = qslot_of[(hl, j)]

            def f():
                nc.tensor.matmul(ps[sl][:, 0:512], k_sb[0:64, hl, kc * 128:(kc + 1) * 128], qb[qs][0:64, :],
                                 start=True, stop=True)
                return nc.tensor.matmul(ps[sl][:, 512:1024], k_sb[64:128, hl, kc * 128:(kc + 1) * 128],
                                        qb[qs][64:128, :], start=True, stop=True)
            S.op("pe", f, reads=["kv", f"qb{qs}"], writes=[f"S{sl}"])
            return sl

        def emit_exp(u, sl):
            hl, j, kc = u
            delta = j * 512 - kc * 128
            pslot = cnt["p"] % 3
            cnt["p"] += 1
            if delta >= 256:
                S.op("act", lambda: nc.scalar.activation(out=P[pslot][:], in_=ps[sl][:], func=AF.Exp,
                                                         bias=cf[:, hl:hl + 1], scale=QK_DIM ** -0.5),
                     reads=[f"S{sl}", "cst"], writes=[f"P{pslot}"])
            else:
                off = delta + 384
                ts_ = cnt["t"] % 2
                cnt["t"] += 1
                for c in range(2):
                    S.op("dve", lambda: nc.vector.scalar_tensor_tensor(
                        tmp[ts_][:, c * 512:(c + 1) * 512], ps[sl][:, c * 512:(c + 1) * 512], QK_DIM ** -0.5,
                        U_sb[:, hl, off:off + 512], ALU.mult, ALU.add),
                        reads=[f"S{sl}", "cst"] if c == 1 else ["cst"] + [f"S{sl}"], writes=[f"tmp{ts_}"])
                S.op("act", lambda: nc.scalar.activation(out=P[pslot][:], in_=tmp[ts_][:], func=AF.Exp),
                     reads=[f"tmp{ts_}"], writes=[f"P{pslot}"])
            return pslot

        def emit_pv(u, pslot):
            hl, j, kc = u
            first = (kc == 0)
            last = (kc == 4 * (j + 1) - 1)

            def f():
                vch = v_sb[:, kc, hl * 128:(hl + 1) * 128]
                nc.tensor.matmul(O1, vch, P[pslot][:, 0:512], start=first, stop=last)
                nc.tensor.matmul(O2, vch, P[pslot][:, 512:1024], start=first, stop=last)
                nc.tensor.matmul(R1, ones[:], P[pslot][:, 0:512], start=first, stop=last)
                return nc.tensor.matmul(R2, ones[:], P[pslot][:, 512:1024], start=first, stop=last)
            S.op("pe", f, reads=[f"P{pslot}", "kv", "ones"], writes=["OR"])
            if last:
                emit_epilogue(hl, j)

        def emit_epilogue(hl, j):
            S.op("dve", lambda: nc.vector.reciprocal(rec[0][:], R1), reads=["OR"], writes=["rec0"])
            S.op("dve", lambda: nc.vector.reciprocal(rec[1][:], R2), reads=["OR"], writes=["rec1"])
            S.op("dve", lambda: nc.vector.tensor_tensor(tt[0][:], O1, rec[0][:], ALU.mult),
                 reads=["OR", "rec0"], writes=["tt0"])
            S.op("dve", lambda: nc.vector.tensor_tensor(tt[1][:], O2, rec[1][:], ALU.mult),
                 reads=["OR", "rec1"], writes=["tt1"])
            S.op("dve", lambda: nc.vector.scalar_tensor_tensor(tt[0][:], tt[1][:], lam[:, 3:4], tt[0][:],
                                                               ALU.mult, ALU.add),
                 reads=["tt0", "tt1", "lam"], writes=["tt0"])
            S.op("dve", lambda: nc.vector.tensor_tensor(sqf[:], tt[0][:], tt[0][:], ALU.mult),
                 reads=["tt0"], writes=["sqf"])
            if dbg and hl == 0 and j == 0:
                S.dma("sp", dbg_o[:, 0, :], rec[0][:], "dbg", reads=["rec0"], writes=["dbg0"])
                S.dma("sp", dbg_o[:, 1, :], rec[1][:], "dbg", reads=["rec1"], writes=["dbg1"])
                S.dma("sp", dbg_o[:, 2, :], tt[0][:], "dbg", reads=["tt0"], writes=["dbg2"])
                S.dma("sp", dbg_o[:, 3, :], tt[1][:], "dbg", reads=["tt1"], writes=["dbg3"])
                S.dma("sp", dbg_o[:, 4, 0:4], lam[:], "dbg", reads=["lam"], writes=["dbg4"])
                S.dma("sp", dbg_o[:, 5, :], P[0][:, 0:256].bitcast(F32) if False else sqf[:], "dbg", reads=["sqf"], writes=["dbg5"])

            def later():
                sl = cnt["s"] % 2
                S.op("pe", lambda: nc.tensor.matmul(ps[sl][:, 0:512], onesf[:], sqf[:], start=True, stop=True),
                     reads=["onesf", "sqf"], writes=[f"S{sl}"])
                S.op("act", lambda: nc.scalar.activation(out=rs[:], in_=ps[sl][:, 0:512], func=AF.Sqrt,
                                                         bias=epss[:], scale=1.0),
                     reads=[f"S{sl}", "epss"], writes=["rs"])
                S.op("dve", lambda: nc.vector.reciprocal(rs[:], rs[:]), reads=["rs"], writes=["rs"])
                a_ = cnt["a"] % 2
                cnt["a"] += 1
                S.op("dve", lambda: nc.vector.scalar_tensor_tensor(ao[a_][:], tt[0][:], gsc[:, 0:1], rs[:],
                                                                   ALU.mult, ALU.mult),
                     reads=["tt0", "gsc", "rs"], writes=[f"ao{a_}"])
                S.dma("sp", aT[hl * 128:(hl + 1) * 128, j * 512:(j + 1) * 512], ao[a_][:], f"ao{a_}",
                      reads=[f"ao{a_}"], writes=[("aT", hl, j)])
            pending.append(later)

        emit_qk(units[0], 0)
        since = 0
        for n, u in enumerate(units):
            if n + 1 < len(units):
                emit_qk(units[n + 1], (n + 1) % 2)
            pslot = emit_exp(u, n % 2)
            cnt["s"] = n
            npend = len(pending)
            emit_pv(u, pslot)
            if npend:
                since += 1
                if since >= 3:
                    pending.pop(0)()
                    since = 0
        while pending:
            pending.pop(0)()
        S.finish()
    return nc


def _rel_bucket_np(dist):
    n = np.maximum(dist, 0)
    me = 16
    nf = np.maximum(n, me).astype(np.float32)
    large = me + (np.log(nf / np.float32(me)) / np.float32(math.log(128 / me)) * np.float32(32 - me)).astype(np.int32)
    large = np.minimum(large, 31)
    return np.where(n < me, n, large)


def _bias_tables(rel_bias):
    kk = np.arange(128)[:, None]
    j = np.arange(1024)[None, :]
    dist = j - kk - 384
    idx = _rel_bucket_np(dist)
    U = rel_bias[idx]
    U = np.where((dist >= 0)[:, :, None], U, np.float32(NEG)).astype(np.float32)
    U = np.ascontiguousarray(U.transpose(0, 2, 1))
    cfar = np.ascontiguousarray(np.broadcast_to(rel_bias[31][None, :], (128, rel_bias.shape[1]))).astype(np.float32)
    return U, cfar


def build_C(tok=TOK, stop=None):
    nc = bass.Bass("TRN2", target_bir_lowering=False)
    TS = 1024
    NS = tok // TS
    NFF = D_FF // 128
    xT = nc.dram_tensor("xT", [D_MODEL, tok], F32, kind="ExternalInput").ap()
    aT = nc.dram_tensor("aT", [A_W, tok], BF16, kind="ExternalInput").ap()
    gbT = nc.dram_tensor("gbT", [C_W, tok], F32, kind="ExternalInput").ap()
    gcT = nc.dram_tensor("gcT", [C_W, tok + 2], F32, kind="ExternalInput").ap()
    hcT = nc.dram_tensor("hcT", [C_W, tok + 2], F32, kind="ExternalInput").ap()
    convw = nc.dram_tensor("convw", [128, 8, 3], F32, kind="ExternalInput").ap()
    gvec = nc.dram_tensor("gvec", [128, 56], F32, kind="ExternalInput").ap()
    w_out = nc.dram_tensor("w_out", [D_MODEL, D_MODEL], F32, kind="ExternalInput").ap()
    w_gate = nc.dram_tensor("w_gate", [D_MODEL, D_FF], F32, kind="ExternalInput").ap()
    w_up = nc.dram_tensor("w_up", [D_MODEL, D_FF], F32, kind="ExternalInput").ap()
    w_down = nc.dram_tensor("w_down", [D_FF, D_MODEL], F32, kind="ExternalInput").ap()
    xoT = nc.dram_tensor("xoT", [D_MODEL, tok], F32, kind="ExternalOutput").ap()
    fscr = nc.dram_tensor("fscr", [D_MODEL, tok], F32).ap()
    wo_v = w_out.rearrange("(kc p) c -> p kc c", p=128)
    wg_v = w_gate.rearrange("(kc p) c -> p kc c", p=128)
    wu_v = w_up.rearrange("(kc p) c -> p kc c", p=128)
    wd_v = w_down.rearrange("(kc p) c -> p kc c", p=128)
    xT_v = xT.rearrange("(kc p) t -> p kc t", p=128)
    xo_v = xoT.rearrange("(kc p) t -> p kc t", p=128)
    fs_v = fscr.rearrange("(kc p) t -> p kc t", p=128)
    aT_v = aT.rearrange("(kc p) t -> p kc t", p=128)
    GC, GP, GF, GO = 0, 8, 24, 40

    with ExitStack() as st:
        T = lambda name, shape, dt: st.enter_context(nc.sbuf_tensor(name, shape, dt))
        cw = T("cw", [128, 8, 3], F32)
        gv = T("gv", [128, 56], F32)
        onesD = T("onesD", [128, 128], BF16)
        onesC = T("onesC", [128, 128], BF16)
        epsn = T("epsn", [128, 1], F32)
        rsf = T("rsf", [128, 1024], F32)
        ftp = [T(f"ftp{i}", [128, 512], F32) for i in range(2)]
        x1p = [T(f"x1p{i}", [128, 512], F32) for i in range(2)]
        p4_items = []
        ps = [st.enter_context(nc.psum_tensor(f"ps{i}", [128, 1024], F32)) for i in range(4)]
        bank = lambda i: ps[i // 2][:, (i % 2) * 512:(i % 2) * 512 + 512]
        st.enter_context(nc.Block())
        S = Sched(nc, st)
        S.dma("sp", cw[:], convw, "cst", writes=["cst"])
        S.dma("sp", gv[:], gvec, "cst", writes=["cst"])
        S.op("dve", lambda: nc.vector.memset(onesD[:], 1.0 / D_MODEL), writes=["onesD"])
        S.op("dve", lambda: nc.vector.memset(onesC[:], 1.0 / C_W), writes=["onesC"])
        S.op("dve", lambda: nc.vector.memset(epsn[:], NORM_EPS), writes=["epsn"])
        cnt = dict(b=0, e=0)

        def evac(dst, src_bank, bkey, wkey):
            cnt["e"] += 1
            if cnt["e"] % 2:
                S.op("act", lambda: nc.scalar.copy(out=dst, in_=src_bank), reads=[bkey], writes=[wkey])
            else:
                S.op("dve", lambda: nc.vector.tensor_copy(dst, src_bank), reads=[bkey], writes=[wkey])

        def rstd_from(dst, stat_ap, skey, dkey):
            S.op("act", lambda: nc.scalar.activation(out=dst, in_=stat_ap, func=AF.Sqrt, bias=epsn[:], scale=1.0),
                 reads=[skey, "epsn"], writes=[dkey])
            S.op("dve", lambda: nc.vector.reciprocal(dst, dst), reads=[dkey], writes=[dkey])

        for s in range(NS):
            T0 = s * TS
            with ExitStack() as sup:
                TT = lambda name, shape, dt, ctx=sup: ctx.enter_context(nc.sbuf_tensor(f"{name}_{s}", shape, dt))
                hn2 = TT("hn2", [128, KC, TS], BF16)
                with ExitStack() as p1:
                    P1 = lambda name, shape, dt: p1.enter_context(nc.sbuf_tensor(f"{name}_{s}", shape, dt))
                    cat = P1("cat", [128, KC, 512], BF16)
                    mix = P1("mix", [128, KC, 512], F32)
                    cpre = P1("cpre", [128, 8, 512], F32)
                    gct = [P1(f"gct{i}", [128, 514], F32) for i in range(2)]
                    hct = [P1(f"hct{i}", [128, 514], F32) for i in range(2)]
                    gbt = [P1(f"gbt{i}", [128, 512], F32) for i in range(2)]
                    ut = P1("ut", [128, 514], F32)
                    yt = P1("yt", [128, 512], F32)
                    wb = [P1(f"wb{i}", [128, KC, 512], BF16) for i in range(2)]
                    xch = [P1(f"xch{i}", [128, 512], F32) for i in range(2)]
                    sqb = [P1(f"sqb{i}", [128, 512], BF16) for i in range(2)]
                    rsd = P1("rsd", [128, 512], F32)
                    nsq = 0
                    for half in range(2):
                        H0 = T0 + half * 512
                        S.dma("sp", cat[:, 0:8, :], aT_v[:, :, H0:H0 + 512], "cata", writes=["cata"])
                        for ch in range(8):
                            b = ch % 2
                            rows = slice(ch * 128, (ch + 1) * 128)
                            S.dma("sp", gct[b][:], gcT[rows, H0:H0 + 514], f"cv{b}", writes=[f"cv{b}"])
                            S.dma("sp", hct[b][:], hcT[rows, H0:H0 + 514], f"cv{b}", writes=[f"cv{b}"])
                            S.dma("sp", gbt[b][:], gbT[rows, H0:H0 + 512], f"cv{b}", writes=[f"cv{b}"])
                            S.op("dve", lambda: nc.vector.tensor_tensor(ut[:], gct[b][:], hct[b][:], ALU.mult),
                                 reads=[f"cv{b}"], writes=["ut"])
                            S.op("dve", lambda: nc.vector.tensor_scalar_mul(yt[:], ut[:, 2:514], cw[:, ch, 2:3]),
                                 reads=["ut", "cst"], writes=["yt"])
                            S.op("dve", lambda: nc.vector.scalar_tensor_tensor(yt[:], ut[:, 1:513], cw[:, ch, 1:2],
                                                                               yt[:], ALU.mult, ALU.add),
                                 reads=["ut", "yt", "cst"], writes=["yt"])
                            S.op("dve", lambda: nc.vector.scalar_tensor_tensor(yt[:], ut[:, 0:512], cw[:, ch, 0:1],
                                                                               yt[:], ALU.mult, ALU.add),
                                 reads=["ut", "yt", "cst"], writes=["yt"])
                            S.op("dve", lambda: nc.vector.tensor_tensor(cpre[:, ch, :], gbt[b][:], yt[:], ALU.mult),
                                 reads=[f"cv{b}", "yt"], writes=[("cpre", ch)])
                            q_ = nsq % 2
                            nsq += 1
                            S.op("act", lambda: nc.scalar.activation(out=sqb[q_][:], in_=cpre[:, ch, :], func=AF.Square),
                                 reads=[("cpre", ch)], writes=[f"sqb{q_}"])
                            S.op("pe", lambda: nc.tensor.matmul(bank(7), onesC[:], sqb[q_][:], start=(ch == 0),
                                                                stop=(ch == 7)),
                                 reads=["onesC", f"sqb{q_}"], writes=["bank7"])
                        rstd_from(rsd[:], bank(7), "bank7", "rsd")
                        for ch in range(8):
                            S.op("dve", lambda: nc.vector.scalar_tensor_tensor(
                                cat[:, 8 + ch, :], cpre[:, ch, :], gv[:, GC + ch:GC + ch + 1], rsd[:], ALU.mult, ALU.mult),
                                reads=[("cpre", ch), "cst", "rsd"], writes=[("catc", ch)])
                        cat_keys = ["cata"] + [("catc", ch) for ch in range(8)]
                        if stop == "p1a":
                            break
                        for blk in range(4):
                            wbi = blk % 2
                            S.dma("pool", wb[wbi][:], wo_v[:, :, blk * 512:(blk + 1) * 512], f"wb{wbi}",
                                  writes=[f"wb{wbi}"])
                            for cc in range(4):
                                fc = blk * 4 + cc
                                bk = cnt["b"] % 6
                                cnt["b"] += 1
                                S.op("pe", lambda: _mm_group(nc, bank(bk), [
                                    (wb[wbi][:, k, cc * 128:(cc + 1) * 128], cat[:, k, :]) for k in range(KC)]),
                                    reads=cat_keys + [f"wb{wbi}"], writes=[f"bank{bk}"])
                                S.op("dve", lambda: nc.vector.tensor_copy(mix[:, fc, :], bank(bk)),
                                     reads=[f"bank{bk}"], writes=[("mix", fc)])
                                q_ = nsq % 2
                                nsq += 1
                                S.op("act", lambda: nc.scalar.activation(out=sqb[q_][:], in_=mix[:, fc, :], func=AF.Square),
                                     reads=[("mix", fc)], writes=[f"sqb{q_}"])
                                S.op("pe", lambda: nc.tensor.matmul(bank(6), onesD[:], sqb[q_][:], start=(fc == 0),
                                                                    stop=(fc == KC - 1)),
                                     reads=["onesD", f"sqb{q_}"], writes=["bank6"])
                        rstd_from(rsd[:], bank(6), "bank6", "rsd")
                        if stop == "p1b":
                            break
                        for fc in range(KC):
                            b = fc % 2
                            S.dma("sp", xch[b][:], xT_v[:, fc, H0:H0 + 512], f"xch{b}", writes=[f"xch{b}"])
                            S.op("dve", lambda: nc.vector.scalar_tensor_tensor(
                                mix[:, fc, :], mix[:, fc, :], gv[:, GP + fc:GP + fc + 1], rsd[:], ALU.mult, ALU.mult),
                                reads=[("mix", fc), "cst", "rsd"], writes=[("mix", fc)])
                            S.op("dve", lambda: nc.vector.tensor_tensor(mix[:, fc, :], mix[:, fc, :], xch[b][:], ALU.add),
                                 reads=[("mix", fc), f"xch{b}"], writes=[("mix", fc)])
                            q_ = nsq % 2
                            nsq += 1
                            S.op("act", lambda: nc.scalar.activation(out=sqb[q_][:], in_=mix[:, fc, :], func=AF.Square),
                                 reads=[("mix", fc)], writes=[f"sqb{q_}"])
                            S.op("pe", lambda: nc.tensor.matmul(bank(7), onesD[:], sqb[q_][:], start=(fc == 0),
                                                                stop=(fc == KC - 1)),
                                 reads=["onesD", f"sqb{q_}"], writes=["bank7"])
                        mix_keys = [("mix", fc) for fc in range(KC)]
                        S.dma("sp", xo_v[:, :, H0:H0 + 512], mix[:], "x1st", reads=mix_keys, writes=[("xo", s, half)])
                        rstd_from(rsd[:], bank(7), "bank7", "rsd")
                        for fc in range(KC):
                            S.op("dve", lambda: nc.vector.scalar_tensor_tensor(
                                hn2[:, fc, half * 512:(half + 1) * 512], mix[:, fc, :], gv[:, GF + fc:GF + fc + 1],
                                rsd[:], ALU.mult, ALU.mult),
                                reads=[("mix", fc), "cst", "rsd"], writes=[("hn2", half)])
                    S.barrier()
                if stop in ("p1", "p1a", "p1b"):
                    continue
                with ExitStack() as p23:
                    hT = p23.enter_context(nc.sbuf_tensor(f"hT_{s}", [128, NFF, TS], BF16))
                    with ExitStack() as p2:
                        P2 = lambda name, shape, dt: p2.enter_context(nc.sbuf_tensor(f"{name}_{s}", shape, dt))
                        wg = [P2(f"wg{i}", [128, KC, 256], BF16) for i in range(2)]
                        wu = [P2(f"wu{i}", [128, KC, 256], BF16) for i in range(2)]
                        sg = [P2(f"sg{i}", [128, 512], F32) for i in range(2)]
                        npair = 0
                        for fb in range(D_FF // 256):
                            wi = fb % 2
                            S.dma("pool", wg[wi][:], wg_v[:, :, fb * 256:(fb + 1) * 256], f"wg{wi}", writes=[f"wg{wi}"])
                            S.dma("pool", wu[wi][:], wu_v[:, :, fb * 256:(fb + 1) * 256], f"wu{wi}", writes=[f"wu{wi}"])
                            for cc in range(2):
                                ffc = fb * 2 + cc
                                for t in range(2):
                                    pr = npair % 3
                                    npair += 1
                                    bg, bu = 2 * pr, 2 * pr + 1
                                    S.op("pe", lambda: _mm_group(nc, bank(bg), [
                                        (wg[wi][:, k, cc * 128:(cc + 1) * 128], hn2[:, k, t * 512:(t + 1) * 512])
                                        for k in range(KC)]), reads=[f"wg{wi}"], writes=[f"bank{bg}"])
                                    S.op("pe", lambda: _mm_group(nc, bank(bu), [
                                        (wu[wi][:, k, cc * 128:(cc + 1) * 128], hn2[:, k, t * 512:(t + 1) * 512])
                                        for k in range(KC)]), reads=[f"wu{wi}"], writes=[f"bank{bu}"])
                                    g_ = npair % 2
                                    S.op("act", lambda: nc.scalar.activation(out=sg[g_][:], in_=bank(bg), func=AF.Silu),
                                         reads=[f"bank{bg}"], writes=[f"sg{g_}"])
                                    S.op("dve", lambda: nc.vector.tensor_tensor(hT[:, ffc, t * 512:(t + 1) * 512],
                                                                                sg[g_][:], bank(bu), ALU.mult),
                                         reads=[f"sg{g_}", f"bank{bu}"], writes=[("hT", ffc, t)])
                                    if p4_items:
                                        p4_items.pop(0)()
                        S.barrier()
                    if stop == "p2":
                        continue
                    with ExitStack() as p3:
                        P3 = lambda name, shape, dt: p3.enter_context(nc.sbuf_tensor(f"{name}_{s}", shape, dt))
                        wd = [P3(f"wd{i}", [128, NFF, 128], BF16) for i in range(2)]
                        fst = [P3(f"fst{i}", [128, TS], F32) for i in range(2)]
                        sqb = [P3(f"sqb3{i}", [128, 512], BF16) for i in range(2)]
                        nsq = 0
                        for fc in range(KC):
                            wi = fc % 2
                            fi = fc % 2
                            S.dma("pool", wd[wi][:], wd_v[:, :, fc * 128:(fc + 1) * 128], f"wd{wi}", writes=[f"wd{wi}"])
                            for t in range(2):
                                bk = cnt["b"] % 6
                                cnt["b"] += 1
                                S.op("pe", lambda: _mm_group(nc, bank(bk), [
                                    (wd[wi][:, k, :], hT[:, k, t * 512:(t + 1) * 512])
                                    for k in range(NFF)]), reads=[f"wd{wi}"], writes=[f"bank{bk}"])
                                S.op("dve", lambda: nc.vector.tensor_copy(fst[fi][:, t * 512:(t + 1) * 512], bank(bk)),
                                     reads=[f"bank{bk}"], writes=[f"fst{fi}"])
                                q_ = nsq % 2
                                nsq += 1
                                S.op("act", lambda: nc.scalar.activation(out=sqb[q_][:], in_=fst[fi][:, t * 512:(t + 1) * 512],
                                                                         func=AF.Square),
                                     reads=[f"fst{fi}"], writes=[f"sqb{q_}"])
                                S.op("pe", lambda: nc.tensor.matmul(bank(6 + t), onesD[:], sqb[q_][:], start=(fc == 0),
                                                                    stop=(fc == KC - 1)),
                                     reads=["onesD", f"sqb{q_}"], writes=[f"bank{6 + t}"])
                            S.dma("sp", fs_v[:, fc, T0:T0 + TS], fst[fi][:], f"fst{fi}", reads=[f"fst{fi}"],
                                  writes=[("fs", fc)])
                        for t in range(2):
                            rstd_from(rsf[:, t * 512:(t + 1) * 512], bank(6 + t), f"bank{6 + t}", "rsf")
                        S.barrier()
                    def mk_item(fc, t, T0=T0):
                        def item():
                            b = (fc * 2 + t) % 2
                            cols = slice(T0 + t * 512, T0 + (t + 1) * 512)
                            S.dma("sp", ftp[b][:], fs_v[:, fc, cols], f"ftp{b}", writes=[f"ftp{b}"])
                            S.dma("sp", x1p[b][:], xo_v[:, fc, cols], f"x1p{b}", writes=[f"x1p{b}"])
                            S.op("dve", lambda: nc.vector.scalar_tensor_tensor(
                                ftp[b][:], ftp[b][:], gv[:, GO + fc:GO + fc + 1], rsf[:, t * 512:(t + 1) * 512],
                                ALU.mult, ALU.mult), reads=[f"ftp{b}", "rsf"], writes=[f"ftp{b}"])
                            S.op("dve", lambda: nc.vector.tensor_tensor(ftp[b][:], ftp[b][:], x1p[b][:], ALU.add),
                                 reads=[f"ftp{b}", f"x1p{b}"], writes=[f"ftp{b}"])
                            S.dma("sp", xo_v[:, fc, cols], ftp[b][:], f"ftp{b}", reads=[f"ftp{b}"],
                                  writes=[("xof", fc, t, T0)])
                        return item
                    for fc in range(KC):
                        for t in range(2):
                            p4_items.append(mk_item(fc, t))
        while p4_items:
            p4_items.pop(0)()
        S.finish()
    return nc


_PROGS = {}


def _prog(name):
    if name not in _PROGS:
        _PROGS[name] = {"A": build_A, "B": build_B, "C": build_C}[name]()
    return _PROGS[name]


def _run(name, in_maps):
    res = run_bass_kernel_spmd(_prog(name), in_maps, core_ids=list(range(N_CORES)))
    return res.results


def _c(a, dt=np.float32):
    return np.ascontiguousarray(a, dtype=dt)


def kernel(x, w_in, w_out, lambda_q1, lambda_k1, lambda_q2, lambda_k2, subln_gain, conv_w, conv_norm_gain,
           rel_bias, w_gate, w_up, w_down, norm_mix_pre, norm_mix_post, norm_ffn_pre, norm_ffn_post):
    f = lambda a: np.asarray(a, dtype=np.float32)
    x = f(x)
    xs = x.reshape(BATCH * SEQ, D_MODEL)
    RPB = N_CORES // BATCH
    xT = [_c(xs[c * TOK:(c + 1) * TOK].T) for c in range(N_CORES)]
    U, cfar = _bias_tables(f(rel_bias))
    w_in, w_out, w_gate, w_up, w_down = f(w_in), f(w_out), f(w_gate), f(w_up), f(w_down)
    for l in range(DEPTH):
        lam_init = 0.8 - 0.6 * math.exp(-0.3 * l)
        g_pre = _c(f(norm_mix_pre)[l].reshape(KC, 128).T)
        wl = _c(w_in[l])
        ra = _run("A", [{"xT": xT[c], "w_in": wl, "g_pre": g_pre} for c in range(N_CORES)])
        lamv = _c(np.broadcast_to(np.stack([f(lambda_q1)[l], f(lambda_k1)[l], f(lambda_q2)[l], f(lambda_k2)[l]])[None],
                                  (128, 4, QK_DIM)))
        cst = _c(np.broadcast_to(np.array([lam_init, 1.0 - lam_init], np.float32)[None], (128, 2)))
        gsub = _c(f(subln_gain)[l].reshape(128, 1))
        inb = []
        for b in range(BATCH):
            for hp in range(RPB):
                rows = slice(hp * 256, (hp + 1) * 256)
                inb.append({
                    "qT": np.ascontiguousarray(np.concatenate([ra[b * RPB + r]["qT"][rows] for r in range(RPB)], axis=1)),
                    "kT": np.ascontiguousarray(np.concatenate([ra[b * RPB + r]["kT"][rows] for r in range(RPB)], axis=1)),
                    "v": np.ascontiguousarray(np.concatenate([ra[b * RPB + r]["v"][:, rows] for r in range(RPB)], axis=0)),
                    "U": _c(U[:, 2 * hp:2 * hp + 2, :]), "cfar": _c(cfar[:, 2 * hp:2 * hp + 2]),
                    "lamv": lamv, "cst": cst, "gsub": gsub})
        rb = _run("B", inb)
        del inb
        gvec = _c(np.concatenate([f(conv_norm_gain)[l].reshape(8, 128).T, f(norm_mix_post)[l].reshape(KC, 128).T,
                                  f(norm_ffn_pre)[l].reshape(KC, 128).T, f(norm_ffn_post)[l].reshape(KC, 128).T], axis=1))
        convw = _c(f(conv_w)[l].reshape(3, 8, 128).transpose(2, 1, 0))
        wo, wg, wu, wd = _c(w_out[l]), _c(w_gate[l]), _c(w_up[l]), _c(w_down[l])
        inc = []
        for c in range(N_CORES):
            b, r = divmod(c, RPB)
            aT = np.ascontiguousarray(np.concatenate(
                [rb[b * RPB + hp]["aT"][:, r * TOK:(r + 1) * TOK] for hp in range(RPB)], axis=0))
            halo = {}
            for nm in ("gcT", "hcT"):
                cur = np.asarray(ra[c][nm])
                if r == 0:
                    left = np.zeros((C_W, 2), np.float32)
                else:
                    left = np.asarray(ra[c - 1][nm])[:, TOK - 2:TOK]
                halo[nm] = np.ascontiguousarray(np.concatenate([left, cur], axis=1))
            inc.append({"xT": xT[c], "aT": aT, "gbT": np.ascontiguousarray(ra[c]["gbT"]), "gcT": halo["gcT"],
                        "hcT": halo["hcT"], "convw": convw, "gvec": gvec, "w_out": wo, "w_gate": wg, "w_up": wu,
                        "w_down": wd})
        del ra, rb
        rc = _run("C", inc)
        del inc
        xT = [np.ascontiguousarray(rc[c]["xoT"]) for c in range(N_CORES)]
        del rc
    out = np.concatenate([t.T for t in xT], axis=0).reshape(BATCH, SEQ, D_MODEL)
    return np.ascontiguousarray(out, dtype=np.float32)
```

```python
import math
from contextlib import ExitStack

import numpy as np
import ml_dtypes

import concourse.bass as bass
import concourse.mybir as mybir
from concourse.bass_utils import run_bass_kernel_spmd

F32 = mybir.dt.float32
BF16 = mybir.dt.bfloat16
AF = mybir.ActivationFunctionType
ALU = mybir.AluOpType
NPBF16 = ml_dtypes.bfloat16

D_MODEL = 2048
BATCH = 2
SEQ = 16384
DEPTH = 4
A_W = 1024
C_W = 1024
N_HEADS = 8
HEAD_DIM = 128
QK_DIM = 64
D_FF = 5632
IN_COLS = 6144
N_CORES = 8
TOK = BATCH * SEQ // N_CORES
KC = D_MODEL // 128
NORM_EPS = 1e-6
SUBLN_EPS = 1e-5
NEG = -30000.0


class Sched:
    def __init__(self, nc, stack):
        self.nc = nc
        self.stack = stack
        self.eng = {}
        for name, h in (("pe", nc.tensor), ("act", nc.scalar), ("dve", nc.vector),
                        ("pool", nc.gpsimd), ("sp", nc.sync)):
            sem = stack.enter_context(nc.semaphore("sem_" + name))
            self.eng[name] = dict(h=h, sem=sem, n=0, waited={}, name=name)
        self.last_w = {}
        self.readers = {}
        self.slots = {}

    def _deps(self, reads, writes):
        deps = []
        for k in reads:
            t = self.last_w.get(k)
            if t is not None:
                deps.append((t, "raw"))
        for k in writes:
            t = self.last_w.get(k)
            if t is not None:
                deps.append((t, "waw"))
            for t in self.readers.get(k, {}).values():
                deps.append((t, "war"))
        return deps

    def _emit_waits(self, e, deps):
        need = {}
        for (tok, kind) in deps:
            sem, val, src = tok
            if src == e["name"]:
                if src in ("pe", "sp") or kind == "war":
                    continue
            key = id(sem)
            if e["waited"].get(key, 0) >= val:
                continue
            if key not in need or need[key][1] < val:
                need[key] = (sem, val)
        for key, (sem, val) in need.items():
            e["h"].wait_ge(sem, val)
            e["waited"][key] = val

    def _record(self, tok, reads, writes):
        for k in writes:
            self.last_w[k] = tok
            self.readers[k] = {}
        for k in reads:
            self.readers.setdefault(k, {})[id(tok[0])] = tok

    def op(self, eng, fn, reads=(), writes=()):
        e = self.eng[eng]
        self._emit_waits(e, self._deps(reads, writes))
        ins = fn()
        e["n"] += 1
        ins.then_inc(e["sem"], 1)
        tok = (e["sem"], e["n"], eng)
        self._record(tok, reads, writes)
        return tok

    def dma(self, queue, out, in_, slot, reads=(), writes=()):
        e = self.eng[queue]
        self._emit_waits(e, self._deps(reads, writes))
        if slot not in self.slots:
            sem = self.stack.enter_context(self.nc.semaphore("dsem_" + slot))
            self.slots[slot] = [sem, 0]
        s = self.slots[slot]
        e["h"].dma_start(out=out, in_=in_).then_inc(s[0], 16)
        s[1] += 16
        tok = (s[0], s[1], "dma")
        self._record(tok, reads, writes)
        return tok

    def barrier(self):
        toks = set()
        for t in self.last_w.values():
            toks.add(t)
        for ts in self.readers.values():
            toks.update(ts.values())
        best = {}
        for (sem, val, src) in toks:
            if id(sem) not in best or best[id(sem)][1] < val:
                best[id(sem)] = (sem, val)
        for e in self.eng.values():
            for key, (sem, val) in best.items():
                if e["sem"] is sem:
                    continue
                if e["waited"].get(key, 0) >= val:
                    continue
                e["h"].wait_ge(sem, val)
                e["waited"][key] = val
        self.last_w.clear()
        self.readers.clear()

    def finish(self):
        e = self.eng["sp"]
        best = {}
        toks = set(self.last_w.values())
        for ts in self.readers.values():
            toks.update(ts.values())
        for (sem, val, src) in toks:
            if id(sem) not in best or best[id(sem)][1] < val:
                best[id(sem)] = (sem, val)
        for key, (sem, val) in best.items():
            if e["sem"] is sem:
                continue
            e["h"].wait_ge(sem, val)


def _mm_group(nc, out, pairs):
    n = len(pairs)
    ins = None
    for i, (l, r) in enumerate(pairs):
        ins = nc.tensor.matmul(out, l, r, start=(i == 0), stop=(i == n - 1))
    return ins


def build_A():
    nc = bass.Bass("TRN2", target_bir_lowering=False)
    TS = 2048
    NS = TOK // TS
    SUB = 256
    xT = nc.dram_tensor("xT", [D_MODEL, TOK], F32, kind="ExternalInput").ap()
    w_in = nc.dram_tensor("w_in", [D_MODEL, IN_COLS], F32, kind="ExternalInput").ap()
    g_pre = nc.dram_tensor("g_pre", [128, KC], F32, kind="ExternalInput").ap()
    qT = nc.dram_tensor("qT", [A_W, TOK], BF16, kind="ExternalOutput").ap()
    kT = nc.dram_tensor("kT", [A_W, TOK], BF16, kind="ExternalOutput").ap()
    v = nc.dram_tensor("v", [TOK, A_W], BF16, kind="ExternalOutput").ap()
    gbT = nc.dram_tensor("gbT", [C_W, TOK], F32, kind="ExternalOutput").ap()
    gcT = nc.dram_tensor("gcT", [C_W, TOK], F32, kind="ExternalOutput").ap()
    hcT = nc.dram_tensor("hcT", [C_W, TOK], F32, kind="ExternalOutput").ap()
    xT_v = xT.rearrange("(kc p) t -> p kc t", p=128)
    w_v = w_in.rearrange("(kc p) c -> p kc c", p=128)

    with ExitStack() as st:
        T = lambda name, shape, dt: st.enter_context(nc.sbuf_tensor(name, shape, dt))
        xt = [T(f"xt{i}", [128, KC, SUB], F32) for i in range(2)]
        sq = T("sq", [128, KC * SUB], BF16)
        hn = T("hn", [128, KC, TS], BF16)
        wb = [T(f"wb{i}", [128, KC, 512], BF16) for i in range(2)]
        stg = [T(f"stg{i}", [128, 2048], F32) for i in range(2)]
        stgb = [T(f"stgb{i}", [128, 2048], BF16) for i in range(2)]
        gp = T("gp", [128, KC], F32)
        ones = T("ones", [128, 128], BF16)
        epsn = T("epsn", [128, 1], F32)
        rstd = [T(f"rstd{i}", [128, SUB], F32) for i in range(2)]
        ps = [st.enter_context(nc.psum_tensor(f"ps{i}", [128, 1024], F32)) for i in range(4)]
        bank = lambda i: ps[i // 2][:, (i % 2) * 512:(i % 2) * 512 + 512]
        st.enter_context(nc.Block())
        S = Sched(nc, st)

        S.dma("sp", gp[:], g_pre, "gp", writes=["gp"])
        S.op("dve", lambda: nc.vector.memset(ones[:], 1.0 / D_MODEL), writes=["ones"])
        S.op("dve", lambda: nc.vector.memset(epsn[:], NORM_EPS), writes=["epsn"])

        nmm = 0
        nst = 0
        for s in range(NS):
            for j in range(TS // SUB):
                b = j % 2
                t0 = s * TS + j * SUB
                S.dma("sp", xt[b][:], xT_v[:, :, t0:t0 + SUB], f"xt{b}", writes=[f"xt{b}"])
                S.op("act", lambda: nc.scalar.activation(out=sq[:], in_=xt[b][:].rearrange("p k t -> p (k t)"),
                                                         func=AF.Square),
                     reads=[f"xt{b}"], writes=["sq"])
                S.op("pe", lambda: _mm_group(nc, bank(7)[:, 0:SUB],
                                             [(ones[:], sq[:, k * SUB:(k + 1) * SUB]) for k in range(KC)]),
                     reads=["ones", "sq"], writes=["bank7"])
                S.op("act", lambda: nc.scalar.activation(out=rstd[b][:], in_=bank(7)[:, 0:SUB], func=AF.Sqrt,
                                                         bias=epsn[:], scale=1.0),
                     reads=["bank7", "epsn"], writes=[f"rstd{b}"])
                S.op("dve", lambda: nc.vector.reciprocal(rstd[b][:], rstd[b][:]),
                     reads=[f"rstd{b}"], writes=[f"rstd{b}"])
                for k in range(KC):
                    S.op("dve", lambda: nc.vector.scalar_tensor_tensor(
                        hn[:, k, j * SUB:(j + 1) * SUB], xt[b][:, k, :], gp[:, k:k + 1], rstd[b][:],
                        ALU.mult, ALU.mult),
                        reads=[f"xt{b}", "gp", f"rstd{b}"], writes=[("hn", j)])
            hn_keys = [("hn", j) for j in range(TS // SUB)]
            for blk in range(IN_COLS // 512):
                wbi = blk % 2
                S.dma("pool", wb[wbi][:], w_v[:, :, blk * 512:(blk + 1) * 512], f"wb{wbi}", writes=[f"wb{wbi}"])
                if blk in (4, 5):
                    for g4 in range(TS // 512):
                        si = nst % 2
                        nst += 1
                        for tt in range(4):
                            tq = g4 * 4 + tt
                            bk = nmm % 6
                            nmm += 1
                            S.op("pe", lambda: _mm_group(nc, bank(bk), [
                                (hn[:, k, tq * 128:(tq + 1) * 128], wb[wbi][:, k, :]) for k in range(KC)]),
                                reads=hn_keys + [f"wb{wbi}"], writes=[f"bank{bk}"])
                            dst = stgb[si][:, tt * 512:(tt + 1) * 512]
                            if nmm % 2:
                                S.op("act", lambda: nc.scalar.copy(out=dst, in_=bank(bk)),
                                     reads=[f"bank{bk}"], writes=[f"stgb{si}"])
                            else:
                                S.op("dve", lambda: nc.vector.tensor_copy(dst, bank(bk)),
                                     reads=[f"bank{bk}"], writes=[f"stgb{si}"])
                        r0 = s * TS + g4 * 512
                        S.dma("sp", v[r0:r0 + 512, (blk - 4) * 512:(blk - 3) * 512].rearrange("(a p) c -> p a c", p=128),
                              stgb[si][:].rearrange("p (a c) -> p a c", a=4), f"stgb{si}",
                              reads=[f"stgb{si}"], writes=[("v", blk, s, g4)])
                    continue
                for cc in range(4):
                    col = blk * 512 + cc * 128
                    is_bf = col < 2 * A_W
                    si = nst % 2
                    nst += 1
                    stag = stgb[si] if is_bf else stg[si]
                    sname = (f"stgb{si}" if is_bf else f"stg{si}")
                    for t in range(TS // 512):
                        bk = nmm % 6
                        nmm += 1
                        S.op("pe", lambda: _mm_group(nc, bank(bk), [
                            (wb[wbi][:, k, cc * 128:(cc + 1) * 128], hn[:, k, t * 512:(t + 1) * 512])
                            for k in range(KC)]),
                            reads=hn_keys + [f"wb{wbi}"], writes=[f"bank{bk}"])
                        dst = stag[:, t * 512:(t + 1) * 512]
                        if nmm % 2:
                            S.op("act", lambda: nc.scalar.copy(out=dst, in_=bank(bk)),
                                 reads=[f"bank{bk}"], writes=[sname])
                        else:
                            S.op("dve", lambda: nc.vector.tensor_copy(dst, bank(bk)),
                                 reads=[f"bank{bk}"], writes=[sname])
                    if col < A_W:
                        dram = qT[col:col + 128, s * TS:(s + 1) * TS]
                    elif col < 2 * A_W:
                        dram = kT[col - A_W:col - A_W + 128, s * TS:(s + 1) * TS]
                    elif col < 3 * A_W + C_W:
                        c0 = col - 3 * A_W
                        dram = gbT[c0:c0 + 128, s * TS:(s + 1) * TS]
                    elif col < 3 * A_W + 2 * C_W:
                        c0 = col - 3 * A_W - C_W
                        dram = gcT[c0:c0 + 128, s * TS:(s + 1) * TS]
                    else:
                        c0 = col - 3 * A_W - 2 * C_W
                        dram = hcT[c0:c0 + 128, s * TS:(s + 1) * TS]
                    S.dma("sp", dram, stag[:], sname, reads=[sname], writes=[("o", col, s)])
        S.finish()
    return nc


def build_B(seq=SEQ, dbg=False):
    nc = bass.Bass("TRN2", target_bir_lowering=False)
    if dbg:
        dbg_o = nc.dram_tensor("dbg_o", [128, 6, 512], F32, kind="ExternalOutput").ap()
    NQB = seq // 512
    NKC = seq // 128
    qT = nc.dram_tensor("qT", [256, seq], BF16, kind="ExternalInput").ap()
    kT = nc.dram_tensor("kT", [256, seq], BF16, kind="ExternalInput").ap()
    v = nc.dram_tensor("v", [seq, 256], BF16, kind="ExternalInput").ap()
    U = nc.dram_tensor("U", [128, 2, 1024], F32, kind="ExternalInput").ap()
    cfar = nc.dram_tensor("cfar", [128, 2], F32, kind="ExternalInput").ap()
    lamv = nc.dram_tensor("lamv", [128, 4, 64], F32, kind="ExternalInput").ap()
    cst = nc.dram_tensor("cst", [128, 2], F32, kind="ExternalInput").ap()
    gsub = nc.dram_tensor("gsub", [128, 1], F32, kind="ExternalInput").ap()
    aT = nc.dram_tensor("aT", [256, seq], BF16, kind="ExternalOutput").ap()

    with ExitStack() as st:
        T = lambda name, shape, dt: st.enter_context(nc.sbuf_tensor(name, shape, dt))
        k_sb = T("k_sb", [128, 2, seq], BF16)
        v_sb = T("v_sb", [128, NKC, 256], BF16)
        qb = [T(f"qb{i}", [128, 512], BF16) for i in range(2)]
        U_sb = T("U_sb", [128, 2, 1024], F32)
        cf = T("cf", [128, 2], F32)
        lv = T("lv", [128, 4, 64], F32)
        cs = T("cs", [128, 2], F32)
        gs = T("gs", [128, 1], F32)
        gsc = T("gsc", [128, 1], F32)
        lam = T("lam", [128, 4], F32)
        prod = T("prod", [128, 64], F32)
        epss = T("epss", [128, 1], F32)
        ones = T("ones", [128, 128], BF16)
        onesf = T("onesf", [128, 128], F32)
        P = [T(f"P{i}", [128, 1024], BF16) for i in range(3)]
        tmp = [T(f"tmp{i}", [128, 1024], F32) for i in range(2)]
        rec = [T(f"rec{i}", [128, 512], F32) for i in range(2)]
        tt = [T(f"tt{i}", [128, 512], F32) for i in range(2)]
        sqf = T("sqf", [128, 512], F32)
        rs = T("rs", [128, 512], F32)
        ao = [T(f"ao{i}", [128, 512], BF16) for i in range(2)]
        ps = [st.enter_context(nc.psum_tensor(f"ps{i}", [128, 1024], F32)) for i in range(4)]
        st.enter_context(nc.Block())
        S = Sched(nc, st)

        for hl in range(2):
            S.dma("pool", k_sb[:, hl, :], kT[hl * 128:(hl + 1) * 128, :], "kv", writes=["kv"])
        vv = v.rearrange("(c p) e -> p c e", p=128)
        npiece = max(1, NKC // 16)
        for i in range(npiece):
            c0 = i * (NKC // npiece)
            c1 = (i + 1) * (NKC // npiece)
            S.dma("pool", v_sb[:, c0:c1, :], vv[:, c0:c1, :], "kv", writes=["kv"])
        S.dma("sp", U_sb[:], U, "cst", writes=["cst"])
        S.dma("sp", cf[:], cfar, "cst", writes=["cst"])
        S.dma("sp", lv[:], lamv, "cst", writes=["cst"])
        S.dma("sp", cs[:], cst, "cst", writes=["cst"])
        S.dma("sp", gs[:], gsub, "cst", writes=["cst"])
        S.op("dve", lambda: nc.vector.memset(ones[:], 1.0), writes=["ones"])
        S.op("dve", lambda: nc.vector.memset(onesf[:], 1.0 / HEAD_DIM), writes=["onesf"])
        S.op("dve", lambda: nc.vector.memset(epss[:], SUBLN_EPS), writes=["epss"])
        for i in range(2):
            S.op("dve", lambda: nc.vector.tensor_tensor(prod[:], lv[:, 2 * i, :], lv[:, 2 * i + 1, :], ALU.mult),
                 reads=["cst"], writes=["prod"])
            S.op("dve", lambda: nc.vector.reduce_sum(lam[:, i:i + 1], prod[:], axis=mybir.AxisListType.X),
                 reads=["prod"], writes=["lam"])
        S.op("act", lambda: nc.scalar.activation(out=lam[:, 0:2], in_=lam[:, 0:2], func=AF.Exp),
             reads=["lam"], writes=["lam"])
        S.op("dve", lambda: nc.vector.tensor_tensor(lam[:, 2:3], lam[:, 0:1], lam[:, 1:2], ALU.subtract),
             reads=["lam"], writes=["lam"])
        S.op("dve", lambda: nc.vector.tensor_tensor(lam[:, 2:3], lam[:, 2:3], cs[:, 0:1], ALU.add),
             reads=["lam", "cst"], writes=["lam"])
        S.op("dve", lambda: nc.vector.tensor_scalar(lam[:, 3:4], lam[:, 2:3], -1.0, None, ALU.mult),
             reads=["lam"], writes=["lam"])
        S.op("dve", lambda: nc.vector.tensor_tensor(gsc[:], gs[:], cs[:, 1:2], ALU.mult),
             reads=["cst"], writes=["gsc"])

        O1 = ps[2][:, 0:512]
        O2 = ps[2][:, 512:1024]
        R1 = ps[3][:, 0:512]
        R2 = ps[3][:, 512:1024]

        units = [(hl, j, kc) for hl in range(2) for j in range(NQB) for kc in range(4 * (j + 1))]
        cnt = dict(s=0, p=0, q=0, t=0, a=0)
        qslot_of = {}
        pending = []

        def load_q(hl, j):
            qs = cnt["q"] % 2
            cnt["q"] += 1
            qslot_of[(hl, j)] = qs
            S.dma("sp", qb[qs][:], qT[hl * 128:(hl + 1) * 128, j * 512:(j + 1) * 512], f"qb{qs}",
                  writes=[f"qb{qs}"])

        def emit_qk(u, sl):
            hl, j, kc = u
            if kc == 0:
                if (hl, j) not in qslot_of:
                    load_q(hl, j)
                nj, nh = (j + 1, hl) if j + 1 < NQB else (0, hl + 1)
                if nh < 2 and (nh, nj) not in qslot_of:
                    load_q(nh, nj)
            qs = qslot_of[(hl, j)]

            def f():
                nc.tensor.matmul(ps[sl][:, 0:512], k_sb[0:64, hl, kc * 128:(kc + 1) * 128], qb[qs][0:64, :],
                                 start=True, stop=True)
                return nc.tensor.matmul(ps[sl][:, 512:1024], k_sb[64:128, hl, kc * 128:(kc + 1) * 128],
                                        qb[qs][64:128, :], start=True, stop=True)
            S.op("pe", f, reads=["kv", f"qb{qs}"], writes=[f"S{sl}"])
            return sl

        def emit_exp(u, sl):
            hl, j, kc = u
            delta = j * 512 - kc * 128
            pslot = cnt["p"] % 3
            cnt["p"] += 1
            if delta >= 256:
                S.op("act", lambda: nc.scalar.activation(out=P[pslot][:], in_=ps[sl][:], func=AF.Exp,
                                                         bias=cf[:, hl:hl + 1], scale=QK_DIM ** -0.5),
                     reads=[f"S{sl}", "cst"], writes=[f"P{pslot}"])
            else:
                off = delta + 384
                ts_ = cnt["t"] % 2
                cnt["t"] += 1
                for c in range(2):
                    S.op("dve", lambda: nc.vector.scalar_tensor_tensor(
                        tmp[ts_][:, c * 512:(c + 1) * 512], ps[sl][:, c * 512:(c + 1) * 512], QK_DIM ** -0.5,
                        U_sb[:, hl, off:off + 512], ALU.mult, ALU.add),
                        reads=[f"S{sl}", "cst"] if c == 1 else ["cst"] + [f"S{sl}"], writes=[f"tmp{ts_}"])
                S.op("act", lambda: nc.scalar.activation(out=P[pslot][:], in_=tmp[ts_][:], func=AF.Exp),
                     reads=[f"tmp{ts_}"], writes=[f"P{pslot}"])
            return pslot

        def emit_pv(u, pslot):
            hl, j, kc = u
            first = (kc == 0)
            last = (kc == 4 * (j + 1) - 1)

            def f():
                vch = v_sb[:, kc, hl * 128:(hl + 1) * 128]
                nc.tensor.matmul(O1, vch, P[pslot][:, 0:512], start=first, stop=last)
                nc.tensor.matmul(O2, vch, P[pslot][:, 512:1024], start=first, stop=last)
                nc.tensor.matmul(R1, ones[:], P[pslot][:, 0:512], start=first, stop=last)
                return nc.tensor.matmul(R2, ones[:], P[pslot][:, 512:1024], start=first, stop=last)
            S.op("pe", f, reads=[f"P{pslot}", "kv", "ones"], writes=["OR"])
            if last:
                emit_epilogue(hl, j)

        def emit_epilogue(hl, j):
            S.op("dve", lambda: nc.vector.reciprocal(rec[0][:], R1), reads=["OR"], writes=["rec0"])
            S.op("dve", lambda: nc.vector.reciprocal(rec[1][:], R2), reads=["OR"], writes=["rec1"])
            S.op("dve", lambda: nc.vector.tensor_tensor(tt[0][:], O1, rec[0][:], ALU.mult),
                 reads=["OR", "rec0"], writes=["tt0"])
            S.op("dve", lambda: nc.vector.tensor_tensor(tt[1][:], O2, rec[1][:], ALU.mult),
                 reads=["OR", "rec1"], writes=["tt1"])
            S.op("dve", lambda: nc.vector.scalar_tensor_tensor(tt[0][:], tt[1][:], lam[:, 3:4], tt[0][:],
                                                               ALU.mult, ALU.add),
                 reads=["tt0", "tt1", "lam"], writes=["tt0"])
            S.op("dve", lambda: nc.vector.tensor_tensor(sqf[:], tt[0][:], tt[0][:], ALU.mult),
                 reads=["tt0"], writes=["sqf"])
            if dbg and hl == 0 and j == 0:
                S.dma("sp", dbg_o[:, 0, :], rec[0][:], "dbg", reads=["rec0"], writes=["dbg0"])
                S.dma("sp", dbg_o[:, 1, :], rec[1][:], "dbg", reads=["rec1"], writes=["dbg1"])
                S.dma("sp", dbg_o[:, 2, :], tt[0][:], "dbg", reads=["tt0"], writes=["dbg2"])
                S.dma("sp", dbg_o[:, 3, :], tt[1][:], "dbg", reads=["tt1"], writes=["dbg3"])
                S.dma("sp", dbg_o[:, 4, 0:4], lam[:], "dbg", reads=["lam"], writes=["dbg4"])
                S.dma("sp", dbg_o[:, 5, :], P[0][:, 0:256].bitcast(F32) if False else sqf[:], "dbg", reads=["sqf"], writes=["dbg5"])

            def later():
                sl = cnt["s"] % 2
                S.op("pe", lambda: nc.tensor.matmul(ps[sl][:, 0:512], onesf[:], sqf[:], start=True, stop=True),
                     reads=["onesf", "sqf"], writes=[f"S{sl}"])
                S.op("act", lambda: nc.scalar.activation(out=rs[:], in_=ps[sl][:, 0:512], func=AF.Sqrt,
                                                         bias=epss[:], scale=1.0),
                     reads=[f"S{sl}", "epss"], writes=["rs"])
                S.op("dve", lambda: nc.vector.reciprocal(rs[:], rs[:]), reads=["rs"], writes=["rs"])
                a_ = cnt["a"] % 2
                cnt["a"] += 1
                S.op("dve", lambda: nc.vector.scalar_tensor_tensor(ao[a_][:], tt[0][:], gsc[:, 0:1], rs[:],
                                                                   ALU.mult, ALU.mult),
                     reads=["tt0", "gsc", "rs"], writes=[f"ao{a_}"])
                S.dma("sp", aT[hl * 128:(hl + 1) * 128, j * 512:(j + 1) * 512], ao[a_][:], f"ao{a_}",
                      reads=[f"ao{a_}"], writes=[("aT", hl, j)])
            pending.append(later)

        emit_qk(units[0], 0)
        since = 0
        for n, u in enumerate(units):
            if n + 1 < len(units):
                emit_qk(units[n + 1], (n + 1) % 2)
            pslot = emit_exp(u, n % 2)
            cnt["s"] = n
            npend = len(pending)
            emit_pv(u, pslot)
            if npend:
                since += 1
                if since >= 3:
                    pending.pop(0)()
                    since = 0
        while pending:
            pending.pop(0)()
        S.finish()
    return nc


def _rel_bucket_np(dist):
    n = np.maximum(dist, 0)
    me = 16
    nf = np.maximum(n, me).astype(np.float32)
    large = me + (np.log(nf / np.float32(me)) / np.float32(math.log(128 / me)) * np.float32(32 - me)).astype(np.int32)
    large = np.minimum(large, 31)
    return np.where(n < me, n, large)


def _bias_tables(rel_bias):
    kk = np.arange(128)[:, None]
    j = np.arange(1024)[None, :]
    dist = j - kk - 384
    idx = _rel_bucket_np(dist)
    U = rel_bias[idx]
    U = np.where((dist >= 0)[:, :, None], U, np.float32(NEG)).astype(np.float32)
    U = np.ascontiguousarray(U.transpose(0, 2, 1))
    cfar = np.ascontiguousarray(np.broadcast_to(rel_bias[31][None, :], (128, rel_bias.shape[1]))).astype(np.float32)
    return U, cfar


def build_C(tok=TOK, stop=None):
    nc = bass.Bass("TRN2", target_bir_lowering=False)
    TS = 1024
    NS = tok // TS
    NFF = D_FF // 128
    xT = nc.dram_tensor("xT", [D_MODEL, tok], F32, kind="ExternalInput").ap()
    aT = nc.dram_tensor("aT", [A_W, tok], BF16, kind="ExternalInput").ap()
    gbT = nc.dram_tensor("gbT", [C_W, tok], F32, kind="ExternalInput").ap()
    gcT = nc.dram_tensor("gcT", [C_W, tok + 2], F32, kind="ExternalInput").ap()
    hcT = nc.dram_tensor("hcT", [C_W, tok + 2], F32, kind="ExternalInput").ap()
    convw = nc.dram_tensor("convw", [128, 8, 3], F32, kind="ExternalInput").ap()
    gvec = nc.dram_tensor("gvec", [128, 56], F32, kind="ExternalInput").ap()
    w_out = nc.dram_tensor("w_out", [D_MODEL, D_MODEL], F32, kind="ExternalInput").ap()
    w_gate = nc.dram_tensor("w_gate", [D_MODEL, D_FF], F32, kind="ExternalInput").ap()
    w_up = nc.dram_tensor("w_up", [D_MODEL, D_FF], F32, kind="ExternalInput").ap()
    w_down = nc.dram_tensor("w_down", [D_FF, D_MODEL], F32, kind="ExternalInput").ap()
    xoT = nc.dram_tensor("xoT", [D_MODEL, tok], F32, kind="ExternalOutput").ap()
    fscr = nc.dram_tensor("fscr", [D_MODEL, tok], F32).ap()
    wo_v = w_out.rearrange("(kc p) c -> p kc c", p=128)
    wg_v = w_gate.rearrange("(kc p) c -> p kc c", p=128)
    wu_v = w_up.rearrange("(kc p) c -> p kc c", p=128)
    wd_v = w_down.rearrange("(kc p) c -> p kc c", p=128)
    xT_v = xT.rearrange("(kc p) t -> p kc t", p=128)
    xo_v = xoT.rearrange("(kc p) t -> p kc t", p=128)
    fs_v = fscr.rearrange("(kc p) t -> p kc t", p=128)
    aT_v = aT.rearrange("(kc p) t -> p kc t", p=128)
    GC, GP, GF, GO = 0, 8, 24, 40

    with ExitStack() as st:
        T = lambda name, shape, dt: st.enter_context(nc.sbuf_tensor(name, shape, dt))
        cw = T("cw", [128, 8, 3], F32)
        gv = T("gv", [128, 56], F32)
        onesD = T("onesD", [128, 128], BF16)
        onesC = T("onesC", [128, 128], BF16)
        epsn = T("epsn", [128, 1], F32)
        rsf = T("rsf", [128, 1024], F32)
        ftp = [T(f"ftp{i}", [128, 512], F32) for i in range(2)]
        x1p = [T(f"x1p{i}", [128, 512], F32) for i in range(2)]
        p4_items = []
        ps = [st.enter_context(nc.psum_tensor(f"ps{i}", [128, 1024], F32)) for i in range(4)]
        bank = lambda i: ps[i // 2][:, (i % 2) * 512:(i % 2) * 512 + 512]
        st.enter_context(nc.Block())
        S = Sched(nc, st)
        S.dma("sp", cw[:], convw, "cst", writes=["cst"])
        S.dma("sp", gv[:], gvec, "cst", writes=["cst"])
        S.op("dve", lambda: nc.vector.memset(onesD[:], 1.0 / D_MODEL), writes=["onesD"])
        S.op("dve", lambda: nc.vector.memset(onesC[:], 1.0 / C_W), writes=["onesC"])
        S.op("dve", lambda: nc.vector.memset(epsn[:], NORM_EPS), writes=["epsn"])
        cnt = dict(b=0, e=0)

        def evac(dst, src_bank, bkey, wkey):
            cnt["e"] += 1
            if cnt["e"] % 2:
                S.op("act", lambda: nc.scalar.copy(out=dst, in_=src_bank), reads=[bkey], writes=[wkey])
            else:
                S.op("dve", lambda: nc.vector.tensor_copy(dst, src_bank), reads=[bkey], writes=[wkey])

        def rstd_from(dst, stat_ap, skey, dkey):
            S.op("act", lambda: nc.scalar.activation(out=dst, in_=stat_ap, func=AF.Sqrt, bias=epsn[:], scale=1.0),
                 reads=[skey, "epsn"], writes=[dkey])
            S.op("dve", lambda: nc.vector.reciprocal(dst, dst), reads=[dkey], writes=[dkey])

        for s in range(NS):
            T0 = s * TS
            with ExitStack() as sup:
                TT = lambda name, shape, dt, ctx=sup: ctx.enter_context(nc.sbuf_tensor(f"{name}_{s}", shape, dt))
                hn2 = TT("hn2", [128, KC, TS], BF16)
                with ExitStack() as p1:
                    P1 = lambda name, shape, dt: p1.enter_context(nc.sbuf_tensor(f"{name}_{s}", shape, dt))
                    cat = [P1(f"cat{i}", [128, KC, 512], BF16) for i in range(2)]
                    mix = P1("mix", [128, KC, 512], F32)
                    cpre = P1("cpre", [128, 8, 512], F32)
                    gct = [P1(f"gct{i}", [128, 514], F32) for i in range(2)]
                    hct = [P1(f"hct{i}", [128, 514], F32) for i in range(2)]
                    gbt = [P1(f"gbt{i}", [128, 512], F32) for i in range(2)]
                    ut = P1("ut", [128, 514], F32)
                    yt = P1("yt", [128, 512], F32)
                    wb = [P1(f"wb{i}", [128, KC, 256], BF16) for i in range(2)]
                    xch = [P1(f"xch{i}", [128, 512], F32) for i in range(2)]
                    sqb = [P1(f"sqb{i}", [128, 512], BF16) for i in range(2)]
                    sqc = [P1(f"sqc{i}", [128, 512], BF16) for i in range(2)]
                    rsd = P1("rsd", [128, 512], F32)
                    rsc = P1("rsc", [128, 512], F32)
                    st1 = dict(nsq=0, nsc=0, ncv=0)

                    def conv_chunk(half, ch):
                        H0 = T0 + half * 512
                        b = st1["ncv"] % 2
                        st1["ncv"] += 1
                        rows = slice(ch * 128, (ch + 1) * 128)
                        S.dma("sp", gct[b][:], gcT[rows, H0:H0 + 514], f"cv{b}", writes=[f"cv{b}"])
                        S.dma("sp", hct[b][:], hcT[rows, H0:H0 + 514], f"cv{b}", writes=[f"cv{b}"])
                        S.dma("sp", gbt[b][:], gbT[rows, H0:H0 + 512], f"cv{b}", writes=[f"cv{b}"])
                        S.op("dve", lambda: nc.vector.tensor_tensor(ut[:], gct[b][:], hct[b][:], ALU.mult),
                             reads=[f"cv{b}"], writes=["ut"])
                        S.op("dve", lambda: nc.vector.tensor_scalar_mul(yt[:], ut[:, 2:514], cw[:, ch, 2:3]),
                             reads=["ut", "cst"], writes=["yt"])
                        S.op("dve", lambda: nc.vector.scalar_tensor_tensor(yt[:], ut[:, 1:513], cw[:, ch, 1:2],
                                                                           yt[:], ALU.mult, ALU.add),
                             reads=["ut", "yt", "cst"], writes=["yt"])
                        S.op("dve", lambda: nc.vector.scalar_tensor_tensor(yt[:], ut[:, 0:512], cw[:, ch, 0:1],
                                                                           yt[:], ALU.mult, ALU.add),
                             reads=["ut", "yt", "cst"], writes=["yt"])
                        S.op("dve", lambda: nc.vector.tensor_tensor(cpre[:, ch, :], gbt[b][:], yt[:], ALU.mult),
                             reads=[f"cv{b}", "yt"], writes=[("cpre", ch)])
                        q_ = st1["nsc"] % 2
                        st1["nsc"] += 1
                        S.op("act", lambda: nc.scalar.activation(out=sqc[q_][:], in_=cpre[:, ch, :], func=AF.Square),
                             reads=[("cpre", ch)], writes=[f"sqc{q_}"])

                        def stats():
                            S.op("pe", lambda: nc.tensor.matmul(bank(7), onesC[:], sqc[q_][:], start=(ch == 0),
                                                                stop=(ch == 7)),
                                 reads=["onesC", f"sqc{q_}"], writes=["bank7"])
                        return stats

                    def conv_finish(half):
                        rstd_from(rsc[:], bank(7), "bank7", "rsc")
                        for ch in range(8):
                            S.op("dve", lambda: nc.vector.scalar_tensor_tensor(
                                cat[half][:, 8 + ch, :], cpre[:, ch, :], gv[:, GC + ch:GC + ch + 1], rsc[:],
                                ALU.mult, ALU.mult),
                                reads=[("cpre", ch), "cst", "rsc"], writes=[("catc", half, ch)])

                    def outproj(half, side=None):
                        cat_keys = [f"cata{half}"] + [("catc", half, ch) for ch in range(8)]
                        pend = []
                        for blk in range(8):
                            wbi = blk % 2
                            S.dma("pool", wb[wbi][:], wo_v[:, :, blk * 256:(blk + 1) * 256], f"wb{wbi}",
                                  writes=[f"wb{wbi}"])
                            for cc in range(2):
                                fc = blk * 2 + cc
                                bk = cnt["b"] % 6
                                cnt["b"] += 1
                                S.op("pe", lambda: _mm_group(nc, bank(bk), [
                                    (wb[wbi][:, k, cc * 128:(cc + 1) * 128], cat[half][:, k, :]) for k in range(KC)]),
                                    reads=cat_keys + [f"wb{wbi}"], writes=[f"bank{bk}"])
                                while pend:
                                    pend.pop(0)()
                                S.op("dve", lambda: nc.vector.tensor_copy(mix[:, fc, :], bank(bk)),
                                     reads=[f"bank{bk}"], writes=[("mix", fc)])
                                q_ = st1["nsq"] % 2
                                st1["nsq"] += 1
                                S.op("act", lambda: nc.scalar.activation(out=sqb[q_][:], in_=mix[:, fc, :], func=AF.Square),
                                     reads=[("mix", fc)], writes=[f"sqb{q_}"])
                                S.op("pe", lambda: nc.tensor.matmul(bank(6), onesD[:], sqb[q_][:], start=(fc == 0),
                                                                    stop=(fc == KC - 1)),
                                     reads=["onesD", f"sqb{q_}"], writes=["bank6"])
                                if side is not None and fc % 2 == 1:
                                    pend.append(conv_chunk(side, fc // 2))
                        while pend:
                            pend.pop(0)()
                        if side is not None:
                            conv_finish(side)
                        rstd_from(rsd[:], bank(6), "bank6", "rsd")

                    def x1phase(half):
                        H0 = T0 + half * 512
                        for fc in range(KC):
                            b = fc % 2
                            S.dma("sp", xch[b][:], xT_v[:, fc, H0:H0 + 512], f"xch{b}", writes=[f"xch{b}"])
                            S.op("dve", lambda: nc.vector.scalar_tensor_tensor(
                                mix[:, fc, :], mix[:, fc, :], gv[:, GP + fc:GP + fc + 1], rsd[:], ALU.mult, ALU.mult),
                                reads=[("mix", fc), "cst", "rsd"], writes=[("mix", fc)])
                            S.op("dve", lambda: nc.vector.tensor_tensor(mix[:, fc, :], mix[:, fc, :], xch[b][:], ALU.add),
                                 reads=[("mix", fc), f"xch{b}"], writes=[("mix", fc)])
                            q_ = st1["nsq"] % 2
                            st1["nsq"] += 1
                            S.op("act", lambda: nc.scalar.activation(out=sqb[q_][:], in_=mix[:, fc, :], func=AF.Square),
                                 reads=[("mix", fc)], writes=[f"sqb{q_}"])
                            S.op("pe", lambda: nc.tensor.matmul(bank(7), onesD[:], sqb[q_][:], start=(fc == 0),
                                                                stop=(fc == KC - 1)),
                                 reads=["onesD", f"sqb{q_}"], writes=["bank7"])
                        mix_keys = [("mix", fc) for fc in range(KC)]
                        S.dma("sp", xo_v[:, :, H0:H0 + 512], mix[:], "x1st", reads=mix_keys, writes=[("xo", s, half)])
                        rstd_from(rsd[:], bank(7), "bank7", "rsd")
                        for fc in range(KC):
                            S.op("dve", lambda: nc.vector.scalar_tensor_tensor(
                                hn2[:, fc, half * 512:(half + 1) * 512], mix[:, fc, :], gv[:, GF + fc:GF + fc + 1],
                                rsd[:], ALU.mult, ALU.mult),
                                reads=[("mix", fc), "cst", "rsd"], writes=[("hn2", half)])

                    for half in range(2):
                        H0 = T0 + half * 512
                        S.dma("sp", cat[half][:, 0:8, :], aT_v[:, :, H0:H0 + 512], f"cata{half}", writes=[f"cata{half}"])
                    for ch in range(8):
                        conv_chunk(0, ch)()
                    conv_finish(0)
                    outproj(0, side=1)
                    x1phase(0)
                    outproj(1)
                    x1phase(1)
                    S.barrier()
                if stop in ("p1", "p1a", "p1b"):
                    continue
                with ExitStack() as p23:
                    hT = p23.enter_context(nc.sbuf_tensor(f"hT_{s}", [128, NFF, TS], BF16))
                    with ExitStack() as p2:
                        P2 = lambda name, shape, dt: p2.enter_context(nc.sbuf_tensor(f"{name}_{s}", shape, dt))
                        wg = [P2(f"wg{i}", [128, KC, 256], BF16) for i in range(2)]
                        wu = [P2(f"wu{i}", [128, KC, 256], BF16) for i in range(2)]
                        sg = [P2(f"sg{i}", [128, 512], F32) for i in range(2)]
                        npair = 0
                        for fb in range(D_FF // 256):
                            wi = fb % 2
                            S.dma("pool", wg[wi][:], wg_v[:, :, fb * 256:(fb + 1) * 256], f"wg{wi}", writes=[f"wg{wi}"])
                            S.dma("pool", wu[wi][:], wu_v[:, :, fb * 256:(fb + 1) * 256], f"wu{wi}", writes=[f"wu{wi}"])
                            for cc in range(2):
                                ffc = fb * 2 + cc
                                for t in range(2):
                                    pr = npair % 3
                                    npair += 1
                                    bg, bu = 2 * pr, 2 * pr + 1
                                    S.op("pe", lambda: _mm_group(nc, bank(bg), [
                                        (wg[wi][:, k, cc * 128:(cc + 1) * 128], hn2[:, k, t * 512:(t + 1) * 512])
                                        for k in range(KC)]), reads=[f"wg{wi}"], writes=[f"bank{bg}"])
                                    S.op("pe", lambda: _mm_group(nc, bank(bu), [
                                        (wu[wi][:, k, cc * 128:(cc + 1) * 128], hn2[:, k, t * 512:(t + 1) * 512])
                                        for k in range(KC)]), reads=[f"wu{wi}"], writes=[f"bank{bu}"])
                                    g_ = npair % 2
                                    S.op("act", lambda: nc.scalar.activation(out=sg[g_][:], in_=bank(bg), func=AF.Silu),
                                         reads=[f"bank{bg}"], writes=[f"sg{g_}"])
                                    S.op("dve", lambda: nc.vector.tensor_tensor(hT[:, ffc, t * 512:(t + 1) * 512],
                                                                                sg[g_][:], bank(bu), ALU.mult),
                                         reads=[f"sg{g_}", f"bank{bu}"], writes=[("hT", ffc, t)])
                                    if p4_items:
                                        p4_items.pop(0)()
                        S.barrier()
                    if stop == "p2":
                        continue
                    with ExitStack() as p3:
                        P3 = lambda name, shape, dt: p3.enter_context(nc.sbuf_tensor(f"{name}_{s}", shape, dt))
                        wd = [P3(f"wd{i}", [128, NFF, 128], BF16) for i in range(2)]
                        fst = [P3(f"fst{i}", [128, TS], F32) for i in range(2)]
                        sqb = [P3(f"sqb3{i}", [128, 512], BF16) for i in range(2)]
                        nsq = 0
                        for fc in range(KC):
                            wi = fc % 2
                            fi = fc % 2
                            S.dma("pool", wd[wi][:], wd_v[:, :, fc * 128:(fc + 1) * 128], f"wd{wi}", writes=[f"wd{wi}"])
                            for t in range(2):
                                bk = cnt["b"] % 6
                                cnt["b"] += 1
                                S.op("pe", lambda: _mm_group(nc, bank(bk), [
                                    (wd[wi][:, k, :], hT[:, k, t * 512:(t + 1) * 512])
                                    for k in range(NFF)]), reads=[f"wd{wi}"], writes=[f"bank{bk}"])
                                S.op("dve", lambda: nc.vector.tensor_copy(fst[fi][:, t * 512:(t + 1) * 512], bank(bk)),
                                     reads=[f"bank{bk}"], writes=[f"fst{fi}"])
                                q_ = nsq % 2
                                nsq += 1
                                S.op("act", lambda: nc.scalar.activation(out=sqb[q_][:], in_=fst[fi][:, t * 512:(t + 1) * 512],
                                                                         func=AF.Square),
                                     reads=[f"fst{fi}"], writes=[f"sqb{q_}"])
                                S.op("pe", lambda: nc.tensor.matmul(bank(6 + t), onesD[:], sqb[q_][:], start=(fc == 0),
                                                                    stop=(fc == KC - 1)),
                                     reads=["onesD", f"sqb{q_}"], writes=[f"bank{6 + t}"])
                            S.dma("sp", fs_v[:, fc, T0:T0 + TS], fst[fi][:], f"fst{fi}", reads=[f"fst{fi}"],
                                  writes=[("fs", fc)])
                        for t in range(2):
                            rstd_from(rsf[:, t * 512:(t + 1) * 512], bank(6 + t), f"bank{6 + t}", "rsf")
                        S.barrier()
                    def mk_item(fc, t, T0=T0):
                        def item():
                            b = (fc * 2 + t) % 2
                            cols = slice(T0 + t * 512, T0 + (t + 1) * 512)
                            S.dma("sp", ftp[b][:], fs_v[:, fc, cols], f"ftp{b}", writes=[f"ftp{b}"])
                            S.dma("sp", x1p[b][:], xo_v[:, fc, cols], f"x1p{b}", writes=[f"x1p{b}"])
                            S.op("dve", lambda: nc.vector.scalar_tensor_tensor(
                                ftp[b][:], ftp[b][:], gv[:, GO + fc:GO + fc + 1], rsf[:, t * 512:(t + 1) * 512],
                                ALU.mult, ALU.mult), reads=[f"ftp{b}", "rsf"], writes=[f"ftp{b}"])
                            S.op("dve", lambda: nc.vector.tensor_tensor(ftp[b][:], ftp[b][:], x1p[b][:], ALU.add),
                                 reads=[f"ftp{b}", f"x1p{b}"], writes=[f"ftp{b}"])
                            S.dma("sp", xo_v[:, fc, cols], ftp[b][:], f"ftp{b}", reads=[f"ftp{b}"],
                                  writes=[("xof", fc, t, T0)])
                        return item
                    for fc in range(KC):
                        for t in range(2):
                            p4_items.append(mk_item(fc, t))
        while p4_items:
            p4_items.pop(0)()
        S.finish()
    return nc


_PROGS = {}


def _prog(name):
    if name not in _PROGS:
        _PROGS[name] = {"A": build_A, "B": build_B, "C": build_C}[name]()
    return _PROGS[name]


def _run(name, in_maps):
    res = run_bass_kernel_spmd(_prog(name), in_maps, core_ids=list(range(N_CORES)))
    return res.results


def _c(a, dt=np.float32):
    return np.ascontiguousarray(a, dtype=dt)


def kernel(x, w_in, w_out, lambda_q1, lambda_k1, lambda_q2, lambda_k2, subln_gain, conv_w, conv_norm_gain,
           rel_bias, w_gate, w_up, w_down, norm_mix_pre, norm_mix_post, norm_ffn_pre, norm_ffn_post):
    f = lambda a: np.asarray(a, dtype=np.float32)
    x = f(x)
    xs = x.reshape(BATCH * SEQ, D_MODEL)
    RPB = N_CORES // BATCH
    xT = [_c(xs[c * TOK:(c + 1) * TOK].T) for c in range(N_CORES)]
    U, cfar = _bias_tables(f(rel_bias))
    w_in, w_out, w_gate, w_up, w_down = f(w_in), f(w_out), f(w_gate), f(w_up), f(w_down)
    for l in range(DEPTH):
        lam_init = 0.8 - 0.6 * math.exp(-0.3 * l)
        g_pre = _c(f(norm_mix_pre)[l].reshape(KC, 128).T)
        wl = _c(w_in[l])
        ra = _run("A", [{"xT": xT[c], "w_in": wl, "g_pre": g_pre} for c in range(N_CORES)])
        lamv = _c(np.broadcast_to(np.stack([f(lambda_q1)[l], f(lambda_k1)[l], f(lambda_q2)[l], f(lambda_k2)[l]])[None],
                                  (128, 4, QK_DIM)))
        cst = _c(np.broadcast_to(np.array([lam_init, 1.0 - lam_init], np.float32)[None], (128, 2)))
        gsub = _c(f(subln_gain)[l].reshape(128, 1))
        inb = []
        for b in range(BATCH):
            for hp in range(RPB):
                rows = slice(hp * 256, (hp + 1) * 256)
                inb.append({
                    "qT": np.ascontiguousarray(np.concatenate([ra[b * RPB + r]["qT"][rows] for r in range(RPB)], axis=1)),
                    "kT": np.ascontiguousarray(np.concatenate([ra[b * RPB + r]["kT"][rows] for r in range(RPB)], axis=1)),
                    "v": np.ascontiguousarray(np.concatenate([ra[b * RPB + r]["v"][:, rows] for r in range(RPB)], axis=0)),
                    "U": _c(U[:, 2 * hp:2 * hp + 2, :]), "cfar": _c(cfar[:, 2 * hp:2 * hp + 2]),
                    "lamv": lamv, "cst": cst, "gsub": gsub})
        rb = _run("B", inb)
        del inb
        gvec = _c(np.concatenate([f(conv_norm_gain)[l].reshape(8, 128).T, f(norm_mix_post)[l].reshape(KC, 128).T,
                                  f(norm_ffn_pre)[l].reshape(KC, 128).T, f(norm_ffn_post)[l].reshape(KC, 128).T], axis=1))
        convw = _c(f(conv_w)[l].reshape(3, 8, 128).transpose(2, 1, 0))
        wo, wg, wu, wd = _c(w_out[l]), _c(w_gate[l]), _c(w_up[l]), _c(w_down[l])
        inc = []
        for c in range(N_CORES):
            b, r = divmod(c, RPB)
            aT = np.ascontiguousarray(np.concatenate(
                [rb[b * RPB + hp]["aT"][:, r * TOK:(r + 1) * TOK] for hp in range(RPB)], axis=0))
            halo = {}
            for nm in ("gcT", "hcT"):
                cur = np.asarray(ra[c][nm])
                if r == 0:
                    left = np.zeros((C_W, 2), np.float32)
                else:
                    left = np.asarray(ra[c - 1][nm])[:, TOK - 2:TOK]
                halo[nm] = np.ascontiguousarray(np.concatenate([left, cur], axis=1))
            inc.append({"xT": xT[c], "aT": aT, "gbT": np.ascontiguousarray(ra[c]["gbT"]), "gcT": halo["gcT"],
                        "hcT": halo["hcT"], "convw": convw, "gvec": gvec, "w_out": wo, "w_gate": wg, "w_up": wu,
                        "w_down": wd})
        del ra, rb
        rc = _run("C", inc)
        del inc
        xT = [np.ascontiguousarray(rc[c]["xoT"]) for c in range(N_CORES)]
        del rc
    out = np.concatenate([t.T for t in xT], axis=0).reshape(BATCH, SEQ, D_MODEL)
    return np.ascontiguousarray(out, dtype=np.float32)
```

```python
import math
from contextlib import ExitStack

import numpy as np
import ml_dtypes

import concourse.bass as bass
import concourse.mybir as mybir
from concourse.bass_utils import run_bass_kernel_spmd

F32 = mybir.dt.float32
BF16 = mybir.dt.bfloat16
AF = mybir.ActivationFunctionType
ALU = mybir.AluOpType
NPBF16 = ml_dtypes.bfloat16

D_MODEL = 2048
BATCH = 2
SEQ = 16384
DEPTH = 4
A_W = 1024
C_W = 1024
N_HEADS = 8
HEAD_DIM = 128
QK_DIM = 64
D_FF = 5632
IN_COLS = 6144
N_CORES = 8
TOK = BATCH * SEQ // N_CORES
KC = D_MODEL // 128
NORM_EPS = 1e-6
SUBLN_EPS = 1e-5
NEG = -30000.0


class Sched:
    def __init__(self, nc, stack):
        self.nc = nc
        self.stack = stack
        self.eng = {}
        for name, h in (("pe", nc.tensor), ("act", nc.scalar), ("dve", nc.vector),
                        ("pool", nc.gpsimd), ("sp", nc.sync)):
            sem = stack.enter_context(nc.semaphore("sem_" + name))
            self.eng[name] = dict(h=h, sem=sem, n=0, waited={}, name=name)
        self.last_w = {}
        self.readers = {}
        self.slots = {}

    def _deps(self, reads, writes):
        deps = []
        for k in reads:
            t = self.last_w.get(k)
            if t is not None:
                deps.append((t, "raw"))
        for k in writes:
            t = self.last_w.get(k)
            if t is not None:
                deps.append((t, "waw"))
            for t in self.readers.get(k, {}).values():
                deps.append((t, "war"))
        return deps

    def _emit_waits(self, e, deps):
        need = {}
        for (tok, kind) in deps:
            sem, val, src = tok
            if src == e["name"]:
                if src in ("pe", "sp") or kind == "war":
                    continue
            key = id(sem)
            if e["waited"].get(key, 0) >= val:
                continue
            if key not in need or need[key][1] < val:
                need[key] = (sem, val)
        for key, (sem, val) in need.items():
            e["h"].wait_ge(sem, val)
            e["waited"][key] = val

    def _record(self, tok, reads, writes):
        for k in writes:
            self.last_w[k] = tok
            self.readers[k] = {}
        for k in reads:
            self.readers.setdefault(k, {})[id(tok[0])] = tok

    def op(self, eng, fn, reads=(), writes=()):
        e = self.eng[eng]
        self._emit_waits(e, self._deps(reads, writes))
        ins = fn()
        e["n"] += 1
        ins.then_inc(e["sem"], 1)
        tok = (e["sem"], e["n"], eng)
        self._record(tok, reads, writes)
        return tok

    def dma(self, queue, out, in_, slot, reads=(), writes=()):
        e = self.eng[queue]
        self._emit_waits(e, self._deps(reads, writes))
        if slot not in self.slots:
            sem = self.stack.enter_context(self.nc.semaphore("dsem_" + slot))
            self.slots[slot] = [sem, 0]
        s = self.slots[slot]
        e["h"].dma_start(out=out, in_=in_).then_inc(s[0], 16)
        s[1] += 16
        tok = (s[0], s[1], "dma")
        self._record(tok, reads, writes)
        return tok

    def barrier(self):
        toks = set()
        for t in self.last_w.values():
            toks.add(t)
        for ts in self.readers.values():
            toks.update(ts.values())
        best = {}
        for (sem, val, src) in toks:
            if id(sem) not in best or best[id(sem)][1] < val:
                best[id(sem)] = (sem, val)
        for e in self.eng.values():
            for key, (sem, val) in best.items():
                if e["sem"] is sem:
                    continue
                if e["waited"].get(key, 0) >= val:
                    continue
                e["h"].wait_ge(sem, val)
                e["waited"][key] = val
        self.last_w.clear()
        self.readers.clear()

    def finish(self):
        e = self.eng["sp"]
        best = {}
        toks = set(self.last_w.values())
        for ts in self.readers.values():
            toks.update(ts.values())
        for (sem, val, src) in toks:
            if id(sem) not in best or best[id(sem)][1] < val:
                best[id(sem)] = (sem, val)
        for key, (sem, val) in best.items():
            if e["sem"] is sem:
                continue
            e["h"].wait_ge(sem, val)


def _mm_group(nc, out, pairs):
    n = len(pairs)
    ins = None
    for i, (l, r) in enumerate(pairs):
        ins = nc.tensor.matmul(out, l, r, start=(i == 0), stop=(i == n - 1))
    return ins


def build_A():
    nc = bass.Bass("TRN2", target_bir_lowering=False)
    TS = 2048
    NS = TOK // TS
    SUB = 256
    xT = nc.dram_tensor("xT", [D_MODEL, TOK], F32, kind="ExternalInput").ap()
    w_in = nc.dram_tensor("w_in", [D_MODEL, IN_COLS], F32, kind="ExternalInput").ap()
    g_pre = nc.dram_tensor("g_pre", [128, KC], F32, kind="ExternalInput").ap()
    qT = nc.dram_tensor("qT", [A_W, TOK], BF16, kind="ExternalOutput").ap()
    kT = nc.dram_tensor("kT", [A_W, TOK], BF16, kind="ExternalOutput").ap()
    v = nc.dram_tensor("v", [TOK, A_W], BF16, kind="ExternalOutput").ap()
    gbT = nc.dram_tensor("gbT", [C_W, TOK], F32, kind="ExternalOutput").ap()
    gcT = nc.dram_tensor("gcT", [C_W, TOK], F32, kind="ExternalOutput").ap()
    hcT = nc.dram_tensor("hcT", [C_W, TOK], F32, kind="ExternalOutput").ap()
    xT_v = xT.rearrange("(kc p) t -> p kc t", p=128)
    w_v = w_in.rearrange("(kc p) c -> p kc c", p=128)

    with ExitStack() as st:
        T = lambda name, shape, dt: st.enter_context(nc.sbuf_tensor(name, shape, dt))
        xt = [T(f"xt{i}", [128, KC, SUB], F32) for i in range(2)]
        sq = T("sq", [128, KC * SUB], BF16)
        hn = T("hn", [128, KC, TS], BF16)
        wb = [T(f"wb{i}", [128, KC, 512], BF16) for i in range(2)]
        stg = [T(f"stg{i}", [128, 2048], F32) for i in range(2)]
        stgb = [T(f"stgb{i}", [128, 2048], BF16) for i in range(2)]
        gp = T("gp", [128, KC], F32)
        ones = T("ones", [128, 128], BF16)
        epsn = T("epsn", [128, 1], F32)
        rstd = [T(f"rstd{i}", [128, SUB], F32) for i in range(2)]
        ps = [st.enter_context(nc.psum_tensor(f"ps{i}", [128, 1024], F32)) for i in range(4)]
        bank = lambda i: ps[i // 2][:, (i % 2) * 512:(i % 2) * 512 + 512]
        st.enter_context(nc.Block())
        S = Sched(nc, st)

        S.dma("sp", gp[:], g_pre, "gp", writes=["gp"])
        S.op("dve", lambda: nc.vector.memset(ones[:], 1.0 / D_MODEL), writes=["ones"])
        S.op("dve", lambda: nc.vector.memset(epsn[:], NORM_EPS), writes=["epsn"])

        nmm = 0
        nst = 0
        for s in range(NS):
            for j in range(TS // SUB):
                b = j % 2
                t0 = s * TS + j * SUB
                S.dma("sp", xt[b][:], xT_v[:, :, t0:t0 + SUB], f"xt{b}", writes=[f"xt{b}"])
                S.op("act", lambda: nc.scalar.activation(out=sq[:], in_=xt[b][:].rearrange("p k t -> p (k t)"),
                                                         func=AF.Square),
                     reads=[f"xt{b}"], writes=["sq"])
                S.op("pe", lambda: _mm_group(nc, bank(7)[:, 0:SUB],
                                             [(ones[:], sq[:, k * SUB:(k + 1) * SUB]) for k in range(KC)]),
                     reads=["ones", "sq"], writes=["bank7"])
                S.op("act", lambda: nc.scalar.activation(out=rstd[b][:], in_=bank(7)[:, 0:SUB], func=AF.Sqrt,
                                                         bias=epsn[:], scale=1.0),
                     reads=["bank7", "epsn"], writes=[f"rstd{b}"])
                S.op("dve", lambda: nc.vector.reciprocal(rstd[b][:], rstd[b][:]),
                     reads=[f"rstd{b}"], writes=[f"rstd{b}"])
                for k in range(KC):
                    S.op("dve", lambda: nc.vector.scalar_tensor_tensor(
                        hn[:, k, j * SUB:(j + 1) * SUB], xt[b][:, k, :], gp[:, k:k + 1], rstd[b][:],
                        ALU.mult, ALU.mult),
                        reads=[f"xt{b}", "gp", f"rstd{b}"], writes=[("hn", j)])
            hn_keys = [("hn", j) for j in range(TS // SUB)]
            for blk in range(IN_COLS // 512):
                wbi = blk % 2
                S.dma("pool", wb[wbi][:], w_v[:, :, blk * 512:(blk + 1) * 512], f"wb{wbi}", writes=[f"wb{wbi}"])
                if blk in (4, 5):
                    for g4 in range(TS // 512):
                        si = nst % 2
                        nst += 1
                        for tt in range(4):
                            tq = g4 * 4 + tt
                            bk = nmm % 6
                            nmm += 1
                            S.op("pe", lambda: _mm_group(nc, bank(bk), [
                                (hn[:, k, tq * 128:(tq + 1) * 128], wb[wbi][:, k, :]) for k in range(KC)]),
                                reads=hn_keys + [f"wb{wbi}"], writes=[f"bank{bk}"])
                            dst = stgb[si][:, tt * 512:(tt + 1) * 512]
                            if nmm % 2:
                                S.op("act", lambda: nc.scalar.copy(out=dst, in_=bank(bk)),
                                     reads=[f"bank{bk}"], writes=[f"stgb{si}"])
                            else:
                                S.op("dve", lambda: nc.vector.tensor_copy(dst, bank(bk)),
                                     reads=[f"bank{bk}"], writes=[f"stgb{si}"])
                        r0 = s * TS + g4 * 512
                        S.dma("sp", v[r0:r0 + 512, (blk - 4) * 512:(blk - 3) * 512].rearrange("(a p) c -> p a c", p=128),
                              stgb[si][:].rearrange("p (a c) -> p a c", a=4), f"stgb{si}",
                              reads=[f"stgb{si}"], writes=[("v", blk, s, g4)])
                    continue
                for cc in range(4):
                    col = blk * 512 + cc * 128
                    is_bf = col < 2 * A_W
                    si = nst % 2
                    nst += 1
                    stag = stgb[si] if is_bf else stg[si]
                    sname = (f"stgb{si}" if is_bf else f"stg{si}")
                    for t in range(TS // 512):
                        bk = nmm % 6
                        nmm += 1
                        S.op("pe", lambda: _mm_group(nc, bank(bk), [
                            (wb[wbi][:, k, cc * 128:(cc + 1) * 128], hn[:, k, t * 512:(t + 1) * 512])
                            for k in range(KC)]),
                            reads=hn_keys + [f"wb{wbi}"], writes=[f"bank{bk}"])
                        dst = stag[:, t * 512:(t + 1) * 512]
                        if nmm % 2:
                            S.op("act", lambda: nc.scalar.copy(out=dst, in_=bank(bk)),
                                 reads=[f"bank{bk}"], writes=[sname])
                        else:
                            S.op("dve", lambda: nc.vector.tensor_copy(dst, bank(bk)),
                                 reads=[f"bank{bk}"], writes=[sname])
                    if col < A_W:
                        dram = qT[col:col + 128, s * TS:(s + 1) * TS]
                    elif col < 2 * A_W:
                        dram = kT[col - A_W:col - A_W + 128, s * TS:(s + 1) * TS]
                    elif col < 3 * A_W + C_W:
                        c0 = col - 3 * A_W
                        dram = gbT[c0:c0 + 128, s * TS:(s + 1) * TS]
                    elif col < 3 * A_W + 2 * C_W:
                        c0 = col - 3 * A_W - C_W
                        dram = gcT[c0:c0 + 128, s * TS:(s + 1) * TS]
                    else:
                        c0 = col - 3 * A_W - 2 * C_W
                        dram = hcT[c0:c0 + 128, s * TS:(s + 1) * TS]
                    S.dma("sp", dram, stag[:], sname, reads=[sname], writes=[("o", col, s)])
        S.finish()
    return nc


def build_B(seq=SEQ, dbg=False):
    nc = bass.Bass("TRN2", target_bir_lowering=False)
    if dbg:
        dbg_o = nc.dram_tensor("dbg_o", [128, 6, 512], F32, kind="ExternalOutput").ap()
    NQB = seq // 512
    NKC = seq // 128
    qT = nc.dram_tensor("qT", [256, seq], BF16, kind="ExternalInput").ap()
    kT = nc.dram_tensor("kT", [256, seq], BF16, kind="ExternalInput").ap()
    v = nc.dram_tensor("v", [seq, 256], BF16, kind="ExternalInput").ap()
    U = nc.dram_tensor("U", [128, 2, 1024], F32, kind="ExternalInput").ap()
    cfar = nc.dram_tensor("cfar", [128, 2], F32, kind="ExternalInput").ap()
    lamv = nc.dram_tensor("lamv", [128, 4, 64], F32, kind="ExternalInput").ap()
    cst = nc.dram_tensor("cst", [128, 2], F32, kind="ExternalInput").ap()
    gsub = nc.dram_tensor("gsub", [128, 1], F32, kind="ExternalInput").ap()
    aT = nc.dram_tensor("aT", [256, seq], BF16, kind="ExternalOutput").ap()

    with ExitStack() as st:
        T = lambda name, shape, dt: st.enter_context(nc.sbuf_tensor(name, shape, dt))
        k_sb = T("k_sb", [128, 2, seq], BF16)
        v_sb = T("v_sb", [128, NKC, 256], BF16)
        qb = [T(f"qb{i}", [128, 512], BF16) for i in range(2)]
        U_sb = T("U_sb", [128, 2, 1024], F32)
        cf = T("cf", [128, 2], F32)
        lv = T("lv", [128, 4, 64], F32)
        cs = T("cs", [128, 2], F32)
        gs = T("gs", [128, 1], F32)
        gsc = T("gsc", [128, 1], F32)
        lam = T("lam", [128, 4], F32)
        prod = T("prod", [128, 64], F32)
        epss = T("epss", [128, 1], F32)
        ones = T("ones", [128, 128], BF16)
        onesf = T("onesf", [128, 128], F32)
        P = [T(f"P{i}", [128, 1024], BF16) for i in range(3)]
        tmp = [T(f"tmp{i}", [128, 1024], F32) for i in range(2)]
        rec = [T(f"rec{i}", [128, 512], F32) for i in range(2)]
        tt = [T(f"tt{i}", [128, 512], F32) for i in range(2)]
        sqf = T("sqf", [128, 512], F32)
        rs = T("rs", [128, 512], F32)
        ao = [T(f"ao{i}", [128, 512], BF16) for i in range(2)]
        ps = [st.enter_context(nc.psum_tensor(f"ps{i}", [128, 1024], F32)) for i in range(4)]
        st.enter_context(nc.Block())
        S = Sched(nc, st)

        PCS = min(2048, seq)
        NPC = seq // PCS
        vv = v.rearrange("(c p) e -> p c e", p=128)
        CPP = PCS // 128
        for hl in range(2):
            for i in range(NPC):
                S.dma("pool", k_sb[:, hl, i * PCS:(i + 1) * PCS], kT[hl * 128:(hl + 1) * 128, i * PCS:(i + 1) * PCS],
                      f"k{hl}_{i}", writes=[("k", hl, i)])
                if hl == 0:
                    S.dma("pool", v_sb[:, i * CPP:(i + 1) * CPP, :], vv[:, i * CPP:(i + 1) * CPP, :], f"v{i}",
                          writes=[("v", i)])
        S.dma("sp", U_sb[:], U, "cst", writes=["cst"])
        S.dma("sp", cf[:], cfar, "cst", writes=["cst"])
        S.dma("sp", lv[:], lamv, "cst", writes=["cst"])
        S.dma("sp", cs[:], cst, "cst", writes=["cst"])
        S.dma("sp", gs[:], gsub, "cst", writes=["cst"])
        S.op("dve", lambda: nc.vector.memset(ones[:], 1.0), writes=["ones"])
        S.op("dve", lambda: nc.vector.memset(onesf[:], 1.0 / HEAD_DIM), writes=["onesf"])
        S.op("dve", lambda: nc.vector.memset(epss[:], SUBLN_EPS), writes=["epss"])
        for i in range(2):
            S.op("dve", lambda: nc.vector.tensor_tensor(prod[:], lv[:, 2 * i, :], lv[:, 2 * i + 1, :], ALU.mult),
                 reads=["cst"], writes=["prod"])
            S.op("dve", lambda: nc.vector.reduce_sum(lam[:, i:i + 1], prod[:], axis=mybir.AxisListType.X),
                 reads=["prod"], writes=["lam"])
        S.op("act", lambda: nc.scalar.activation(out=lam[:, 0:2], in_=lam[:, 0:2], func=AF.Exp),
             reads=["lam"], writes=["lam"])
        S.op("dve", lambda: nc.vector.tensor_tensor(lam[:, 2:3], lam[:, 0:1], lam[:, 1:2], ALU.subtract),
             reads=["lam"], writes=["lam"])
        S.op("dve", lambda: nc.vector.tensor_tensor(lam[:, 2:3], lam[:, 2:3], cs[:, 0:1], ALU.add),
             reads=["lam", "cst"], writes=["lam"])
        S.op("dve", lambda: nc.vector.tensor_scalar(lam[:, 3:4], lam[:, 2:3], -1.0, None, ALU.mult),
             reads=["lam"], writes=["lam"])
        S.op("dve", lambda: nc.vector.tensor_tensor(gsc[:], gs[:], cs[:, 1:2], ALU.mult),
             reads=["cst"], writes=["gsc"])

        O1 = ps[2][:, 0:512]
        O2 = ps[2][:, 512:1024]
        R1 = ps[3][:, 0:512]
        R2 = ps[3][:, 512:1024]

        units = [(hl, j, kc) for hl in range(2) for j in range(NQB) for kc in range(4 * (j + 1))]
        cnt = dict(s=0, p=0, q=0, t=0, a=0)
        qslot_of = {}
        pending = []

        def load_q(hl, j):
            qs = cnt["q"] % 2
            cnt["q"] += 1
            qslot_of[(hl, j)] = qs
            S.dma("sp", qb[qs][:], qT[hl * 128:(hl + 1) * 128, j * 512:(j + 1) * 512], f"qb{qs}",
                  writes=[f"qb{qs}"])

        def emit_qk(u, sl):
            hl, j, kc = u
            if kc == 0:
                if (hl, j) not in qslot_of:
                    load_q(hl, j)
                nj, nh = (j + 1, hl) if j + 1 < NQB else (0, hl + 1)
                if nh < 2 and (nh, nj) not in qslot_of:
                    load_q(nh, nj)
            qs = qslot_of[(hl, j)]

            def f():
                nc.tensor.matmul(ps[sl][:, 0:512], k_sb[0:64, hl, kc * 128:(kc + 1) * 128], qb[qs][0:64, :],
                                 start=True, stop=True)
                return nc.tensor.matmul(ps[sl][:, 512:1024], k_sb[64:128, hl, kc * 128:(kc + 1) * 128],
                                        qb[qs][64:128, :], start=True, stop=True)
            S.op("pe", f, reads=[("k", hl, (kc * 128) // PCS), f"qb{qs}"], writes=[f"S{sl}"])
            return sl

        def emit_exp(u, sl):
            hl, j, kc = u
            delta = j * 512 - kc * 128
            pslot = cnt["p"] % 3
            cnt["p"] += 1
            if delta >= 256:
                for c in range(2):
                    cs = slice(c * 512, (c + 1) * 512)
                    S.op("act", lambda: nc.scalar.activation(out=P[pslot][:, cs], in_=ps[sl][:, cs], func=AF.Exp,
                                                             bias=cf[:, hl:hl + 1], scale=QK_DIM ** -0.5),
                         reads=[f"S{sl}", "cst"], writes=[(f"P{pslot}", c)])
            else:
                off = delta + 384
                ts_ = cnt["t"] % 2
                cnt["t"] += 1
                for c in range(2):
                    cs = slice(c * 512, (c + 1) * 512)
                    S.op("dve", lambda: nc.vector.scalar_tensor_tensor(
                        tmp[ts_][:, cs], ps[sl][:, cs], QK_DIM ** -0.5,
                        U_sb[:, hl, off:off + 512], ALU.mult, ALU.add),
                        reads=[f"S{sl}", "cst"], writes=[(f"tmp{ts_}", c)])
                    S.op("act", lambda: nc.scalar.activation(out=P[pslot][:, cs], in_=tmp[ts_][:, cs], func=AF.Exp),
                         reads=[(f"tmp{ts_}", c)], writes=[(f"P{pslot}", c)])
            return pslot

        def emit_pv(u, pslot):
            hl, j, kc = u
            first = (kc == 0)
            last = (kc == 4 * (j + 1) - 1)

            vch = v_sb[:, kc, hl * 128:(hl + 1) * 128]
            vkey = ("v", (kc * 128) // PCS)

            def fa():
                nc.tensor.matmul(O1, vch, P[pslot][:, 0:512], start=first, stop=last)
                return nc.tensor.matmul(R1, ones[:], P[pslot][:, 0:512], start=first, stop=last)

            def fb():
                nc.tensor.matmul(O2, vch, P[pslot][:, 512:1024], start=first, stop=last)
                return nc.tensor.matmul(R2, ones[:], P[pslot][:, 512:1024], start=first, stop=last)
            S.op("pe", fa, reads=[(f"P{pslot}", 0), vkey, "ones"], writes=["OR"])
            S.op("pe", fb, reads=[(f"P{pslot}", 1), vkey, "ones"], writes=["OR"])
            if last:
                emit_epilogue(hl, j)

        def emit_epilogue(hl, j):
            S.op("dve", lambda: nc.vector.reciprocal(rec[0][:], R1), reads=["OR"], writes=["rec0"])
            S.op("dve", lambda: nc.vector.reciprocal(rec[1][:], R2), reads=["OR"], writes=["rec1"])
            S.op("dve", lambda: nc.vector.tensor_tensor(tt[0][:], O1, rec[0][:], ALU.mult),
                 reads=["OR", "rec0"], writes=["tt0"])
            S.op("dve", lambda: nc.vector.tensor_tensor(tt[1][:], O2, rec[1][:], ALU.mult),
                 reads=["OR", "rec1"], writes=["tt1"])
            S.op("dve", lambda: nc.vector.scalar_tensor_tensor(tt[0][:], tt[1][:], lam[:, 3:4], tt[0][:],
                                                               ALU.mult, ALU.add),
                 reads=["tt0", "tt1", "lam"], writes=["tt0"])
            S.op("dve", lambda: nc.vector.tensor_tensor(sqf[:], tt[0][:], tt[0][:], ALU.mult),
                 reads=["tt0"], writes=["sqf"])
            if dbg and hl == 0 and j == 0:
                S.dma("sp", dbg_o[:, 0, :], rec[0][:], "dbg", reads=["rec0"], writes=["dbg0"])
                S.dma("sp", dbg_o[:, 1, :], rec[1][:], "dbg", reads=["rec1"], writes=["dbg1"])
                S.dma("sp", dbg_o[:, 2, :], tt[0][:], "dbg", reads=["tt0"], writes=["dbg2"])
                S.dma("sp", dbg_o[:, 3, :], tt[1][:], "dbg", reads=["tt1"], writes=["dbg3"])
                S.dma("sp", dbg_o[:, 4, 0:4], lam[:], "dbg", reads=["lam"], writes=["dbg4"])
                S.dma("sp", dbg_o[:, 5, :], P[0][:, 0:256].bitcast(F32) if False else sqf[:], "dbg", reads=["sqf"], writes=["dbg5"])

            def later():
                sl = cnt["s"] % 2
                S.op("pe", lambda: nc.tensor.matmul(ps[sl][:, 0:512], onesf[:], sqf[:], start=True, stop=True),
                     reads=["onesf", "sqf"], writes=[f"S{sl}"])
                S.op("act", lambda: nc.scalar.activation(out=rs[:], in_=ps[sl][:, 0:512], func=AF.Sqrt,
                                                         bias=epss[:], scale=1.0),
                     reads=[f"S{sl}", "epss"], writes=["rs"])
                S.op("dve", lambda: nc.vector.reciprocal(rs[:], rs[:]), reads=["rs"], writes=["rs"])
                a_ = cnt["a"] % 2
                cnt["a"] += 1
                S.op("dve", lambda: nc.vector.scalar_tensor_tensor(ao[a_][:], tt[0][:], gsc[:, 0:1], rs[:],
                                                                   ALU.mult, ALU.mult),
                     reads=["tt0", "gsc", "rs"], writes=[f"ao{a_}"])
                S.dma("sp", aT[hl * 128:(hl + 1) * 128, j * 512:(j + 1) * 512], ao[a_][:], f"ao{a_}",
                      reads=[f"ao{a_}"], writes=[("aT", hl, j)])
            pending.append(later)

        emit_qk(units[0], 0)
        since = 0
        for n, u in enumerate(units):
            if n + 1 < len(units):
                emit_qk(units[n + 1], (n + 1) % 2)
            pslot = emit_exp(u, n % 2)
            cnt["s"] = n
            npend = len(pending)
            emit_pv(u, pslot)
            if npend:
                since += 1
                if since >= 3:
                    pending.pop(0)()
                    since = 0
        while pending:
            pending.pop(0)()
        S.finish()
    return nc


def _rel_bucket_np(dist):
    n = np.maximum(dist, 0)
    me = 16
    nf = np.maximum(n, me).astype(np.float32)
    large = me + (np.log(nf / np.float32(me)) / np.float32(math.log(128 / me)) * np.float32(32 - me)).astype(np.int32)
    large = np.minimum(large, 31)
    return np.where(n < me, n, large)


def _bias_tables(rel_bias):
    kk = np.arange(128)[:, None]
    j = np.arange(1024)[None, :]
    dist = j - kk - 384
    idx = _rel_bucket_np(dist)
    U = rel_bias[idx]
    U = np.where((dist >= 0)[:, :, None], U, np.float32(NEG)).astype(np.float32)
    U = np.ascontiguousarray(U.transpose(0, 2, 1))
    cfar = np.ascontiguousarray(np.broadcast_to(rel_bias[31][None, :], (128, rel_bias.shape[1]))).astype(np.float32)
    return U, cfar


def build_C(tok=TOK, stop=None):
    nc = bass.Bass("TRN2", target_bir_lowering=False)
    TS = 1024
    NS = tok // TS
    NFF = D_FF // 128
    xT = nc.dram_tensor("xT", [D_MODEL, tok], F32, kind="ExternalInput").ap()
    aT = nc.dram_tensor("aT", [A_W, tok], BF16, kind="ExternalInput").ap()
    gbT = nc.dram_tensor("gbT", [C_W, tok], F32, kind="ExternalInput").ap()
    gcT = nc.dram_tensor("gcT", [C_W, tok + 2], F32, kind="ExternalInput").ap()
    hcT = nc.dram_tensor("hcT", [C_W, tok + 2], F32, kind="ExternalInput").ap()
    convw = nc.dram_tensor("convw", [128, 8, 3], F32, kind="ExternalInput").ap()
    gvec = nc.dram_tensor("gvec", [128, 56], F32, kind="ExternalInput").ap()
    w_out = nc.dram_tensor("w_out", [D_MODEL, D_MODEL], F32, kind="ExternalInput").ap()
    w_gate = nc.dram_tensor("w_gate", [D_MODEL, D_FF], F32, kind="ExternalInput").ap()
    w_up = nc.dram_tensor("w_up", [D_MODEL, D_FF], F32, kind="ExternalInput").ap()
    w_down = nc.dram_tensor("w_down", [D_FF, D_MODEL], F32, kind="ExternalInput").ap()
    xoT = nc.dram_tensor("xoT", [D_MODEL, tok], F32, kind="ExternalOutput").ap()
    fscr = nc.dram_tensor("fscr", [D_MODEL, tok], F32).ap()
    wo_v = w_out.rearrange("(kc p) c -> p kc c", p=128)
    wg_v = w_gate.rearrange("(kc p) c -> p kc c", p=128)
    wu_v = w_up.rearrange("(kc p) c -> p kc c", p=128)
    wd_v = w_down.rearrange("(kc p) c -> p kc c", p=128)
    xT_v = xT.rearrange("(kc p) t -> p kc t", p=128)
    xo_v = xoT.rearrange("(kc p) t -> p kc t", p=128)
    fs_v = fscr.rearrange("(kc p) t -> p kc t", p=128)
    aT_v = aT.rearrange("(kc p) t -> p kc t", p=128)
    GC, GP, GF, GO = 0, 8, 24, 40

    with ExitStack() as st:
        T = lambda name, shape, dt: st.enter_context(nc.sbuf_tensor(name, shape, dt))
        cw = T("cw", [128, 8, 3], F32)
        gv = T("gv", [128, 56], F32)
        onesD = T("onesD", [128, 128], BF16)
        onesC = T("onesC", [128, 128], BF16)
        epsn = T("epsn", [128, 1], F32)
        rsf = T("rsf", [128, 1024], F32)
        ftp = [T(f"ftp{i}", [128, 512], F32) for i in range(2)]
        x1p = [T(f"x1p{i}", [128, 512], F32) for i in range(2)]
        p4_items = []
        ps = [st.enter_context(nc.psum_tensor(f"ps{i}", [128, 1024], F32)) for i in range(4)]
        bank = lambda i: ps[i // 2][:, (i % 2) * 512:(i % 2) * 512 + 512]
        st.enter_context(nc.Block())
        S = Sched(nc, st)
        S.dma("sp", cw[:], convw, "cst", writes=["cst"])
        S.dma("sp", gv[:], gvec, "cst", writes=["cst"])
        S.op("dve", lambda: nc.vector.memset(onesD[:], 1.0 / D_MODEL), writes=["onesD"])
        S.op("dve", lambda: nc.vector.memset(onesC[:], 1.0 / C_W), writes=["onesC"])
        S.op("dve", lambda: nc.vector.memset(epsn[:], NORM_EPS), writes=["epsn"])
        cnt = dict(b=0, e=0)

        def evac(dst, src_bank, bkey, wkey):
            cnt["e"] += 1
            if cnt["e"] % 2:
                S.op("act", lambda: nc.scalar.copy(out=dst, in_=src_bank), reads=[bkey], writes=[wkey])
            else:
                S.op("dve", lambda: nc.vector.tensor_copy(dst, src_bank), reads=[bkey], writes=[wkey])

        def rstd_from(dst, stat_ap, skey, dkey):
            S.op("act", lambda: nc.scalar.activation(out=dst, in_=stat_ap, func=AF.Sqrt, bias=epsn[:], scale=1.0),
                 reads=[skey, "epsn"], writes=[dkey])
            S.op("dve", lambda: nc.vector.reciprocal(dst, dst), reads=[dkey], writes=[dkey])

        for s in range(NS):
            T0 = s * TS
            with ExitStack() as sup:
                TT = lambda name, shape, dt, ctx=sup: ctx.enter_context(nc.sbuf_tensor(f"{name}_{s}", shape, dt))
                hn2 = TT("hn2", [128, KC, TS], BF16)
                with ExitStack() as p1:
                    P1 = lambda name, shape, dt: p1.enter_context(nc.sbuf_tensor(f"{name}_{s}", shape, dt))
                    cat = [P1(f"cat{i}", [128, KC, 512], BF16) for i in range(2)]
                    mix = P1("mix", [128, KC, 512], F32)
                    cpre = P1("cpre", [128, 8, 512], F32)
                    gct = [P1(f"gct{i}", [128, 514], F32) for i in range(2)]
                    hct = [P1(f"hct{i}", [128, 514], F32) for i in range(2)]
                    gbt = [P1(f"gbt{i}", [128, 512], F32) for i in range(2)]
                    ut = P1("ut", [128, 514], F32)
                    yt = P1("yt", [128, 512], F32)
                    wb = [P1(f"wb{i}", [128, KC, 256], BF16) for i in range(2)]
                    xch = [P1(f"xch{i}", [128, 512], F32) for i in range(2)]
                    sqb = [P1(f"sqb{i}", [128, 512], BF16) for i in range(2)]
                    sqc = [P1(f"sqc{i}", [128, 512], BF16) for i in range(2)]
                    rsd = P1("rsd", [128, 512], F32)
                    rsc = P1("rsc", [128, 512], F32)
                    st1 = dict(nsq=0, nsc=0, ncv=0)

                    def conv_chunk(half, ch):
                        H0 = T0 + half * 512
                        b = st1["ncv"] % 2
                        st1["ncv"] += 1
                        rows = slice(ch * 128, (ch + 1) * 128)
                        S.dma("sp", gct[b][:], gcT[rows, H0:H0 + 514], f"cv{b}", writes=[f"cv{b}"])
                        S.dma("sp", hct[b][:], hcT[rows, H0:H0 + 514], f"cv{b}", writes=[f"cv{b}"])
                        S.dma("sp", gbt[b][:], gbT[rows, H0:H0 + 512], f"cv{b}", writes=[f"cv{b}"])
                        S.op("dve", lambda: nc.vector.tensor_tensor(ut[:], gct[b][:], hct[b][:], ALU.mult),
                             reads=[f"cv{b}"], writes=["ut"])
                        S.op("dve", lambda: nc.vector.tensor_scalar_mul(yt[:], ut[:, 2:514], cw[:, ch, 2:3]),
                             reads=["ut", "cst"], writes=["yt"])
                        S.op("dve", lambda: nc.vector.scalar_tensor_tensor(yt[:], ut[:, 1:513], cw[:, ch, 1:2],
                                                                           yt[:], ALU.mult, ALU.add),
                             reads=["ut", "yt", "cst"], writes=["yt"])
                        S.op("dve", lambda: nc.vector.scalar_tensor_tensor(yt[:], ut[:, 0:512], cw[:, ch, 0:1],
                                                                           yt[:], ALU.mult, ALU.add),
                             reads=["ut", "yt", "cst"], writes=["yt"])
                        S.op("dve", lambda: nc.vector.tensor_tensor(cpre[:, ch, :], gbt[b][:], yt[:], ALU.mult),
                             reads=[f"cv{b}", "yt"], writes=[("cpre", ch)])
                        q_ = st1["nsc"] % 2
                        st1["nsc"] += 1
                        S.op("act", lambda: nc.scalar.activation(out=sqc[q_][:], in_=cpre[:, ch, :], func=AF.Square),
                             reads=[("cpre", ch)], writes=[f"sqc{q_}"])

                        def stats():
                            S.op("pe", lambda: nc.tensor.matmul(bank(7), onesC[:], sqc[q_][:], start=(ch == 0),
                                                                stop=(ch == 7)),
                                 reads=["onesC", f"sqc{q_}"], writes=["bank7"])
                        return stats

                    def conv_finish(half):
                        rstd_from(rsc[:], bank(7), "bank7", "rsc")
                        for ch in range(8):
                            S.op("dve", lambda: nc.vector.scalar_tensor_tensor(
                                cat[half][:, 8 + ch, :], cpre[:, ch, :], gv[:, GC + ch:GC + ch + 1], rsc[:],
                                ALU.mult, ALU.mult),
                                reads=[("cpre", ch), "cst", "rsc"], writes=[("catc", half, ch)])

                    def outproj(half, side=None):
                        cat_keys = [f"cata{half}"] + [("catc", half, ch) for ch in range(8)]
                        pend = []
                        for blk in range(8):
                            wbi = blk % 2
                            S.dma("pool", wb[wbi][:], wo_v[:, :, blk * 256:(blk + 1) * 256], f"wb{wbi}",
                                  writes=[f"wb{wbi}"])
                            for cc in range(2):
                                fc = blk * 2 + cc
                                bk = cnt["b"] % 6
                                cnt["b"] += 1
                                S.op("pe", lambda: _mm_group(nc, bank(bk), [
                                    (wb[wbi][:, k, cc * 128:(cc + 1) * 128], cat[half][:, k, :]) for k in range(KC)]),
                                    reads=cat_keys + [f"wb{wbi}"], writes=[f"bank{bk}"])
                                while pend:
                                    pend.pop(0)()
                                S.op("dve", lambda: nc.vector.tensor_copy(mix[:, fc, :], bank(bk)),
                                     reads=[f"bank{bk}"], writes=[("mix", fc)])
                                q_ = st1["nsq"] % 2
                                st1["nsq"] += 1
                                S.op("act", lambda: nc.scalar.activation(out=sqb[q_][:], in_=mix[:, fc, :], func=AF.Square),
                                     reads=[("mix", fc)], writes=[f"sqb{q_}"])
                                S.op("pe", lambda: nc.tensor.matmul(bank(6), onesD[:], sqb[q_][:], start=(fc == 0),
                                                                    stop=(fc == KC - 1)),
                                     reads=["onesD", f"sqb{q_}"], writes=["bank6"])
                                if side is not None and fc % 2 == 1:
                                    pend.append(conv_chunk(side, fc // 2))
                        while pend:
                            pend.pop(0)()
                        if side is not None:
                            conv_finish(side)
                        rstd_from(rsd[:], bank(6), "bank6", "rsd")

                    def x1phase(half):
                        H0 = T0 + half * 512
                        for fc in range(KC):
                            b = fc % 2
                            S.dma("sp", xch[b][:], xT_v[:, fc, H0:H0 + 512], f"xch{b}", writes=[f"xch{b}"])
                            S.op("dve", lambda: nc.vector.scalar_tensor_tensor(
                                mix[:, fc, :], mix[:, fc, :], gv[:, GP + fc:GP + fc + 1], rsd[:], ALU.mult, ALU.mult),
                                reads=[("mix", fc), "cst", "rsd"], writes=[("mix", fc)])
                            S.op("dve", lambda: nc.vector.tensor_tensor(mix[:, fc, :], mix[:, fc, :], xch[b][:], ALU.add),
                                 reads=[("mix", fc), f"xch{b}"], writes=[("mix", fc)])
                            q_ = st1["nsq"] % 2
                            st1["nsq"] += 1
                            S.op("act", lambda: nc.scalar.activation(out=sqb[q_][:], in_=mix[:, fc, :], func=AF.Square),
                                 reads=[("mix", fc)], writes=[f"sqb{q_}"])
                            S.op("pe", lambda: nc.tensor.matmul(bank(7), onesD[:], sqb[q_][:], start=(fc == 0),
                                                                stop=(fc == KC - 1)),
                                 reads=["onesD", f"sqb{q_}"], writes=["bank7"])
                        mix_keys = [("mix", fc) for fc in range(KC)]
                        S.dma("sp", xo_v[:, :, H0:H0 + 512], mix[:], "x1st", reads=mix_keys, writes=[("xo", s, half)])
                        rstd_from(rsd[:], bank(7), "bank7", "rsd")
                        for fc in range(KC):
                            S.op("dve", lambda: nc.vector.scalar_tensor_tensor(
                                hn2[:, fc, half * 512:(half + 1) * 512], mix[:, fc, :], gv[:, GF + fc:GF + fc + 1],
                                rsd[:], ALU.mult, ALU.mult),
                                reads=[("mix", fc), "cst", "rsd"], writes=[("hn2", half)])

                    for half in range(2):
                        H0 = T0 + half * 512
                        S.dma("sp", cat[half][:, 0:8, :], aT_v[:, :, H0:H0 + 512], f"cata{half}", writes=[f"cata{half}"])
                    for ch in range(8):
                        conv_chunk(0, ch)()
                    conv_finish(0)
                    outproj(0, side=1)
                    x1phase(0)
                    outproj(1)
                    x1phase(1)
                    S.barrier()
                if stop in ("p1", "p1a", "p1b"):
                    continue
                with ExitStack() as p23:
                    hT = p23.enter_context(nc.sbuf_tensor(f"hT_{s}", [128, NFF, TS], BF16))
                    with ExitStack() as p2:
                        P2 = lambda name, shape, dt: p2.enter_context(nc.sbuf_tensor(f"{name}_{s}", shape, dt))
                        wg = [P2(f"wg{i}", [128, KC, 256], BF16) for i in range(2)]
                        wu = [P2(f"wu{i}", [128, KC, 256], BF16) for i in range(2)]
                        sg = [P2(f"sg{i}", [128, 512], F32) for i in range(2)]
                        npair = 0
                        for fb in range(D_FF // 256):
                            wi = fb % 2
                            S.dma("pool", wg[wi][:], wg_v[:, :, fb * 256:(fb + 1) * 256], f"wg{wi}", writes=[f"wg{wi}"])
                            S.dma("pool", wu[wi][:], wu_v[:, :, fb * 256:(fb + 1) * 256], f"wu{wi}", writes=[f"wu{wi}"])
                            for cc in range(2):
                                ffc = fb * 2 + cc
                                for t in range(2):
                                    pr = npair % 3
                                    npair += 1
                                    bg, bu = 2 * pr, 2 * pr + 1
                                    S.op("pe", lambda: _mm_group(nc, bank(bg), [
                                        (wg[wi][:, k, cc * 128:(cc + 1) * 128], hn2[:, k, t * 512:(t + 1) * 512])
                                        for k in range(KC)]), reads=[f"wg{wi}"], writes=[f"bank{bg}"])
                                    S.op("pe", lambda: _mm_group(nc, bank(bu), [
                                        (wu[wi][:, k, cc * 128:(cc + 1) * 128], hn2[:, k, t * 512:(t + 1) * 512])
                                        for k in range(KC)]), reads=[f"wu{wi}"], writes=[f"bank{bu}"])
                                    g_ = npair % 2
                                    S.op("act", lambda: nc.scalar.activation(out=sg[g_][:], in_=bank(bg), func=AF.Silu),
                                         reads=[f"bank{bg}"], writes=[f"sg{g_}"])
                                    S.op("dve", lambda: nc.vector.tensor_tensor(hT[:, ffc, t * 512:(t + 1) * 512],
                                                                                sg[g_][:], bank(bu), ALU.mult),
                                         reads=[f"sg{g_}", f"bank{bu}"], writes=[("hT", ffc, t)])
                                    if p4_items:
                                        p4_items.pop(0)()
                        S.barrier()
                    if stop == "p2":
                        continue
                    with ExitStack() as p3:
                        P3 = lambda name, shape, dt: p3.enter_context(nc.sbuf_tensor(f"{name}_{s}", shape, dt))
                        wd = [P3(f"wd{i}", [128, NFF, 128], BF16) for i in range(2)]
                        fst = [P3(f"fst{i}", [128, TS], F32) for i in range(2)]
                        sqb = [P3(f"sqb3{i}", [128, 512], BF16) for i in range(2)]
                        nsq = 0
                        for fc in range(KC):
                            wi = fc % 2
                            fi = fc % 2
                            S.dma("pool", wd[wi][:], wd_v[:, :, fc * 128:(fc + 1) * 128], f"wd{wi}", writes=[f"wd{wi}"])
                            for t in range(2):
                                bk = cnt["b"] % 6
                                cnt["b"] += 1
                                S.op("pe", lambda: _mm_group(nc, bank(bk), [
                                    (wd[wi][:, k, :], hT[:, k, t * 512:(t + 1) * 512])
                                    for k in range(NFF)]), reads=[f"wd{wi}"], writes=[f"bank{bk}"])
                                S.op("dve", lambda: nc.vector.tensor_copy(fst[fi][:, t * 512:(t + 1) * 512], bank(bk)),
                                     reads=[f"bank{bk}"], writes=[f"fst{fi}"])
                                q_ = nsq % 2
                                nsq += 1
                                S.op("act", lambda: nc.scalar.activation(out=sqb[q_][:], in_=fst[fi][:, t * 512:(t + 1) * 512],
                                                                         func=AF.Square),
                                     reads=[f"fst{fi}"], writes=[f"sqb{q_}"])
                                S.op("pe", lambda: nc.tensor.matmul(bank(6 + t), onesD[:], sqb[q_][:], start=(fc == 0),
                                                                    stop=(fc == KC - 1)),
                                     reads=["onesD", f"sqb{q_}"], writes=[f"bank{6 + t}"])
                            S.dma("sp", fs_v[:, fc, T0:T0 + TS], fst[fi][:], f"fst{fi}", reads=[f"fst{fi}"],
                                  writes=[("fs", fc)])
                        for t in range(2):
                            rstd_from(rsf[:, t * 512:(t + 1) * 512], bank(6 + t), f"bank{6 + t}", "rsf")
                        S.barrier()
                    def mk_item(fc, t, T0=T0):
                        def item():
                            b = (fc * 2 + t) % 2
                            cols = slice(T0 + t * 512, T0 + (t + 1) * 512)
                            S.dma("sp", ftp[b][:], fs_v[:, fc, cols], f"ftp{b}", writes=[f"ftp{b}"])
                            S.dma("sp", x1p[b][:], xo_v[:, fc, cols], f"x1p{b}", writes=[f"x1p{b}"])
                            S.op("dve", lambda: nc.vector.scalar_tensor_tensor(
                                ftp[b][:], ftp[b][:], gv[:, GO + fc:GO + fc + 1], rsf[:, t * 512:(t + 1) * 512],
                                ALU.mult, ALU.mult), reads=[f"ftp{b}", "rsf"], writes=[f"ftp{b}"])
                            S.op("dve", lambda: nc.vector.tensor_tensor(ftp[b][:], ftp[b][:], x1p[b][:], ALU.add),
                                 reads=[f"ftp{b}", f"x1p{b}"], writes=[f"ftp{b}"])
                            S.dma("sp", xo_v[:, fc, cols], ftp[b][:], f"ftp{b}", reads=[f"ftp{b}"],
                                  writes=[("xof", fc, t, T0)])
                        return item
                    for fc in range(KC):
                        for t in range(2):
                            p4_items.append(mk_item(fc, t))
        while p4_items:
            p4_items.pop(0)()
        S.finish()
    return nc


_PROGS = {}


def _prog(name):
    if name not in _PROGS:
        _PROGS[name] = {"A": build_A, "B": build_B, "C": build_C}[name]()
    return _PROGS[name]


def _run(name, in_maps):
    res = run_bass_kernel_spmd(_prog(name), in_maps, core_ids=list(range(N_CORES)))
    return res.results


def _c(a, dt=np.float32):
    return np.ascontiguousarray(a, dtype=dt)


def kernel(x, w_in, w_out, lambda_q1, lambda_k1, lambda_q2, lambda_k2, subln_gain, conv_w, conv_norm_gain,
           rel_bias, w_gate, w_up, w_down, norm_mix_pre, norm_mix_post, norm_ffn_pre, norm_ffn_post):
    f = lambda a: np.asarray(a, dtype=np.float32)
    x = f(x)
    xs = x.reshape(BATCH * SEQ, D_MODEL)
    RPB = N_CORES // BATCH
    xT = [_c(xs[c * TOK:(c + 1) * TOK].T) for c in range(N_CORES)]
    U, cfar = _bias_tables(f(rel_bias))
    w_in, w_out, w_gate, w_up, w_down = f(w_in), f(w_out), f(w_gate), f(w_up), f(w_down)
    for l in range(DEPTH):
        lam_init = 0.8 - 0.6 * math.exp(-0.3 * l)
        g_pre = _c(f(norm_mix_pre)[l].reshape(KC, 128).T)
        wl = _c(w_in[l])
        ra = _run("A", [{"xT": xT[c], "w_in": wl, "g_pre": g_pre} for c in range(N_CORES)])
        lamv = _c(np.broadcast_to(np.stack([f(lambda_q1)[l], f(lambda_k1)[l], f(lambda_q2)[l], f(lambda_k2)[l]])[None],
                                  (128, 4, QK_DIM)))
        cst = _c(np.broadcast_to(np.array([lam_init, 1.0 - lam_init], np.float32)[None], (128, 2)))
        gsub = _c(f(subln_gain)[l].reshape(128, 1))
        inb = []
        for b in range(BATCH):
            for hp in range(RPB):
                rows = slice(hp * 256, (hp + 1) * 256)
                inb.append({
                    "qT": np.ascontiguousarray(np.concatenate([ra[b * RPB + r]["qT"][rows] for r in range(RPB)], axis=1)),
                    "kT": np.ascontiguousarray(np.concatenate([ra[b * RPB + r]["kT"][rows] for r in range(RPB)], axis=1)),
                    "v": np.ascontiguousarray(np.concatenate([ra[b * RPB + r]["v"][:, rows] for r in range(RPB)], axis=0)),
                    "U": _c(U[:, 2 * hp:2 * hp + 2, :]), "cfar": _c(cfar[:, 2 * hp:2 * hp + 2]),
                    "lamv": lamv, "cst": cst, "gsub": gsub})
        rb = _run("B", inb)
        del inb
        gvec = _c(np.concatenate([f(conv_norm_gain)[l].reshape(8, 128).T, f(norm_mix_post)[l].reshape(KC, 128).T,
                                  f(norm_ffn_pre)[l].reshape(KC, 128).T, f(norm_ffn_post)[l].reshape(KC, 128).T], axis=1))
        convw = _c(f(conv_w)[l].reshape(3, 8, 128).transpose(2, 1, 0))
        wo, wg, wu, wd = _c(w_out[l]), _c(w_gate[l]), _c(w_up[l]), _c(w_down[l])
        inc = []
        for c in range(N_CORES):
            b, r = divmod(c, RPB)
            aT = np.ascontiguousarray(np.concatenate(
                [rb[b * RPB + hp]["aT"][:, r * TOK:(r + 1) * TOK] for hp in range(RPB)], axis=0))
            halo = {}
            for nm in ("gcT", "hcT"):
                cur = np.asarray(ra[c][nm])
                if r == 0:
                    left = np.zeros((C_W, 2), np.float32)
                else:
                    left = np.asarray(ra[c - 1][nm])[:, TOK - 2:TOK]
                halo[nm] = np.ascontiguousarray(np.concatenate([left, cur], axis=1))
            inc.append({"xT": xT[c], "aT": aT, "gbT": np.ascontiguousarray(ra[c]["gbT"]), "gcT": halo["gcT"],
                        "hcT": halo["hcT"], "convw": convw, "gvec": gvec, "w_out": wo, "w_gate": wg, "w_up": wu,
                        "w_down": wd})
        del ra, rb
        rc = _run("C", inc)
        del inc
        xT = [np.ascontiguousarray(rc[c]["xoT"]) for c in range(N_CORES)]
        del rc
    out = np.concatenate([t.T for t in xT], axis=0).reshape(BATCH, SEQ, D_MODEL)
    return np.ascontiguousarray(out, dtype=np.float32)
```
